# Optimizing a Trainium2 kernel written in Bass

```python
import math
import jax
import jax.numpy as jnp
from jax import lax
import numpy as np

D_MODEL = 1024
BATCH = 8
SEQ = 4096
DEPTH = 4

HEAD_DIM = 64
MIX_WIDTH = 1024
A_HEADS = 4
A_KV_HEADS = 2
B_HEADS = 4
B_QK_DIM = 32
B_V_DIM = 64
C_HEADS = 4
C_Q_LORA = 256
C_KV_LORA = 128
C_NOPE = 64
C_ROPE = 32
C_V_DIM = 64
D_HEADS = 4
DILATED_CONFIGS = ((128, 1), (512, 4), (2048, 16))
D_FF = 2816
GRID_W = 64
Q_BLOCK = 128
ROPE_THETA = 10000.0
NORM_EPS = 1e-6

IN_SPLITS = (
    A_HEADS * HEAD_DIM, A_KV_HEADS * HEAD_DIM, A_KV_HEADS * HEAD_DIM,
    B_HEADS * 2 * B_QK_DIM, B_HEADS * 2 * B_QK_DIM, B_HEADS * B_V_DIM,
    C_Q_LORA, C_KV_LORA, C_ROPE,
    D_HEADS * HEAD_DIM, D_HEADS * HEAD_DIM, D_HEADS * HEAD_DIM,
)
IN_WIDTH = sum(IN_SPLITS)

kernel_name = 'hybrid_parallel_heads_encoder'


def _rmsnorm(x, g):
    xf = x.astype(jnp.float32)
    y = xf * lax.rsqrt(jnp.mean(xf * xf, axis=-1, keepdims=True) + NORM_EPS)
    return (y * g.astype(jnp.float32)).astype(x.dtype)


def _split_cols(a, sizes):
    parts, off = [], 0
    for n in sizes:
        parts.append(a[..., off:off + n])
        off += n
    return parts


def _heads(a, n_heads):
    b, s, _ = a.shape
    return a.reshape(b, s, n_heads, -1).transpose(0, 2, 1, 3)


def _rope_freqs(dim):
    return 1.0 / (ROPE_THETA ** (jnp.arange(0, dim, 2, dtype=jnp.float32) / dim))


def _rotate(x, ang):
    n = ang.shape[-1]
    x1 = x[..., :n].astype(jnp.float32)
    x2 = x[..., n:].astype(jnp.float32)
    c, s = jnp.cos(ang), jnp.sin(ang)
    return jnp.concatenate([x1 * c - x2 * s, x1 * s + x2 * c], axis=-1).astype(x.dtype)


def _axial_rope(x, ang_row, ang_col):
    half = x.shape[-1] // 2
    return jnp.concatenate([_rotate(x[..., :half], ang_row), _rotate(x[..., half:], ang_col)], axis=-1)


def _alibi_slopes(n):
    return 2.0 ** (-8.0 * jnp.arange(1, n + 1, dtype=jnp.float32) / n)


def _to_blocks(a):
    *lead, s, d = a.shape
    return jnp.moveaxis(a.reshape(*lead, s // Q_BLOCK, Q_BLOCK, d), -3, 0)


def _from_blocks(a):
    a = jnp.moveaxis(a, 0, -3)
    *lead, nb, qb, d = a.shape
    return a.reshape(*lead, nb * qb, d)


def _dense_attention(q, k, v, scale):
    def block(qb):
        s = jnp.einsum('bhgqd,bhkd->bhgqk', qb, k).astype(jnp.float32) * scale
        p = jax.nn.softmax(s, axis=-1)
        return jnp.einsum('bhgqk,bhkd->bhgqd', p.astype(v.dtype), v)
    return _from_blocks(lax.map(block, _to_blocks(q)))


def _diff_attention(q1, q2, k1, k2, v, lam, slopes, scale):
    s_len = k1.shape[-2]
    kpos = jnp.arange(s_len)

    def block(args):
        q1b, q2b, qpos = args
        dist = jnp.abs(qpos[:, None] - kpos[None, :]).astype(jnp.float32)
        bias = -slopes[:, None, None] * dist
        a1 = jax.nn.softmax(jnp.einsum('bhqd,bhkd->bhqk', q1b, k1).astype(jnp.float32) * scale + bias, axis=-1)
        a2 = jax.nn.softmax(jnp.einsum('bhqd,bhkd->bhqk', q2b, k2).astype(jnp.float32) * scale + bias, axis=-1)
        return jnp.einsum('bhqk,bhkd->bhqd', (a1 - lam * a2).astype(v.dtype), v)

    qpos = jnp.arange(s_len).reshape(-1, Q_BLOCK)
    return _from_blocks(lax.map(block, (_to_blocks(q1), _to_blocks(q2), qpos)))


def _banded_attention(q, k, v, radius, step, slopes, scale):
    b, h, r, l_len, dh = q.shape
    qb = min(Q_BLOCK, l_len)
    nb = -(-l_len // qb)
    lp = nb * qb

    def pad(a, lo, hi):
        return jnp.pad(a, ((0, 0), (0, 0), (0, 0), (lo, hi), (0, 0)))

    qp = pad(q, 0, lp - l_len).reshape(b, h, r, nb, qb, dh)
    kp = pad(k, radius, lp - l_len + radius)
    vp = pad(v, radius, lp - l_len + radius)
    band = jnp.arange(nb)[:, None] * qb + jnp.arange(qb + 2 * radius)[None, :]
    kb = kp[:, :, :, band]
    vb = vp[:, :, :, band]
    qpos = jnp.arange(lp).reshape(nb, qb)
    kpos = band - radius
    rel = jnp.abs(qpos[:, :, None] - kpos[:, None, :])
    in_range = ((kpos >= 0) & (kpos < l_len))[:, None, :]
    allowed = (rel <= radius) & (in_range | (rel == 0))
    s = jnp.einsum('bhrnqd,bhrnkd->bhrnqk', qp, kb).astype(jnp.float32) * scale
    s = s - slopes[:, None, None, None, None] * (rel * step).astype(jnp.float32)
    s = jnp.where(allowed, s, -jnp.inf)
    lse = jax.nn.logsumexp(s, axis=-1, keepdims=True)
    p = jnp.exp(s - lse)
    out = jnp.einsum('bhrnqk,bhrnkd->bhrnqd', p.astype(v.dtype), vb)
    out = out.reshape(b, h, r, lp, dh)[:, :, :, :l_len]
    lse = lse.reshape(b, h, r, lp)[:, :, :, :l_len]
    return out, lse


def _dilated_attention(q, k, v, slopes, scale):
    b, h, s_len, dh = q.shape
    outs, lses = [], []
    for window, dilation in DILATED_CONFIGS:
        l_len = s_len // dilation

        def by_residue(a):
            return a.reshape(b, h, l_len, dilation, dh).transpose(0, 1, 3, 2, 4)

        o, lse = _banded_attention(by_residue(q), by_residue(k), by_residue(v),
                                   window // (2 * dilation), dilation, slopes, scale)
        outs.append(o.transpose(0, 1, 3, 2, 4).reshape(b, h, s_len, dh))
        lses.append(lse.transpose(0, 1, 3, 2).reshape(b, h, s_len))
    w = jax.nn.softmax(jnp.stack(lses), axis=0)
    out = jnp.sum(w[..., None] * jnp.stack(outs).astype(jnp.float32), axis=0)
    return out.astype(q.dtype)


def _dwconv3(u, w, bias):
    up = jnp.pad(u, ((0, 0), (1, 1), (0, 0)))
    return up[:, :-2] * w[0] + up[:, 1:-1] * w[1] + up[:, 2:] * w[2] + bias


def setup_inputs(seed: int = 0) -> dict:
    key = jax.random.key(seed)
    ks = jax.random.split(key, 26)
    f32 = jnp.float32

    def nrm(k, shape, fan_in):
        return jax.random.normal(k, shape, f32) * fan_in ** -0.5

    def gain(k, n):
        return 1.0 + 0.02 * jax.random.normal(k, (DEPTH, n), f32)

    return {
        'x': jax.random.normal(ks[0], (BATCH, SEQ, D_MODEL), f32),
        'norm1_g': gain(ks[1], D_MODEL),
        'w_in': nrm(ks[2], (DEPTH, D_MODEL, IN_WIDTH), D_MODEL),
        'a_qn_g': gain(ks[3], HEAD_DIM),
        'a_kn_g': gain(ks[4], HEAD_DIM),
        'b_qn_g': gain(ks[5], B_QK_DIM),
        'b_kn_g': gain(ks[6], B_QK_DIM),
        'b_lam_q1': 0.1 * jax.random.normal(ks[7], (DEPTH, B_QK_DIM), f32),
        'b_lam_k1': 0.1 * jax.random.normal(ks[8], (DEPTH, B_QK_DIM), f32),
        'b_lam_q2': 0.1 * jax.random.normal(ks[9], (DEPTH, B_QK_DIM), f32),
        'b_lam_k2': 0.1 * jax.random.normal(ks[10], (DEPTH, B_QK_DIM), f32),
        'b_sub_g': gain(ks[11], B_V_DIM),
        'c_qa_g': gain(ks[12], C_Q_LORA),
        'c_kva_g': gain(ks[13], C_KV_LORA),
        'c_wqb': nrm(ks[14], (DEPTH, C_Q_LORA, C_HEADS * (C_NOPE + C_ROPE)), C_Q_LORA),
        'c_wkvb': nrm(ks[15], (DEPTH, C_KV_LORA, C_HEADS * (C_NOPE + C_V_DIM)), C_KV_LORA),
        'c_qn_g': gain(ks[16], C_NOPE + C_ROPE),
        'c_kn_g': gain(ks[17], C_NOPE + C_ROPE),
        'd_qn_g': gain(ks[18], HEAD_DIM),
        'd_kn_g': gain(ks[19], HEAD_DIM),
        'w_out': nrm(ks[20], (DEPTH, MIX_WIDTH, D_MODEL), MIX_WIDTH),
        'norm2_g': gain(ks[21], D_MODEL),
        'w_up': nrm(ks[22], (DEPTH, D_MODEL, 2 * D_FF), D_MODEL),
        'conv_w': nrm(ks[23], (DEPTH, 3, 2 * D_FF), 3),
        'conv_b': 0.02 * jax.random.normal(ks[24], (DEPTH, 2 * D_FF), f32),
        'w_down': nrm(ks[25], (DEPTH, D_FF, D_MODEL), D_FF),
    }


def reference(x, norm1_g, w_in, a_qn_g, a_kn_g, b_qn_g, b_kn_g, b_lam_q1, b_lam_k1,
              b_lam_q2, b_lam_k2, b_sub_g, c_qa_g, c_kva_g, c_wqb, c_wkvb, c_qn_g, c_kn_g,
              d_qn_g, d_kn_g, w_out, norm2_g, w_up, conv_w, conv_b, w_down):
    b, s_len, _ = x.shape
    n_rows = s_len // GRID_W
    rows, cols = jnp.meshgrid(jnp.arange(n_rows), jnp.arange(GRID_W), indexing='ij')
    rows = rows.reshape(-1).astype(jnp.float32)
    cols = cols.reshape(-1).astype(jnp.float32)
    pos = jnp.arange(s_len, dtype=jnp.float32)
    ax_freq = _rope_freqs(HEAD_DIM // 2)
    ang_row = rows[:, None] * ax_freq
    ang_col = cols[:, None] * ax_freq
    ang_mla = pos[:, None] * _rope_freqs(C_ROPE)
    slopes = _alibi_slopes(B_HEADS + D_HEADS)
    slopes_b, slopes_d = slopes[:B_HEADS], slopes[B_HEADS:]

    for l in range(DEPTH):
        h = _rmsnorm(x, norm1_g[l])
        proj = h @ w_in[l]
        (aq, ak, av, bq, bk, bv, cql, ckvl, ckr, dq, dk, dv) = _split_cols(proj, IN_SPLITS)

        qa = _axial_rope(_rmsnorm(_heads(aq, A_HEADS), a_qn_g[l]), ang_row, ang_col)
        ka = _axial_rope(_rmsnorm(_heads(ak, A_KV_HEADS), a_kn_g[l]), ang_row, ang_col)
        qa = qa.reshape(b, A_KV_HEADS, A_HEADS // A_KV_HEADS, s_len, HEAD_DIM)
        oa = _dense_attention(qa, ka, _heads(av, A_KV_HEADS), HEAD_DIM ** -0.5)
        oa = oa.reshape(b, A_HEADS, s_len, HEAD_DIM)

        qb = _heads(bq, B_HEADS)
        kb = _heads(bk, B_HEADS)
        q1 = _rmsnorm(qb[..., :B_QK_DIM], b_qn_g[l])
        q2 = _rmsnorm(qb[..., B_QK_DIM:], b_qn_g[l])
        k1 = _rmsnorm(kb[..., :B_QK_DIM], b_kn_g[l])
        k2 = _rmsnorm(kb[..., B_QK_DIM:], b_kn_g[l])
        lam_init = 0.8 - 0.6 * math.exp(-0.3 * l)
        lam = (jnp.exp(jnp.sum(b_lam_q1[l] * b_lam_k1[l]).astype(jnp.float32))
               - jnp.exp(jnp.sum(b_lam_q2[l] * b_lam_k2[l]).astype(jnp.float32)) + lam_init)
        ob = _diff_attention(q1, q2, k1, k2, _heads(bv, B_HEADS), lam, slopes_b, B_QK_DIM ** -0.5)
        ob = _rmsnorm(ob, b_sub_g[l]) * (1.0 - lam_init)

        qc = _heads(_rmsnorm(cql, c_qa_g[l]) @ c_wqb[l], C_HEADS)
        kvc = _heads(_rmsnorm(ckvl, c_kva_g[l]) @ c_wkvb[l], C_HEADS)
        k_nope, vc = kvc[..., :C_NOPE], kvc[..., C_NOPE:]
        k_rope = jnp.broadcast_to(ckr[:, None], (b, C_HEADS, s_len, C_ROPE))
        kc = jnp.concatenate([k_nope, k_rope], axis=-1)
        qc = _rmsnorm(qc, c_qn_g[l])
        kc = _rmsnorm(kc, c_kn_g[l])
        qc = jnp.concatenate([qc[..., :C_NOPE], _rotate(qc[..., C_NOPE:], ang_mla)], axis=-1)
        kc = jnp.concatenate([kc[..., :C_NOPE], _rotate(kc[..., C_NOPE:], ang_mla)], axis=-1)
        oc = _dense_attention(qc[:, :, None], kc, vc, (C_NOPE + C_ROPE) ** -0.5)[:, :, 0]

        qd = _rmsnorm(_heads(dq, D_HEADS), d_qn_g[l])
        kd = _rmsnorm(_heads(dk, D_HEADS), d_kn_g[l])
        od = _dilated_attention(qd, kd, _heads(dv, D_HEADS), slopes_d, HEAD_DIM ** -0.5)

        mix = jnp.concatenate([oa, ob, oc, od], axis=1)
        mix = mix.transpose(0, 2, 1, 3).reshape(b, s_len, MIX_WIDTH)
        x = x + mix @ w_out[l]

        h = _rmsnorm(x, norm2_g[l])
        u = _dwconv3(h @ w_up[l], conv_w[l], conv_b[l])
        gate, val = u[..., :D_FF], u[..., D_FF:]
        x = x + (jax.nn.silu(gate) * val) @ w_down[l]
    return x
```

```python
import contextlib
import math
import numpy as np
import concourse.bass as bass
import concourse.mybir as mybir
from concourse.bass_utils import run_bass_kernel_spmd

F32 = mybir.dt.float32
BF16 = mybir.dt.bfloat16
ALU = mybir.AluOpType
AF = mybir.ActivationFunctionType

S = 4096
DM = 1024
DEPTH = 4
NT = 8
DFF = 2816
NCF = 22
EPS = 1e-6
INW = 2464
ENGS = ("pe", "act", "dve", "pool", "sp")
PIPE_DEPTH = 1


class Buf:
    __slots__ = ("name", "last_w", "readers")

    def __init__(self, name):
        self.name = name
        self.last_w = None
        self.readers = []


class Op:
    __slots__ = ("eng", "fn", "waits", "sig", "sigval", "dmakey", "phase", "fin")

    def __init__(self, eng, fn, dmakey=None):
        self.eng = eng
        self.fn = fn
        self.waits = {}
        self.sig = False
        self.sigval = 0
        self.dmakey = dmakey


class Sched:
    def __init__(self, nc):
        self.nc = nc
        self.ops = {e: [] for e in ENGS}
        self.known = {e: {} for e in ENGS}
        self.dma_ops = {}
        self.bufs = {}
        self.seq = []
        self.phase = "init"

    def buf(self, name):
        b = self.bufs.get(name)
        if b is None:
            b = Buf(name)
            self.bufs[name] = b
        return b

    def op(self, eng, fn, reads=(), writes=(), dmakey=None):
        o = Op(eng, fn, dmakey)
        lst = self.ops[eng]
        idx = len(lst)
        if dmakey is not None:
            dl = self.dma_ops.setdefault(dmakey, [])
            tok = ("dma:" + dmakey, len(dl))
            dl.append(o)
        else:
            tok = (eng, idx)
        deps = {}

        def need(t):
            if t is None:
                return
            k, i = t
            if k == "pe" and tok[0] == "pe":
                return
            if deps.get(k, -1) < i:
                deps[k] = i

        ps_reads = [x for x in reads if x.startswith("ps")]
        if ps_reads:
            reads = [x for x in reads if not x.startswith("ps")]
            writes = list(writes) + [x for x in ps_reads if x not in writes]
        rb = [self.buf(x) for x in reads]
        wb = [self.buf(x) for x in writes]
        for b in rb:
            need(b.last_w)
        for b in wb:
            need(b.last_w)
            for r in b.readers:
                if r[0] != tok[0] or tok[0].startswith("dma:"):
                    need(r)
        kn = self.known[eng]
        for k, i in deps.items():
            if kn.get(k, -1) < i:
                kn[k] = i
                o.waits[k] = i
                if k.startswith("dma:"):
                    self.dma_ops[k[4:]][i].sig = True
                else:
                    self.ops[k][i].sig = True
        for b in rb:
            b.readers.append(tok)
        for b in wb:
            b.last_w = tok
            b.readers = []
        lst.append(o)
        o.phase = self.phase
        self.seq.append(o)
        return o

    def estimate(self, ghz=1.95, sem_lat=150.0, dma_fixed=2000.0, dma_bpns=150.0):
        free = {e: 0.0 for e in ENGS}
        busy = {}
        span = {}
        dma_free = [0.0]

        def fs(ap):
            v = ap.free_size
            return v() if callable(v) else v

        def nb(ap):
            v = ap.nbytes
            return v() if callable(v) else v

        def cost(o):
            m = getattr(o.fn, "meta", None)
            if m is None:
                return 100.0, 0
            fn, args, kw = m
            if fn == "dma_start":
                ap = kw["out"]
                return 60.0, nb(ap)
            if fn == "matmul":
                n = fs(kw["rhs"])
                c = max(n, 64) / ghz + 8
                if kw["rhs"].dtype == F32:
                    c *= 4
                return c, 0
            out = kw.get("out", args[0] if args else None)
            n = fs(out)
            if fn == "activation":
                return 186 + 0.833 * n + 93, 0
            if fn in ("memset",):
                return (n / 2 + 100) / 0.96, 0
            srcs = [kw.get(k) for k in ("in0", "in1", "in_") if kw.get(k) is not None]
            psum = any(type(a.tensor).__name__.startswith("PSum") for a in srcs if hasattr(a, "tensor"))
            allbf = all(getattr(a, "dtype", None) == BF16 for a in srcs + [out] if hasattr(a, "dtype"))
            if fn in ("tensor_tensor", "scalar_tensor_tensor"):
                rate = 2.0 if (allbf and not psum) else 1.0
            else:
                rate = 1.0 if psum else 2.0
            return (n / rate + 151) / 0.96, 0

        for o in self.seq:
            st_ = free[o.eng]
            for k, i in o.waits.items():
                dep = self.dma_ops[k[4:]][i] if k.startswith("dma:") else self.ops[k][i]
                st_ = max(st_, dep.fin + (sem_lat if k != o.eng else 60.0))
            c, nbytes = cost(o)
            if o.dmakey is not None:
                free[o.eng] = st_ + c
                t0 = max(st_ + c, dma_free[0])
                dma_free[0] = t0 + nbytes / dma_bpns
                o.fin = max(st_ + dma_fixed, dma_free[0] + 500.0)
                b = busy.setdefault(o.phase, {}); b["dma"] = b.get("dma", 0.0) + nbytes / dma_bpns
            else:
                o.fin = st_ + c
                free[o.eng] = o.fin
                b = busy.setdefault(o.phase, {}); b[o.eng] = b.get(o.eng, 0.0) + c
            sp_ = span.setdefault(o.phase, [st_, o.fin])
            sp_[0] = min(sp_[0], st_); sp_[1] = max(sp_[1], o.fin)
        total = max(o.fin for o in self.seq)
        return total, span, busy

    def emit(self, st, final_waits=()):
        nc = self.nc
        sems = {}
        for e in ENGS:
            sems[e] = st.enter_context(nc.semaphore("s_" + e))
        for k in self.dma_ops:
            sems["dma:" + k] = st.enter_context(nc.semaphore("d_" + k))
        for e in ENGS:
            c = 0
            for o in self.ops[e]:
                if o.dmakey is None and o.sig:
                    c += 1
                o.sigval = c
        for k, dl in self.dma_ops.items():
            c = 0
            for o in dl:
                c += 16
                o.sigval = c
        block = st.enter_context(nc.Block())
        engmap = {"pe": "tensor", "act": "scalar", "dve": "vector", "pool": "gpsimd", "sp": "sync"}

        def make(e):
            def body(engine):
                for o in self.ops[e]:
                    for k, i in o.waits.items():
                        if k.startswith("dma:"):
                            v = self.dma_ops[k[4:]][i].sigval
                        else:
                            v = self.ops[k][i].sigval
                        engine.wait_ge(sems[k], v)
                    ins = o.fn(engine)
                    if o.dmakey is not None:
                        ins.then_inc(sems["dma:" + o.dmakey], 16)
                    elif o.sig:
                        ins.then_inc(sems[e], 1)
                if e == "sp":
                    for k in final_waits:
                        dl = self.dma_ops[k]
                        engine.wait_ge(sems["dma:" + k], dl[-1].sigval)
            return body

        for e in ENGS:
            getattr(block, engmap[e])(make(e))


class Ring:
    def __init__(self, items):
        self.items = items
        self.i = 0

    def next(self):
        it = self.items[self.i % len(self.items)]
        self.i += 1
        return it


def I(fn, *args, **kw):
    f = lambda e: getattr(e, fn)(*args, **kw)
    f.meta = (fn, args, kw)
    return f


PP = {}
_off = 0
for _n, _w in [("n1g", 8), ("n2g", 8), ("cw0", NCF * 2), ("cw1", NCF * 2), ("cw2", NCF * 2), ("cb", NCF * 2),
               ("a_qn", 1), ("a_kn", 1), ("b_qn", 1), ("b_kn", 1), ("b_sub", 1), ("c_qa", 2), ("c_kva", 1),
               ("c_qn", 1), ("c_kn", 1), ("d_qn", 1), ("d_kn", 1), ("lq1", 1), ("lk1", 1), ("lq2", 1),
               ("lk2", 1), ("lam_init", 1), ("oml", 1)]:
    PP[_n] = (_off, _w)
    _off += _w
NPP = _off

A_Q, A_K, A_V = 0, 256, 384
B_Q, B_K, B_V = 512, 768, 1024
C_QL, C_KVL, C_KR = 1280, 1536, 1664
D_Q, D_K, D_V = 1696, 1952, 2208
MIXER_COLS = {"A": (0, 512), "B": (512, 1280), "C": (1280, 1696), "D": (1696, 2464)}

SLOPES = [2.0 ** (-(i + 1)) for i in range(8)]
MS_C0 = 1408
MS_W = 2944


def _host_constants():
    f32 = np.float32
    pos = np.arange(S, dtype=f32)
    rows = np.floor(pos / 64).astype(f32)
    cols = (pos - rows * 64).astype(f32)

    def freqs(dim):
        return (1.0 / (f32(10000.0) ** (np.arange(0, dim, 2, dtype=f32) / f32(dim)))).astype(f32)

    fa = freqs(32)
    ang_row = (rows[:, None] * fa[None, :]).astype(f32)
    ang_col = (cols[:, None] * fa[None, :]).astype(f32)
    ang_mla = (pos[:, None] * freqs(32)[None, :]).astype(f32)
    ropeA = np.zeros((2, 64, S), f32)
    for m in range(64):
        ang = ang_row if m < 32 else ang_col
        f = (m % 32) % 16
        ropeA[0, m] = np.cos(ang[:, f])
        ropeA[1, m] = np.sin(ang[:, f])
    ropeC = np.zeros((2, 96, S), f32)
    ropeC[0, :64] = 1.0
    for m in range(64, 96):
        f = (m - 64) % 16
        ropeC[0, m] = np.cos(ang_mla[:, f])
        ropeC[1, m] = np.sin(ang_mla[:, f])
    cR = np.zeros((2, 96, 96), f32)
    for base in (0, 32):
        for j in range(16):
            cR[0, base + j + 16, base + j] = -1.0
            cR[0, base + j, base + j + 16] = 1.0
    for j in range(16):
        cR[1, 64 + j + 16, 64 + j] = -1.0
        cR[1, 64 + j, 64 + j + 16] = 1.0
    ki = np.arange(128, dtype=f32)[:, None]
    cL = (ki - np.arange(512, dtype=f32)[None, :]).astype(f32)
    cW0 = (-np.abs(np.arange(896, dtype=f32)[None, :] - ki - 384.0)).astype(f32)
    delta = (np.arange(MS_W)[None, :] - np.arange(128)[:, None] - MS_C0)
    ad = np.abs(delta)
    mult = np.zeros_like(delta, dtype=f32)
    for d in (1, 4, 16):
        mult += ((delta % d == 0) & (ad <= 64 * d)).astype(f32)
    return {"ropeA": ropeA, "ropeC": ropeC, "cR": cR, "cL": cL, "cW0": cW0, "cMs": mult.astype(f32)}


def _pack_pp(inp):
    f32 = np.float32
    pp = np.zeros((DEPTH, 128, NPP), f32)

    def put(name, l, arr2d):
        o, w = PP[name]
        r = arr2d.shape[0]
        pp[l, :r, o:o + w] = arr2d

    for l in range(DEPTH):
        put("n1g", l, np.asarray(inp["norm1_g"][l]).reshape(8, 128).T)
        put("n2g", l, np.asarray(inp["norm2_g"][l]).reshape(8, 128).T)
        cw = np.asarray(inp["conv_w"][l])
        for t in range(3):
            put("cw%d" % t, l, cw[t].reshape(2 * NCF, 128).T)
        put("cb", l, np.asarray(inp["conv_b"][l]).reshape(2 * NCF, 128).T)
        put("a_qn", l, np.asarray(inp["a_qn_g"][l])[:, None])
        put("a_kn", l, np.asarray(inp["a_kn_g"][l])[:, None])
        put("b_qn", l, np.concatenate([inp["b_qn_g"][l], inp["b_qn_g"][l]])[:, None])
        put("b_kn", l, np.concatenate([inp["b_kn_g"][l], inp["b_kn_g"][l]])[:, None])
        put("b_sub", l, np.asarray(inp["b_sub_g"][l])[:, None])
        put("c_qa", l, np.asarray(inp["c_qa_g"][l]).reshape(2, 128).T)
        put("c_kva", l, np.asarray(inp["c_kva_g"][l])[:, None])
        put("c_qn", l, np.asarray(inp["c_qn_g"][l])[:, None])
        put("c_kn", l, np.asarray(inp["c_kn_g"][l])[:, None])
        put("d_qn", l, np.asarray(inp["d_qn_g"][l])[:, None])
        put("d_kn", l, np.asarray(inp["d_kn_g"][l])[:, None])
        put("lq1", l, np.asarray(inp["b_lam_q1"][l])[:, None])
        put("lk1", l, np.asarray(inp["b_lam_k1"][l])[:, None])
        put("lq2", l, np.asarray(inp["b_lam_q2"][l])[:, None])
        put("lk2", l, np.asarray(inp["b_lam_k2"][l])[:, None])
        lam_init = 0.8 - 0.6 * math.exp(-0.3 * l)
        put("lam_init", l, np.full((128, 1), lam_init, f32))
        put("oml", l, np.full((128, 1), 1.0 - lam_init, f32))
    return pp


_CACHE = {}


def build_program(n_layers=DEPTH, mixers="ABCD", do_ffn=True, dbg=False, do_attn=True):
    nc = bass.Bass("TRN2", target_bir_lowering=False)
    L = DEPTH

    def din(name, shape, dt=F32):
        return nc.dram_tensor(name, shape, dt, kind="ExternalInput").ap()

    xT = din("xT", [DM, S])
    w_in = din("w_in", [L, DM, INW])
    w_out = din("w_out", [L, DM, DM])
    w_up = din("w_up", [L, DM, 2 * DFF])
    w_down = din("w_down", [L, DFF, DM])
    c_wqb = din("c_wqb", [L, 256, 384])
    c_wkvb = din("c_wkvb", [L, 128, 512])
    ppd = din("pp", [L, 128, NPP])
    ropeA = din("ropeA", [2, 64, S])
    ropeC = din("ropeC", [2, 96, S])
    cR = din("cR", [2, 96, 96])
    cL = din("cL", [128, 512])
    cW0 = din("cW0", [128, 896])
    cMs = din("cMs", [128, MS_W])
    outT = nc.dram_tensor("outT", [DM, S], F32, kind="ExternalOutput").ap()
    skind = "ExternalOutput" if dbg else "Internal"
    XA = nc.dram_tensor("XA", [DM, S], F32, kind=skind).ap()
    XB = nc.dram_tensor("XB", [DM, S], F32, kind=skind).ap()
    HT = nc.dram_tensor("HT", [DM, S], BF16, kind=skind).ap()
    MIX = nc.dram_tensor("MIX", [DM, S], BF16, kind=skind).ap()

    st = contextlib.ExitStack()
    with st:
        def SB(name, shape, dt):
            return st.enter_context(nc.sbuf_tensor(name, shape, dt))

        s = Sched(nc)
        GUARD = "GUARD"

        G = 2
        ARENA_N = 32768 + 4 * 32 * 65
        arena = SB("arena", [128, ARENA_N], BF16)
        QT = arena[:, 0:16384].rearrange("p (h t) -> p h t", h=4)
        KT = arena[:, 16384:32768].rearrange("p (h t) -> p h t", h=4)
        VV = arena[:, 32768:32768 + 4 * 32 * 65].rearrange("p (h b d) -> p h b d", h=4, b=32)
        H2 = arena[:, 0:8 * G * 512].rearrange("p (k w t) -> p k w t", k=8, w=G)
        ACTT = arena[:, 8 * G * 512:(8 + NCF) * G * 512].rearrange("p (k w t) -> p k w t", k=NCF, w=G)
        warena = SB("warena", [128, 12288], BF16)
        WSLOT = [warena[:, i * 6144:(i + 1) * 6144].rearrange("p (k n) -> p k n", k=8) for i in range(2)]
        WUP = [warena[:, i * 2048:(i + 1) * 2048].rearrange("p (k n) -> p k n", k=8) for i in range(3)]
        WDN = [warena[:, 6144 + i * 2816:6144 + (i + 1) * 2816].rearrange("p (k n) -> p k n", k=NCF) for i in range(2)]
        hb = [SB("hbuf%d" % i, [128, 8, 512], BF16) for i in range(2)]
        hring = Ring([(hb[i], "hbuf%d" % i) for i in range(2)])
        xp = [SB("xp%d" % i, [128, 512], F32) for i in range(3)]
        xring = Ring([(xp[i], "xp%d" % i) for i in range(3)])
        xo = [SB("xo%d" % i, [128, 512], F32) for i in range(2)]
        xoring = Ring([(xo[i], "xo%d" % i) for i in range(2)])

        def mkring(prefix, n, shape, dt):
            ts = [SB("%s%d" % (prefix, i), shape, dt) for i in range(n)]
            return Ring([(ts[i], "%s%d" % (prefix, i)) for i in range(n)])

        sq_r = mkring("sqb", 2, [128, 512], BF16)
        qs_r = mkring("qsb", 2, [128, 512], BF16)
        ln_r = mkring("lnv", 2, [128, 512], F32)
        rs_r = mkring("rstd", 2, [128, 512], F32)
        t1_r = mkring("t1_", 2, [128, 512], F32)
        t2_r = mkring("t2_", 2, [128, 512], F32)
        t3_r = mkring("t3_", 2, [128, 512], F32)
        E_r = mkring("E", 3, [128, 1024], BF16)
        tb_r = mkring("tb", 3, [128, 512], F32)
        rp_r = mkring("rope", 2, [96, 2, 512], F32)
        mo_r = mkring("mo", 2, [64, 512], BF16)
        fin_r = mkring("fin", 3, [64, 512], F32)
        qa_t = SB("qa_t", [128, 2, 512], BF16)
        kvn_t = SB("kvn_t", [128, 512], BF16)
        kcraw = SB("kcraw", [96, 512], F32)
        ppt = [SB("ppt%d" % i, [128, NPP], F32) for i in range(2)]
        pdt = [SB("pdt%d" % i, [128, 8], F32) for i in range(2)]
        ones_bf = SB("ones_bf", [128, 128], BF16)
        blk32 = SB("blk32", [64, 64], BF16)
        ones_f = SB("ones_f", [128, 128], F32)
        R0 = SB("R0", [96, 2, 96], F32)
        RG = [SB("RG%d" % i, [96, 4, 96], BF16) for i in range(2)]
        cLt = SB("cLt", [128, 512], F32)
        cW0t = SB("cW0t", [128, 896], F32)
        cMst = SB("cMst", [128, MS_W], BF16)
        wqb_t = SB("wqb_t", [128, 2, 384], BF16)
        wkvn_t = SB("wkvn_t", [128, 4, 64], BF16)
        wkvv_t = SB("wkvv_t", [128, 4, 64], BF16)
        dummy = SB("dummy_t", [128, 8], F32)

        PS2 = [st.enter_context(nc.psum_tensor("psd%d" % i, [128, 1024], F32)) for i in range(4)]
        PS = [PS2[i // 2][:, (i % 2) * 512:(i % 2 + 1) * 512] for i in range(8)]
        psS = Ring([(PS[i], "ps%d" % i) for i in (0, 1, 2, 3)])
        psP = Ring([(PS2[i], ("ps%d" % (2 * i), "ps%d" % (2 * i + 1))) for i in (0, 1)])
        psO = Ring([(PS[i], "ps%d" % i) for i in (4, 5)])
        psB = Ring([(PS[i], "ps%d" % i) for i in (6, 7)])
        psM = Ring([(PS[i], "ps%d" % i) for i in (4, 6, 5, 7)])

        s.op("dve", I("memset", ones_bf[:], 1.0), writes=["ones_bf"])
        s.op("dve", I("memset", ones_f[:], 1.0), writes=["ones_f"])
        s.op("dve", I("memset", blk32[:], 0.0), writes=["blk32"])
        s.op("dve", I("memset", blk32[0:32, 0:32], 1.0), writes=["blk32"])
        s.op("dve", I("memset", blk32[32:64, 32:64], 1.0), writes=["blk32"])
        s.op("dve", I("memset", dummy[:], 0.0), writes=["dummy"])
        s.op("sp", I("dma_start", out=R0[:], in_=cR.rearrange("a k m -> k a m")), writes=["R0"], dmakey="c_R0")
        s.op("sp", I("dma_start", out=cLt[:], in_=cL[:, :]), writes=["cLt"], dmakey="c_L")
        s.op("sp", I("dma_start", out=cW0t[:], in_=cW0[:, :]), writes=["cW0t"], dmakey="c_W0")
        s.op("pool", I("dma_start", out=cMst[:], in_=cMs[:, :]), writes=["cMst"], dmakey="c_Ms")

        def phase_barrier():
            s.op("pool", I("memset", dummy[:, 0:1], 0.0), writes=[GUARD, "dummy"])

        def rstd_from_ssq(ssq_ap, ssq_name, d, inv_n, n=512):
            lt, ln_ = ln_r.next()
            rt, rn = rs_r.next()
            s.op("act", I("activation", out=lt[0:d, 0:n], in_=ssq_ap, func=AF.Ln, bias=EPS, scale=inv_n),
                 reads=[ssq_name], writes=[ln_])
            s.op("act", I("activation", out=rt[0:d, 0:n], in_=lt[0:d, 0:n], func=AF.Exp, scale=-0.5),
                 reads=[ln_], writes=[rn])
            return rt, rn

        def norm_rope(src, src_name, d, onesm, inv_n, g_ap, g_name, rg, rg_name, tabs, tabs_name, dest, dest_names,
                      ncol=512, ring=None):
            n = ncol
            src_names = list(src_name) if isinstance(src_name, (list, tuple)) else [src_name]
            ring = ring or psM
            sqt, sqn = sq_r.next()
            s.op("act", I("activation", out=sqt[0:d, 0:n], in_=src, func=AF.Square),
                 reads=src_names, writes=[sqn])
            if rg is not None:
                qt, qn = qs_r.next()
                s.op("act", I("activation", out=qt[0:d, 0:n], in_=src, func=AF.Copy),
                     reads=src_names, writes=[qn])
            pm, pmn = ring.next()
            s.op("pe", I("matmul", pm[0:d, 0:n], lhsT=onesm, rhs=sqt[0:d, 0:n], start=True, stop=True),
                 reads=[sqn, "ones_bf", "blk32"], writes=[pmn])
            lt, ln_ = ln_r.next()
            rt, rn = rs_r.next()
            s.op("act", I("activation", out=lt[0:d, 0:n], in_=pm[0:d, 0:n], func=AF.Ln, bias=EPS, scale=inv_n),
                 reads=[pmn], writes=[ln_])
            s.op("act", I("activation", out=rt[0:d, 0:n], in_=lt[0:d, 0:n], func=AF.Exp, scale=-0.5),
                 reads=[ln_], writes=[rn])
            if rg is None:
                s.op("dve", I("scalar_tensor_tensor", out=dest, in0=src, scalar=g_ap, in1=rt[0:d, 0:n],
                                                             op0=ALU.mult, op1=ALU.mult),
                     reads=src_names + [g_name, rn, GUARD], writes=dest_names)
                return
            pr, prn = ring.next()
            s.op("pe", I("matmul", pr[0:d, 0:n], lhsT=rg, rhs=qt[0:d, 0:n], start=True, stop=True),
                 reads=[qn, rg_name], writes=[prn])
            a1, a1n = t1_r.next()
            a2, a2n = t2_r.next()
            a3, a3n = t3_r.next()
            cos_ap, sin_ap = tabs
            s.op("dve", I("scalar_tensor_tensor", out=a1[0:d, 0:n], in0=src, scalar=g_ap, in1=cos_ap,
                                                         op0=ALU.mult, op1=ALU.mult),
                 reads=src_names + [g_name, tabs_name], writes=[a1n])
            s.op("dve", I("tensor_tensor", out=a2[0:d, 0:n], in0=pr[0:d, 0:n], in1=sin_ap, op=ALU.mult),
                 reads=[prn, tabs_name], writes=[a2n])
            s.op("dve", I("tensor_tensor", out=a3[0:d, 0:n], in0=a1[0:d, 0:n], in1=a2[0:d, 0:n], op=ALU.add),
                 reads=[a1n, a2n], writes=[a3n])
            s.op("dve", I("tensor_tensor", out=dest, in0=a3[0:d, 0:n], in1=rt[0:d, 0:n], op=ALU.mult),
                 reads=[a3n, rn, GUARD], writes=dest_names)

        def proj_fm(W, wname, c0, M, hbt, hbn):
            p, pn = psS.next()
            for kc in range(8):
                s.op("pe", I("matmul", p[0:M, :], lhsT=W[:, kc, c0:c0 + M], rhs=hbt[:, kc, :],
                             start=(kc == 0), stop=(kc == 7)),
                     reads=[wname, hbn, GUARD], writes=[pn])
            return p, pn

        def proj_v(W, wname, c0, nh, hbt, hbn, t):
            ncols = nh * 64
            for blk in range(4):
                p, pn = psS.next()
                for kc in range(8):
                    s.op("pe", I("matmul",
                        p[:, 0:ncols], lhsT=hbt[:, kc, blk * 128:(blk + 1) * 128], rhs=W[:, kc, c0:c0 + ncols],
                        start=(kc == 0), stop=(kc == 7)),
                        reads=[wname, hbn, GUARD], writes=[pn])
                b = t * 4 + blk
                s.op("act", I("activation",
                    out=VV[:, 0:nh, b, 0:64], in_=p[:, 0:ncols].rearrange("p (h d) -> p h d", h=nh), func=AF.Copy),
                    reads=[pn, GUARD], writes=["V:%d" % b])

        def proj_v1(W, wname, c0, nh, hbt, hbn, blk):
            ncols = nh * 64
            p, pn = psS.next()
            for kc in range(8):
                s.op("pe", I("matmul", p[:, 0:ncols], lhsT=hbt[:, kc, blk * 128:(blk + 1) * 128],
                             rhs=W[:, kc, c0:c0 + ncols], start=(kc == 0), stop=(kc == 7)),
                     reads=[wname, hbn, GUARD], writes=[pn])
            return p, pn

        def evac_v(r, nh, b):
            p, pn = r
            s.op("act", I("activation", out=VV[:, 0:nh, b, 0:64],
                          in_=p[:, 0:nh * 64].rearrange("p (h d) -> p h d", h=nh), func=AF.Copy),
                 reads=[pn, GUARD], writes=["V:%d" % b])

        def load_rope(tab_dram, d, t):
            rt_, rn_ = rp_r.next()
            s.op("sp", I("dma_start", out=rt_[0:d, :, :],
                                             in_=tab_dram[:, :, t * 512:(t + 1) * 512].rearrange("a d t -> d a t")),
                 writes=[rn_], dmakey=rn_)
            return (rt_[0:d, 0, :], rt_[0:d, 1, :]), rn_

        def layer(l, Xin, xin_name, Xout, xout_name):
            pt = ppt[l % 2]
            ptn = "ppt%d" % (l % 2)
            pd = pdt[l % 2]
            pdn = "pdt%d" % (l % 2)
            rgt = RG[l % 2]
            rgn = "RG%d" % (l % 2)

            def pcol(name, j=0, rows=128):
                o, w = PP[name]
                return pt[0:rows, o + j:o + j + 1]

            s.op("sp", I("dma_start", out=pt[:], in_=ppd[l]), writes=[ptn], dmakey=ptn)
            s.op("dve", I("tensor_tensor", out=pd[0:32, 2:3], in0=pcol("lq1", 0, 32), in1=pcol("lk1", 0, 32),
                                                  op=ALU.mult), reads=[ptn], writes=[pdn + "a"])
            s.op("dve", I("tensor_tensor", out=pd[0:32, 3:4], in0=pcol("lq2", 0, 32), in1=pcol("lk2", 0, 32),
                                                  op=ALU.mult), reads=[ptn], writes=[pdn + "a"])
            pm, pmn = psM.next()
            s.op("pe", I("matmul", pm[:, 0:2], lhsT=ones_f[0:32, :], rhs=pd[0:32, 2:4], start=True, stop=True),
                 reads=[pdn + "a", "ones_f"], writes=[pmn])
            s.op("act", I("activation", out=pd[:, 4:6], in_=pm[:, 0:2], func=AF.Exp),
                 reads=[pmn], writes=[pdn + "b"])
            s.op("dve", I("tensor_tensor", out=pd[:, 6:7], in0=pd[:, 5:6], in1=pd[:, 4:5], op=ALU.subtract),
                 reads=[pdn + "b"], writes=[pdn + "c"])
            s.op("dve", I("tensor_tensor", out=pd[:, 0:1], in0=pd[:, 6:7], in1=pcol("lam_init"), op=ALU.subtract),
                 reads=[pdn + "c", ptn], writes=[pdn])
            s.op("dve", I("tensor_tensor", out=pd[:, 1:2], in0=pcol("b_sub"), in1=pcol("oml"), op=ALU.mult),
                 reads=[ptn], writes=[pdn])
            for j, (gname, ri, d) in enumerate([("a_qn", 0, 64), ("a_kn", 0, 64), ("c_qn", 1, 96), ("c_kn", 1, 96)]):
                s.op("dve", I("tensor_scalar",
                    out=rgt[0:d, j, 0:d], in0=R0[0:d, ri, 0:d], scalar1=pcol(gname, 0, d), scalar2=None,
                    op0=ALU.mult), reads=[ptn, "R0"], writes=[rgn])

            xin_fm = Xin.rearrange("(kc p) t -> p kc t", p=128)
            ht_fm = HT.rearrange("(kc p) t -> p kc t", p=128)
            mix_fm = MIX.rearrange("(kc p) t -> p kc t", p=128)
            win_fm = w_in[l].rearrange("(kc p) n -> p kc n", p=128)

            def norm_chunk(Xsrc_fm, xname_fn, gname, tok0, ncol, col_lo, col_hi, dest_fn, dest_names, zero_pad=False):
                pm_, pmn_ = psM.next()
                for kc in range(8):
                    xt_, xn_ = xring.next()
                    if zero_pad:
                        s.op("dve", I("memset", xt_[:, 0:ncol], 0.0), writes=[xn_])
                    s.op("sp", I("dma_start", out=xt_[:, col_lo:col_hi], in_=Xsrc_fm[:, kc, tok0 + col_lo:tok0 + col_hi]),
                        reads=xname_fn(), writes=[xn_], dmakey=xn_)
                    sqt, sqn = sq_r.next()
                    s.op("act", I("activation", out=sqt[:, 0:ncol], in_=xt_[:, 0:ncol],
                                                                          func=AF.Square),
                         reads=[xn_], writes=[sqn])
                    s.op("pe", I("matmul", pm_[:, 0:ncol], lhsT=ones_bf[:, :], rhs=sqt[:, 0:ncol],
                                                                  start=(kc == 0), stop=(kc == 7)),
                         reads=[sqn, "ones_bf"], writes=[pmn_])
                rt, rn = rstd_from_ssq(pm_[:, 0:ncol], pmn_, 128, 1.0 / DM, ncol)
                for kc in range(8):
                    xt_, xn_ = xring.next()
                    if zero_pad:
                        s.op("dve", I("memset", xt_[:, 0:ncol], 0.0), writes=[xn_])
                    s.op("sp", I("dma_start", out=xt_[:, col_lo:col_hi], in_=Xsrc_fm[:, kc, tok0 + col_lo:tok0 + col_hi]),
                        reads=xname_fn(), writes=[xn_], dmakey=xn_)
                    s.op("dve", I("scalar_tensor_tensor", out=dest_fn(kc), in0=xt_[:, 0:ncol], scalar=pcol(gname, kc), in1=rt[:, 0:ncol],
                        op0=ALU.mult, op1=ALU.mult),
                        reads=[xn_, ptn, rn, GUARD], writes=dest_names)

            def load_w(slot, c0, c1):
                W = WSLOT[slot]
                wn = "wslot%d" % slot
                s.op("pool", I("dma_start", out=W[:, :, 0:c1 - c0], in_=win_fm[:, :, c0:c1]),
                     reads=[GUARD], writes=[wn], dmakey=wn)
                return W, wn

            wslot_i = [0]

            first_pass = True
            phase_barrier()
            s.op("dve", I("memset", VV[:, :, :, 64:65], 1.0), reads=[GUARD], writes=["VONES"])
            for mx in mixers:
                s.phase = "L%d proj%s" % (l, mx)
                c0, c1 = MIXER_COLS[mx]
                W, wn = load_w(wslot_i[0] % 2, c0, c1)
                wslot_i[0] += 1
                if mx == "C":
                    s.op("pool", I("dma_start", out=wqb_t[:], in_=c_wqb[l].rearrange("(kc p) n -> p kc n", p=128)),
                         writes=["wqb_t"], dmakey="wqb_t")
                    kvv = c_wkvb[l].rearrange("p (h two d) -> p h two d", h=4, two=2)
                    s.op("pool", I("dma_start", out=wkvn_t[:], in_=kvv[:, :, 0, :]), writes=["wkvn_t"],
                         dmakey="wkvn_t")
                    s.op("pool", I("dma_start", out=wkvv_t[:], in_=kvv[:, :, 1, :]), writes=["wkvv_t"],
                         dmakey="wkvv_t")
                for t in range(NT):
                    hbt, hbn = hring.next()
                    tsl = slice(t * 512, (t + 1) * 512)
                    if first_pass:
                        norm_chunk(xin_fm, lambda t=t: ["%s:%d" % (xin_name, t)], "n1g", t * 512, 512, 0, 512,
                                   lambda kc, hbt=hbt: hbt[:, kc, :], [hbn])
                        s.op("sp", I("dma_start", out=ht_fm[:, :, tsl], in_=hbt[:]),
                             reads=[hbn], writes=["HT:%d" % t], dmakey="st_" + hbn)
                    else:
                        s.op("sp", I("dma_start", out=hbt[:], in_=ht_fm[:, :, tsl]),
                             reads=["HT:%d" % t], writes=[hbn], dmakey="ld_" + hbn)
                    items = []
                    if mx == "A":
                        tabs, tabn = load_rope(ropeA, 64, t)
                        for h in range(4):
                            items.append((
                                lambda h=h: proj_fm(W, wn, A_Q + 64 * h, 64, hbt, hbn),
                                lambda r, h=h: norm_rope(r[0][0:64, :], r[1], 64, ones_bf[0:64, 0:64], 1.0 / 64,
                                                         pcol("a_qn", 0, 64), ptn, rgt[0:64, 0, 0:64], rgn, tabs, tabn,
                                                         QT[0:64, h, tsl], ["Q:%d:%d" % (h, t)])))
                        for h in range(2):
                            items.append((
                                lambda h=h: proj_fm(W, wn, A_K + 64 * h, 64, hbt, hbn),
                                lambda r, h=h: norm_rope(r[0][0:64, :], r[1], 64, ones_bf[0:64, 0:64], 1.0 / 64,
                                                         pcol("a_kn", 0, 64), ptn, rgt[0:64, 1, 0:64], rgn, tabs, tabn,
                                                         KT[0:64, h, tsl], ["K:%d:%d" % (h, t)])))
                        for blk in range(4):
                            items.append((lambda blk=blk: proj_v1(W, wn, A_V, 2, hbt, hbn, blk),
                                          lambda r, blk=blk: evac_v(r, 2, t * 4 + blk)))
                    elif mx in "BD":
                        qg, kg = ("b_qn", "b_kn") if mx == "B" else ("d_qn", "d_kn")
                        onesm = blk32[0:64, 0:64] if mx == "B" else ones_bf[0:64, 0:64]
                        inv_n = 1.0 / 32 if mx == "B" else 1.0 / 64
                        for h in range(4):
                            for (gname, coff, dst, tag) in ((qg, 0, QT, "Q"), (kg, 256, KT, "K")):
                                items.append((
                                    lambda h=h, coff=coff: proj_fm(W, wn, coff + 64 * h, 64, hbt, hbn),
                                    lambda r, h=h, gname=gname, dst=dst, tag=tag: norm_rope(
                                        r[0][0:64, :], r[1], 64, onesm, inv_n, pcol(gname, 0, 64), ptn, None, None, None,
                                        None, dst[0:64, h, tsl], ["%s:%d:%d" % (tag, h, t)])))
                        for blk in range(4):
                            items.append((lambda blk=blk: proj_v1(W, wn, 512, 4, hbt, hbn, blk),
                                          lambda r, blk=blk: evac_v(r, 4, t * 4 + blk)))
                    else:
                        tabs, tabn = load_rope(ropeC, 96, t)

                        def c_qlat1():
                            return proj_fm(W, wn, 0, 128, hbt, hbn), proj_fm(W, wn, 128, 128, hbt, hbn)

                        def c_qlat2(r):
                            pm_, pmn_ = psM.next()
                            for j, (pp_, ppn_) in enumerate(r):
                                sqt, sqn = sq_r.next()
                                s.op("act", I("activation", out=sqt[:, :], in_=pp_[:, :], func=AF.Square),
                                     reads=[ppn_], writes=[sqn])
                                s.op("pe", I("matmul", pm_[:, :], lhsT=ones_bf[:, :], rhs=sqt[:, :],
                                             start=(j == 0), stop=(j == 1)), reads=[sqn, "ones_bf"], writes=[pmn_])
                            rt, rn = rstd_from_ssq(pm_[:, :], pmn_, 128, 1.0 / 256)
                            for j, (pp_, ppn_) in enumerate(r):
                                s.op("dve", I("scalar_tensor_tensor", out=qa_t[:, j, :], in0=pp_[:, :],
                                              scalar=pcol("c_qa", j), in1=rt[:, :], op0=ALU.mult, op1=ALU.mult),
                                     reads=[ppn_, ptn, rn], writes=["qa_t"])

                        items.append((c_qlat1, c_qlat2))
                        items.append((lambda: proj_fm(W, wn, 256, 128, hbt, hbn),
                                      lambda r: norm_rope(r[0][:, :], r[1], 128, ones_bf[:, :], 1.0 / 128, pcol("c_kva"),
                                                          ptn, None, None, None, None, kvn_t[:, :], ["kvn_t"])))
                        items.append((lambda: proj_fm(W, wn, 384, 32, hbt, hbn),
                                      lambda r: s.op("act", I("activation", out=kcraw[64:96, :], in_=r[0][0:32, :],
                                                              func=AF.Copy), reads=[r[1]], writes=["kcraw_r"])))

                        def c_q1(h):
                            p, pn = psS.next()
                            for j in range(2):
                                s.op("pe", I("matmul", p[0:96, :], lhsT=wqb_t[:, j, 96 * h:96 * h + 96], rhs=qa_t[:, j, :],
                                             start=(j == 0), stop=(j == 1)), reads=["wqb_t", "qa_t"], writes=[pn])
                            return p, pn

                        def c_k1(h):
                            p, pn = psS.next()
                            s.op("pe", I("matmul", p[0:64, :], lhsT=wkvn_t[:, h, :], rhs=kvn_t[:, :], start=True, stop=True),
                                 reads=["wkvn_t", "kvn_t"], writes=[pn])
                            return p, pn

                        def c_k2(r, h):
                            s.op("act", I("activation", out=kcraw[0:64, :], in_=r[0][0:64, :], func=AF.Copy),
                                 reads=[r[1]], writes=["kcraw_n"])
                            norm_rope(kcraw[0:96, :], ["kcraw_n", "kcraw_r"], 96, ones_bf[0:96, 0:96], 1.0 / 96,
                                      pcol("c_kn", 0, 96), ptn, rgt[0:96, 3, 0:96], rgn, tabs, tabn,
                                      KT[0:96, h, tsl], ["K:%d:%d" % (h, t)])

                        for h in range(4):
                            items.append((lambda h=h: c_q1(h),
                                          lambda r, h=h: norm_rope(r[0][0:96, :], r[1], 96, ones_bf[0:96, 0:96], 1.0 / 96,
                                                                   pcol("c_qn", 0, 96), ptn, rgt[0:96, 2, 0:96], rgn, tabs,
                                                                   tabn, QT[0:96, h, tsl], ["Q:%d:%d" % (h, t)])))
                            items.append((lambda h=h: c_k1(h), lambda r, h=h: c_k2(r, h)))

                        def c_v1(blk):
                            p, pn = psS.next()
                            s.op("pe", I("matmul", p[:, 0:256], lhsT=kvn_t[:, blk * 128:(blk + 1) * 128],
                                         rhs=wkvv_t[:].rearrange("p h d -> p (h d)"), start=True, stop=True),
                                 reads=["kvn_t", "wkvv_t"], writes=[pn])
                            return p, pn

                        for blk in range(4):
                            items.append((lambda blk=blk: c_v1(blk), lambda r, blk=blk: evac_v(r, 4, t * 4 + blk)))
                    pq = []
                    for (st1, st2) in items:
                        pq.append((st2, st1()))
                        if len(pq) > PIPE_DEPTH:
                            f, r = pq.pop(0)
                            f(r)
                    for f, r in pq:
                        f(r)
                first_pass = False
                if do_attn:
                    s.phase = "L%d attn%s" % (l, mx)
                    attention(l, mx, pd, pdn, pt, ptn)

            s.phase = "L%d wout" % l
            wo_fm = w_out[l].rearrange("(kc p) n -> p kc n", p=128)
            for half in range(2):
                Wh = WSLOT[half]
                s.op("pool", I("dma_start", out=Wh[:, :, 0:512],
                                                                     in_=wo_fm[:, :, half * 512:(half + 1) * 512]),
                     reads=[GUARD], writes=["wslot%d" % half], dmakey="wslot%d" % half)
            for t in range(NT):
                hbt, hbn = hring.next()
                tsl = slice(t * 512, (t + 1) * 512)
                s.op("sp", I("dma_start", out=hbt[:], in_=mix_fm[:, :, tsl]),
                     reads=["MIX:%d" % t], writes=[hbn], dmakey="ld_" + hbn)
                xl = {}

                def xload(c):
                    xt_, xn_ = xring.next()
                    s.op("sp", I("dma_start", out=xt_[:, :], in_=xin_fm[:, c, tsl]),
                         reads=["%s:%d" % (xin_name, t)], writes=[xn_], dmakey=xn_)
                    xl[c] = (xt_, xn_)

                xload(0)
                xload(1)
                for c in range(8):
                    Wh = WSLOT[c // 4]
                    cc = (c % 4) * 128
                    p, pn = psS.next()
                    for kc in range(8):
                        s.op("pe", I("matmul", p[:, :], lhsT=Wh[:, kc, cc:cc + 128], rhs=hbt[:, kc, :],
                                     start=(kc == 0), stop=(kc == 7)),
                             reads=["wslot%d" % (c // 4), hbn, GUARD], writes=[pn])
                    if c + 2 < 8:
                        xload(c + 2)
                    xt_, xn_ = xl[c]
                    ot, on = xoring.next()
                    s.op("dve", I("tensor_tensor", out=ot[:, :], in0=p[:, :], in1=xt_[:, :], op=ALU.add),
                         reads=[pn, xn_], writes=[on])
                    s.op("sp", I("dma_start", out=XA[c * 128:(c + 1) * 128, tsl], in_=ot[:, :]),
                         reads=[on], writes=["XA:%d" % t], dmakey="st_" + on)

            if not do_ffn:
                return
            phase_barrier()
            s.phase = "L%d ffn" % l
            xa_fm = XA.rearrange("(kc p) t -> p kc t", p=128)
            wup_fm = w_up[l].rearrange("(kc p) n -> p kc n", p=128)
            wdn_fm = w_down[l].rearrange("(kc p) n -> p kc n", p=128)
            wins = []
            for w in range(9):
                tok0 = 510 * w - 1
                n_in = min(512, S + 1 - tok0)
                lo = 1 if w == 0 else 0
                hi = min(n_in, S - tok0)
                wins.append((w, tok0, n_in, lo, hi))
            wup_i = [0]
            wdn_i = [0]
            for g0 in range(0, 9, G):
                grp = wins[g0:g0 + G]
                for wi, (w, tok0, n_in, lo, hi) in enumerate(grp):
                    chunks = sorted(set([max(tok0, 0) // 512, min(tok0 + n_in - 1, S - 1) // 512]))
                    norm_chunk(xa_fm, lambda chunks=chunks: ["XA:%d" % c for c in chunks], "n2g", tok0, n_in, lo, hi,
                               lambda kc, wi=wi, n_in=n_in: H2[:, kc, wi, 0:n_in], ["H2:%d" % wi],
                               zero_pad=(lo > 0 or hi < n_in))
                for j in range(NCF):
                    wu = WUP[wup_i[0] % 3]
                    wun = "wup%d" % (wup_i[0] % 3)
                    wup_i[0] += 1
                    for half in range(2):
                        s.op("pool", I("dma_start", out=wu[:, :, half * 128:(half + 1) * 128],
                                       in_=wup_fm[:, :, half * DFF + j * 128:half * DFF + (j + 1) * 128]),
                             reads=[GUARD], writes=[wun], dmakey=wun)
                    for wi, (w, tok0, n_in, lo, hi) in enumerate(grp):
                        no = n_in - 2
                        t3s = []
                        for half in range(2):
                            p, pn = psS.next()
                            for kc in range(8):
                                s.op("pe", I("matmul", p[:, 0:n_in], lhsT=wu[:, kc, half * 128:(half + 1) * 128], rhs=H2[:, kc, wi, 0:n_in],
                                    start=(kc == 0), stop=(kc == 7)),
                                    reads=[wun, "H2:%d" % wi, GUARD], writes=[pn])
                            cj = half * NCF + j
                            a1, a1n = t1_r.next()
                            a2, a2n = t2_r.next()
                            a3, a3n = (t3_r if half == 0 else tb_r).next()
                            s.op("act", I("activation", out=a1[:, 0:no], in_=p[:, 1:1 + no], func=AF.Identity,
                                bias=pt[:, PP["cb"][0] + cj:PP["cb"][0] + cj + 1],
                                scale=pt[:, PP["cw1"][0] + cj:PP["cw1"][0] + cj + 1]),
                                reads=[pn, ptn], writes=[a1n])
                            s.op("dve", I("scalar_tensor_tensor", out=a2[:, 0:no], in0=p[:, 0:no], scalar=pt[:, PP["cw0"][0] + cj:PP["cw0"][0] + cj + 1],
                                in1=a1[:, 0:no], op0=ALU.mult, op1=ALU.add), reads=[pn, a1n, ptn], writes=[a2n])
                            s.op("dve", I("scalar_tensor_tensor", out=a3[:, 0:no], in0=p[:, 2:2 + no], scalar=pt[:, PP["cw2"][0] + cj:PP["cw2"][0] + cj + 1],
                                in1=a2[:, 0:no], op0=ALU.mult, op1=ALU.add), reads=[pn, a2n, ptn], writes=[a3n])
                            t3s.append((a3, a3n))
                        (gt, gn), (vt, vn) = t3s
                        sg, sgn = ln_r.next()
                        s.op("act", I("activation", out=sg[:, 0:no], in_=gt[:, 0:no],
                                                                                func=AF.Silu),
                             reads=[gn], writes=[sgn])
                        s.op("dve", I("tensor_tensor", out=ACTT[:, j, wi, 0:no], in0=sg[:, 0:no], in1=vt[:, 0:no], op=ALU.mult),
                            reads=[sgn, vn, GUARD], writes=["ACT:%d" % wi])
                for c in range(8):
                    wd = WDN[wdn_i[0] % 2]
                    wdn = "wdn%d" % (wdn_i[0] % 2)
                    wdn_i[0] += 1
                    s.op("pool", I("dma_start", out=wd[:], in_=wdn_fm[:, :, c * 128:(c + 1) * 128]),
                         reads=[GUARD], writes=[wdn], dmakey=wdn)
                    for wi, (w, tok0, n_in, lo, hi) in enumerate(grp):
                        no = n_in - 2
                        o0 = tok0 + 1
                        p, pn = psS.next()
                        for kc in range(NCF):
                            s.op("pe", I("matmul", p[:, 0:no], lhsT=wd[:, kc, :], rhs=ACTT[:, kc, wi, 0:no],
                                start=(kc == 0), stop=(kc == NCF - 1)),
                                reads=[wdn, "ACT:%d" % wi, GUARD], writes=[pn])
                        chunks = sorted(set([o0 // 512, (o0 + no - 1) // 512]))
                        xt_, xn_ = xring.next()
                        s.op("sp", I("dma_start", out=xt_[:, 0:no], in_=xa_fm[:, c, o0:o0 + no]),
                            reads=["XA:%d" % cc for cc in chunks], writes=[xn_], dmakey=xn_)
                        ot, on = xoring.next()
                        s.op("dve", I("tensor_tensor", out=ot[:, 0:no], in0=p[:, 0:no], in1=xt_[:, 0:no], op=ALU.add),
                            reads=[pn, xn_], writes=[on])
                        s.op("sp", I("dma_start", out=Xout[c * 128:(c + 1) * 128, o0:o0 + no], in_=ot[:, 0:no]),
                            reads=[on], writes=["%s:%d" % (xout_name, cc) for cc in chunks], dmakey="st_" + on)

        def attention(l, mx, pd, pdn, pt, ptn):
            base_head = {"A": 0, "B": 4, "C": 8, "D": 12}[mx]
            d = {"A": 64, "B": 32, "C": 96, "D": 64}[mx]
            scale = float(d) ** -0.5

            def kv_of(h):
                return h // 2 if mx == "A" else h

            def dist_min(k0, q0):
                if k0 + 127 < q0:
                    return q0 - (k0 + 127)
                if k0 > q0 + 511:
                    return k0 - (q0 + 511)
                return 0

            def finalize_one(po, pon):
                rd, rdn = ln_r.next()
                s.op("dve", I("reciprocal", out=rd[64:65, :], in_=po[64:65, :]), reads=[pon], writes=[rdn])
                pb, pbn = psB.next()
                s.op("pe", I("matmul", pb[0:64, :], lhsT=ones_f[64:65, 0:64], rhs=rd[64:65, :], start=True, stop=True),
                     reads=[rdn, "ones_f"], writes=[pbn])
                rc, rcn = fin_r.next()
                s.op("dve", I("tensor_copy", out=rc[:, :], in_=pb[0:64, :]), reads=[pbn], writes=[rcn])
                return rc, rcn

            deferred = []
            pend = []
            WIDE_KEEP = 2
            SLOPE_KEEP = 2

            def do_pv(item, last_flush=False):
                po, pon, Eap, En, kb, kvh, q0, first, last, on_last = item
                while deferred:
                    deferred.pop(0)()
                if mx == "D":
                    c_lo = MS_C0 - (kb * 128 - q0)
                    s.op("dve", I("tensor_tensor", out=Eap, in0=Eap, in1=cMst[:, c_lo:c_lo + 512], op=ALU.mult),
                         reads=[En, "cMst"], writes=[En])
                s.op("pe", I("matmul", po[0:65, :], lhsT=VV[:, kvh, kb, :], rhs=Eap, start=first, stop=last),
                     reads=[En, "V:%d" % kb, "VONES", GUARD], writes=[pon])
                if last and on_last is not None:
                    deferred.append(on_last)

            def flush_to(n):
                while len(pend) > n:
                    do_pv(pend.pop(0))

            for h in range(4):
                slope = (SLOPES[h] if mx == "B" else SLOPES[4 + h]) if mx in "BD" else None
                maps = [(0, 32), (32, 64)] if mx == "B" else [(0, d)]
                kvh = kv_of(h)
                for qc in range(8):
                    q0 = qc * 512
                    if mx == "D":
                        kbs = [kb for kb in range(32) if q0 - 1024 <= kb * 128 < q0 + 512 + 1024]
                    elif mx == "B":
                        kbs = [kb for kb in range(32) if slope * dist_min(kb * 128, q0) <= 60.0]
                    else:
                        kbs = list(range(32))
                    accs = []

                    def fin(accs=accs, h=h, q0=q0, qc=qc):
                        mo, mon = mo_r.next()
                        if mx != "B":
                            po, pon = accs[0]
                            rc, rcn = finalize_one(po, pon)
                            s.op("dve", I("tensor_tensor", out=mo[:, :], in0=po[0:64, :], in1=rc[:, :], op=ALU.mult),
                                 reads=[pon, rcn], writes=[mon])
                        else:
                            ns = []
                            for (po, pon) in accs:
                                rc, rcn = finalize_one(po, pon)
                                nn, nnn = fin_r.next()
                                s.op("dve", I("tensor_tensor", out=nn[:, :], in0=po[0:64, :], in1=rc[:, :], op=ALU.mult),
                                     reads=[pon, rcn], writes=[nnn])
                                ns.append((nn, nnn))
                            (n1, n1n), (n2, n2n) = ns
                            df, dfn = t1_r.next()
                            s.op("dve", I("scalar_tensor_tensor", out=df[0:64, :], in0=n2[:, :], scalar=pd[0:64, 0:1],
                                          in1=n1[:, :], op0=ALU.mult, op1=ALU.add), reads=[n1n, n2n, pdn], writes=[dfn])
                            norm_rope(df[0:64, :], dfn, 64, ones_bf[0:64, 0:64], 1.0 / 64, pd[0:64, 1:2], pdn, None,
                                      None, None, None, mo[:, :], [mon], ring=psB)
                        hh = base_head + h
                        s.op("sp", I("dma_start", out=MIX[hh * 64:(hh + 1) * 64, q0:q0 + 512], in_=mo[:, :]),
                             reads=[mon], writes=["MIX:%d" % qc], dmakey="st_" + mon)

                    for mi, (r0, r1) in enumerate(maps):
                        po, pon = psO.next()
                        accs.append((po, pon))
                        on_last = fin if mi == len(maps) - 1 else None
                        nk = len(kbs)
                        if slope is None:
                            i = 0
                            while i < nk:
                                grp = kbs[i:i + 2]
                                pp2, (pna, pnb) = psP.next()
                                names = [pna, pnb][:len(grp)]
                                for gi, kb in enumerate(grp):
                                    k0 = kb * 128
                                    s.op("pe", I("matmul", pp2[:, gi * 512:(gi + 1) * 512], lhsT=KT[r0:r1, kvh, k0:k0 + 128],
                                                 rhs=QT[r0:r1, h, q0:q0 + 512], start=True, stop=True),
                                         reads=["K:%d:%d" % (kvh, kb // 4), "Q:%d:%d" % (h, qc), GUARD], writes=[names[gi]])
                                flush_to(WIDE_KEEP)
                                Et, En = E_r.next()
                                w = 512 * len(grp)
                                s.op("act", I("activation", out=Et[:, 0:w], in_=pp2[:, 0:w], func=AF.Exp, scale=scale),
                                     reads=names, writes=[En])
                                for gi, kb in enumerate(grp):
                                    idx = i + gi
                                    pend.append((po, pon, Et[:, gi * 512:(gi + 1) * 512], En, kb, kvh, q0, idx == 0,
                                                 idx == nk - 1, on_last))
                                i += 2
                        else:
                            for i, kb in enumerate(kbs):
                                k0 = kb * 128
                                ps_, psn = psS.next()
                                s.op("pe", I("matmul", ps_[:, :], lhsT=KT[r0:r1, kvh, k0:k0 + 128],
                                             rhs=QT[r0:r1, h, q0:q0 + 512], start=True, stop=True),
                                     reads=["K:%d:%d" % (kvh, kb // 4), "Q:%d:%d" % (h, qc), GUARD], writes=[psn])
                                flush_to(SLOPE_KEEP)
                                Et, En = E_r.next()
                                tt, ttn = tb_r.next()
                                if k0 + 127 < q0:
                                    bias_ap, mul, cb = cLt[:, :], slope / scale, -slope * (q0 - k0)
                                elif k0 > q0 + 511:
                                    bias_ap, mul, cb = cLt[:, :], -slope / scale, -slope * (k0 - q0)
                                else:
                                    j = (k0 - q0) // 128
                                    bias_ap, mul, cb = cW0t[:, 384 - 128 * j:896 - 128 * j], slope / scale, 0.0
                                s.op("dve", I("scalar_tensor_tensor", out=tt[:, :], in0=bias_ap, scalar=float(mul),
                                              in1=ps_[:, :], op0=ALU.mult, op1=ALU.add),
                                     reads=[psn, "cLt", "cW0t"], writes=[ttn])
                                s.op("act", I("activation", out=Et[:, 0:512], in_=tt[:, :], func=AF.Exp, scale=scale,
                                              bias=float(cb)), reads=[ttn], writes=[En])
                                pend.append((po, pon, Et[:, 0:512], En, kb, kvh, q0, i == 0, i == nk - 1, on_last))
            while pend:
                do_pv(pend.pop(0))
            while deferred:
                deferred.pop(0)()

        for l in range(n_layers):
            Xin, xin_name = (xT, "X0") if l == 0 else (XB, "XB")
            last = (l == n_layers - 1)
            Xout, xout_name = (outT, "OUT") if last else (XB, "XB")
            if not do_ffn:
                pass
            layer(l, Xin, xin_name, Xout, xout_name)

        finals = [k for k in s.dma_ops if k.startswith("st_")]
        s.emit(st, final_waits=finals)
        counts = {e: len(s.ops[e]) for e in ENGS}
        _CACHE["sched"] = s
    return nc, counts


def _prep_inputs(inputs):
    f32 = np.float32
    consts = _host_constants()
    pp = _pack_pp(inputs)
    shared = {
        "w_in": np.ascontiguousarray(inputs["w_in"], dtype=f32),
        "w_out": np.ascontiguousarray(inputs["w_out"], dtype=f32),
        "w_up": np.ascontiguousarray(inputs["w_up"], dtype=f32),
        "w_down": np.ascontiguousarray(inputs["w_down"], dtype=f32),
        "c_wqb": np.ascontiguousarray(inputs["c_wqb"], dtype=f32),
        "c_wkvb": np.ascontiguousarray(inputs["c_wkvb"], dtype=f32),
        "pp": pp,
    }
    shared.update(consts)
    return shared


def kernel(**inputs):
    x = np.asarray(inputs["x"], dtype=np.float32)
    shared = _prep_inputs(inputs)
    if "nc" not in _CACHE:
        _CACHE["nc"] = build_program()[0]
    nc = _CACHE["nc"]
    in_maps = []
    for c in range(8):
        m = dict(shared)
        m["xT"] = np.ascontiguousarray(x[c].T)
        in_maps.append(m)
    res = run_bass_kernel_spmd(nc, in_maps, core_ids=list(range(8)))
    out = np.stack([np.asarray(r["outT"]).T for r in res.results], axis=0)
    return np.ascontiguousarray(out.astype(np.float32))
```

```python
import contextlib
import math
import numpy as np
import concourse.bass as bass
import concourse.mybir as mybir
from concourse.bass_utils import run_bass_kernel_spmd

F32 = mybir.dt.float32
BF16 = mybir.dt.bfloat16
ALU = mybir.AluOpType
AF = mybir.ActivationFunctionType

S = 4096
DM = 1024
DEPTH = 4
NT = 8
DFF = 2816
NCF = 22
EPS = 1e-6
INW = 2464
ENGS = ("pe", "act", "dve", "pool", "sp")
PIPE_DEPTH = 1


class Buf:
    __slots__ = ("name", "last_w", "readers")

    def __init__(self, name):
        self.name = name
        self.last_w = None
        self.readers = []


class Op:
    __slots__ = ("eng", "fn", "waits", "sig", "sigval", "dmakey", "phase", "fin")

    def __init__(self, eng, fn, dmakey=None):
        self.eng = eng
        self.fn = fn
        self.waits = {}
        self.sig = False
        self.sigval = 0
        self.dmakey = dmakey


class Sched:
    def __init__(self, nc):
        self.nc = nc
        self.ops = {e: [] for e in ENGS}
        self.known = {e: {} for e in ENGS}
        self.dma_ops = {}
        self.bufs = {}
        self.seq = []
        self.phase = "init"

    def buf(self, name):
        b = self.bufs.get(name)
        if b is None:
            b = Buf(name)
            self.bufs[name] = b
        return b

    def op(self, eng, fn, reads=(), writes=(), dmakey=None):
        o = Op(eng, fn, dmakey)
        lst = self.ops[eng]
        idx = len(lst)
        if dmakey is not None:
            dl = self.dma_ops.setdefault(dmakey, [])
            tok = ("dma:" + dmakey, len(dl))
            dl.append(o)
        else:
            tok = (eng, idx)
        deps = {}

        def need(t):
            if t is None:
                return
            k, i = t
            if k == "pe" and tok[0] == "pe":
                return
            if deps.get(k, -1) < i:
                deps[k] = i

        ps_reads = [x for x in reads if x.startswith("ps")]
        if ps_reads:
            reads = [x for x in reads if not x.startswith("ps")]
            writes = list(writes) + [x for x in ps_reads if x not in writes]
        rb = [self.buf(x) for x in reads]
        wb = [self.buf(x) for x in writes]
        for b in rb:
            need(b.last_w)
        for b in wb:
            need(b.last_w)
            for r in b.readers:
                if r[0] != tok[0] or tok[0].startswith("dma:"):
                    need(r)
        kn = self.known[eng]
        for k, i in deps.items():
            if kn.get(k, -1) < i:
                kn[k] = i
                o.waits[k] = i
                if k.startswith("dma:"):
                    self.dma_ops[k[4:]][i].sig = True
                else:
                    self.ops[k][i].sig = True
        for b in rb:
            b.readers.append(tok)
        for b in wb:
            b.last_w = tok
            b.readers = []
        lst.append(o)
        o.phase = self.phase
        self.seq.append(o)
        return o

    def estimate(self, ghz=1.95, sem_lat=150.0, dma_fixed=2000.0, dma_bpns=150.0):
        free = {e: 0.0 for e in ENGS}
        busy = {}
        span = {}
        dma_free = [0.0]

        def fs(ap):
            v = ap.free_size
            return v() if callable(v) else v

        def nb(ap):
            v = ap.nbytes
            return v() if callable(v) else v

        def cost(o):
            m = getattr(o.fn, "meta", None)
            if m is None:
                return 100.0, 0
            fn, args, kw = m
            if fn == "dma_start":
                ap = kw["out"]
                return 60.0, nb(ap)
            if fn == "matmul":
                n = fs(kw["rhs"])
                c = max(n, 64) / ghz + 8
                if kw["rhs"].dtype == F32:
                    c *= 4
                return c, 0
            out = kw.get("out", args[0] if args else None)
            n = fs(out)
            if fn == "activation":
                return 186 + 0.833 * n + 93, 0
            if fn in ("memset",):
                return (n / 2 + 100) / 0.96, 0
            srcs = [kw.get(k) for k in ("in0", "in1", "in_") if kw.get(k) is not None]
            psum = any(type(a.tensor).__name__.startswith("PSum") for a in srcs if hasattr(a, "tensor"))
            allbf = all(getattr(a, "dtype", None) == BF16 for a in srcs + [out] if hasattr(a, "dtype"))
            if fn in ("tensor_tensor", "scalar_tensor_tensor"):
                rate = 2.0 if (allbf and not psum) else 1.0
            else:
                rate = 1.0 if psum else 2.0
            return (n / rate + 151) / 0.96, 0

        for o in self.seq:
            st_ = free[o.eng]
            for k, i in o.waits.items():
                dep = self.dma_ops[k[4:]][i] if k.startswith("dma:") else self.ops[k][i]
                st_ = max(st_, dep.fin + (sem_lat if k != o.eng else 60.0))
            c, nbytes = cost(o)
            if o.dmakey is not None:
                free[o.eng] = st_ + c
                t0 = max(st_ + c, dma_free[0])
                dma_free[0] = t0 + nbytes / dma_bpns
                o.fin = max(st_ + dma_fixed, dma_free[0] + 500.0)
                b = busy.setdefault(o.phase, {}); b["dma"] = b.get("dma", 0.0) + nbytes / dma_bpns
            else:
                o.fin = st_ + c
                free[o.eng] = o.fin
                b = busy.setdefault(o.phase, {}); b[o.eng] = b.get(o.eng, 0.0) + c
            sp_ = span.setdefault(o.phase, [st_, o.fin])
            sp_[0] = min(sp_[0], st_); sp_[1] = max(sp_[1], o.fin)
        total = max(o.fin for o in self.seq)
        return total, span, busy

    def emit(self, st, final_waits=()):
        nc = self.nc
        sems = {}
        for e in ENGS:
            sems[e] = st.enter_context(nc.semaphore("s_" + e))
        for k in self.dma_ops:
            sems["dma:" + k] = st.enter_context(nc.semaphore("d_" + k))
        for e in ENGS:
            c = 0
            for o in self.ops[e]:
                if o.dmakey is None and o.sig:
                    c += 1
                o.sigval = c
        for k, dl in self.dma_ops.items():
            c = 0
            for o in dl:
                c += 16
                o.sigval = c
        block = st.enter_context(nc.Block())
        engmap = {"pe": "tensor", "act": "scalar", "dve": "vector", "pool": "gpsimd", "sp": "sync"}

        def make(e):
            def body(engine):
                for o in self.ops[e]:
                    for k, i in o.waits.items():
                        if k.startswith("dma:"):
                            v = self.dma_ops[k[4:]][i].sigval
                        else:
                            v = self.ops[k][i].sigval
                        engine.wait_ge(sems[k], v)
                    ins = o.fn(engine)
                    if o.dmakey is not None:
                        ins.then_inc(sems["dma:" + o.dmakey], 16)
                    elif o.sig:
                        ins.then_inc(sems[e], 1)
                if e == "sp":
                    for k in final_waits:
                        dl = self.dma_ops[k]
                        engine.wait_ge(sems["dma:" + k], dl[-1].sigval)
            return body

        for e in ENGS:
            getattr(block, engmap[e])(make(e))


class Ring:
    def __init__(self, items):
        self.items = items
        self.i = 0

    def next(self):
        it = self.items[self.i % len(self.items)]
        self.i += 1
        return it


def I(fn, *args, **kw):
    f = lambda e: getattr(e, fn)(*args, **kw)
    f.meta = (fn, args, kw)
    return f


PP = {}
_off = 0
for _n, _w in [("n1g", 8), ("n2g", 8), ("cw0", NCF * 2), ("cw1", NCF * 2), ("cw2", NCF * 2), ("cb", NCF * 2),
               ("a_qn", 1), ("a_kn", 1), ("b_qn", 1), ("b_kn", 1), ("b_sub", 1), ("c_qa", 2), ("c_kva", 1),
               ("c_qn", 1), ("c_kn", 1), ("d_qn", 1), ("d_kn", 1), ("lq1", 1), ("lk1", 1), ("lq2", 1),
               ("lk2", 1), ("lam_init", 1), ("oml", 1)]:
    PP[_n] = (_off, _w)
    _off += _w
NPP = _off

A_Q, A_K, A_V = 0, 256, 384
B_Q, B_K, B_V = 512, 768, 1024
C_QL, C_KVL, C_KR = 1280, 1536, 1664
D_Q, D_K, D_V = 1696, 1952, 2208
MIXER_COLS = {"A": (0, 512), "B": (512, 1280), "C": (1280, 1696), "D": (1696, 2464)}

SLOPES = [2.0 ** (-(i + 1)) for i in range(8)]
MS_C0 = 1408
MS_W = 2944


def _host_constants():
    f32 = np.float32
    pos = np.arange(S, dtype=f32)
    rows = np.floor(pos / 64).astype(f32)
    cols = (pos - rows * 64).astype(f32)

    def freqs(dim):
        return (1.0 / (f32(10000.0) ** (np.arange(0, dim, 2, dtype=f32) / f32(dim)))).astype(f32)

    fa = freqs(32)
    ang_row = (rows[:, None] * fa[None, :]).astype(f32)
    ang_col = (cols[:, None] * fa[None, :]).astype(f32)
    ang_mla = (pos[:, None] * freqs(32)[None, :]).astype(f32)
    ropeA = np.zeros((2, 64, S), f32)
    for m in range(64):
        ang = ang_row if m < 32 else ang_col
        f = (m % 32) % 16
        ropeA[0, m] = np.cos(ang[:, f])
        ropeA[1, m] = np.sin(ang[:, f])
    ropeC = np.zeros((2, 96, S), f32)
    ropeC[0, :64] = 1.0
    for m in range(64, 96):
        f = (m - 64) % 16
        ropeC[0, m] = np.cos(ang_mla[:, f])
        ropeC[1, m] = np.sin(ang_mla[:, f])
    cR = np.zeros((2, 96, 96), f32)
    for base in (0, 32):
        for j in range(16):
            cR[0, base + j + 16, base + j] = -1.0
            cR[0, base + j, base + j + 16] = 1.0
    for j in range(16):
        cR[1, 64 + j + 16, 64 + j] = -1.0
        cR[1, 64 + j, 64 + j + 16] = 1.0
    ki = np.arange(128, dtype=f32)[:, None]
    cL = (ki - np.arange(512, dtype=f32)[None, :]).astype(f32)
    cW0 = (-np.abs(np.arange(896, dtype=f32)[None, :] - ki - 384.0)).astype(f32)
    delta = (np.arange(MS_W)[None, :] - np.arange(128)[:, None] - MS_C0)
    ad = np.abs(delta)
    mult = np.zeros_like(delta, dtype=f32)
    for d in (1, 4, 16):
        mult += ((delta % d == 0) & (ad <= 64 * d)).astype(f32)
    return {"ropeA": ropeA, "ropeC": ropeC, "cR": cR, "cL": cL, "cW0": cW0, "cMs": mult.astype(f32)}


def _pack_pp(inp):
    f32 = np.float32
    pp = np.zeros((DEPTH, 128, NPP), f32)

    def put(name, l, arr2d):
        o, w = PP[name]
        r = arr2d.shape[0]
        pp[l, :r, o:o + w] = arr2d

    for l in range(DEPTH):
        put("n1g", l, np.asarray(inp["norm1_g"][l]).reshape(8, 128).T)
        put("n2g", l, np.asarray(inp["norm2_g"][l]).reshape(8, 128).T)
        cw = np.asarray(inp["conv_w"][l])
        for t in range(3):
            put("cw%d" % t, l, cw[t].reshape(2 * NCF, 128).T)
        put("cb", l, np.asarray(inp["conv_b"][l]).reshape(2 * NCF, 128).T)
        put("a_qn", l, np.asarray(inp["a_qn_g"][l])[:, None])
        put("a_kn", l, np.asarray(inp["a_kn_g"][l])[:, None])
        put("b_qn", l, np.concatenate([inp["b_qn_g"][l], inp["b_qn_g"][l]])[:, None])
        put("b_kn", l, np.concatenate([inp["b_kn_g"][l], inp["b_kn_g"][l]])[:, None])
        put("b_sub", l, np.asarray(inp["b_sub_g"][l])[:, None])
        put("c_qa", l, np.asarray(inp["c_qa_g"][l]).reshape(2, 128).T)
        put("c_kva", l, np.asarray(inp["c_kva_g"][l])[:, None])
        put("c_qn", l, np.asarray(inp["c_qn_g"][l])[:, None])
        put("c_kn", l, np.asarray(inp["c_kn_g"][l])[:, None])
        put("d_qn", l, np.asarray(inp["d_qn_g"][l])[:, None])
        put("d_kn", l, np.asarray(inp["d_kn_g"][l])[:, None])
        put("lq1", l, np.asarray(inp["b_lam_q1"][l])[:, None])
        put("lk1", l, np.asarray(inp["b_lam_k1"][l])[:, None])
        put("lq2", l, np.asarray(inp["b_lam_q2"][l])[:, None])
        put("lk2", l, np.asarray(inp["b_lam_k2"][l])[:, None])
        lam_init = 0.8 - 0.6 * math.exp(-0.3 * l)
        put("lam_init", l, np.full((128, 1), lam_init, f32))
        put("oml", l, np.full((128, 1), 1.0 - lam_init, f32))
    return pp


_CACHE = {}


def build_program(n_layers=DEPTH, mixers="ABCD", do_ffn=True, dbg=False, do_attn=True):
    nc = bass.Bass("TRN2", target_bir_lowering=False)
    L = DEPTH

    def din(name, shape, dt=F32):
        return nc.dram_tensor(name, shape, dt, kind="ExternalInput").ap()

    xT = din("xT", [DM, S])
    w_in = din("w_in", [L, DM, INW])
    w_out = din("w_out", [L, DM, DM])
    w_up = din("w_up", [L, DM, 2 * DFF])
    w_down = din("w_down", [L, DFF, DM])
    c_wqb = din("c_wqb", [L, 256, 384])
    c_wkvb = din("c_wkvb", [L, 128, 512])
    ppd = din("pp", [L, 128, NPP])
    ropeA = din("ropeA", [2, 64, S])
    ropeC = din("ropeC", [2, 96, S])
    cR = din("cR", [2, 96, 96])
    cL = din("cL", [128, 512])
    cW0 = din("cW0", [128, 896])
    cMs = din("cMs", [128, MS_W])
    outT = nc.dram_tensor("outT", [DM, S], F32, kind="ExternalOutput").ap()
    skind = "ExternalOutput" if dbg else "Internal"
    XA = nc.dram_tensor("XA", [DM, S], F32, kind=skind).ap()
    XB = nc.dram_tensor("XB", [DM, S], F32, kind=skind).ap()
    HT = nc.dram_tensor("HT", [DM, S], BF16, kind=skind).ap()
    MIX = nc.dram_tensor("MIX", [DM, S], BF16, kind=skind).ap()
    WUPB = nc.dram_tensor("WUPB", [DM, 2 * DFF], BF16).ap()
    WDNB = nc.dram_tensor("WDNB", [DFF, DM], BF16).ap()

    st = contextlib.ExitStack()
    with st:
        def SB(name, shape, dt):
            return st.enter_context(nc.sbuf_tensor(name, shape, dt))

        s = Sched(nc)
        GUARD = "GUARD"

        G = 2
        ARENA_N = 32768 + 4 * 32 * 65
        arena = SB("arena", [128, ARENA_N], BF16)
        QT = arena[:, 0:16384].rearrange("p (h t) -> p h t", h=4)
        KT = arena[:, 16384:32768].rearrange("p (h t) -> p h t", h=4)
        VV = arena[:, 32768:32768 + 4 * 32 * 65].rearrange("p (h b d) -> p h b d", h=4, b=32)
        H2 = arena[:, 0:8 * G * 512].rearrange("p (k w t) -> p k w t", k=8, w=G)
        ACTT = arena[:, 8 * G * 512:(8 + NCF) * G * 512].rearrange("p (k w t) -> p k w t", k=NCF, w=G)
        warena = SB("warena", [128, 12288], BF16)
        WSLOT = [warena[:, i * 6144:(i + 1) * 6144].rearrange("p (k n) -> p k n", k=8) for i in range(2)]
        WUP = [warena[:, i * 2048:(i + 1) * 2048].rearrange("p (k n) -> p k n", k=8) for i in range(3)]
        WDN = [warena[:, 6144 + i * 2816:6144 + (i + 1) * 2816].rearrange("p (k n) -> p k n", k=NCF) for i in range(2)]
        hb = [SB("hbuf%d" % i, [128, 8, 512], BF16) for i in range(2)]
        hring = Ring([(hb[i], "hbuf%d" % i) for i in range(2)])
        xp = [SB("xp%d" % i, [128, 512], F32) for i in range(3)]
        xring = Ring([(xp[i], "xp%d" % i) for i in range(3)])
        xo = [SB("xo%d" % i, [128, 512], F32) for i in range(2)]
        xoring = Ring([(xo[i], "xo%d" % i) for i in range(2)])

        def mkring(prefix, n, shape, dt):
            ts = [SB("%s%d" % (prefix, i), shape, dt) for i in range(n)]
            return Ring([(ts[i], "%s%d" % (prefix, i)) for i in range(n)])

        sq_r = mkring("sqb", 2, [128, 512], BF16)
        qs_r = mkring("qsb", 2, [128, 512], BF16)
        ln_r = mkring("lnv", 2, [128, 512], F32)
        rs_r = mkring("rstd", 2, [128, 512], F32)
        t1_r = mkring("t1_", 2, [128, 512], F32)
        t2_r = mkring("t2_", 2, [128, 512], F32)
        t3_r = mkring("t3_", 2, [128, 512], F32)
        E_r = mkring("E", 3, [128, 1024], BF16)
        tb_r = mkring("tb", 3, [128, 512], F32)
        rp_r = mkring("rope", 2, [96, 2, 512], F32)
        mo_r = mkring("mo", 2, [64, 512], BF16)
        fin_r = mkring("fin", 3, [64, 512], F32)
        qa_t = SB("qa_t", [128, 2, 512], BF16)
        kvn_t = SB("kvn_t", [128, 512], BF16)
        kcraw = SB("kcraw", [96, 512], F32)
        ppt = [SB("ppt%d" % i, [128, NPP], F32) for i in range(2)]
        pdt = [SB("pdt%d" % i, [128, 8], F32) for i in range(2)]
        ones_bf = SB("ones_bf", [128, 128], BF16)
        blk32 = SB("blk32", [64, 64], BF16)
        ones_f = SB("ones_f", [128, 128], F32)
        R0 = SB("R0", [96, 2, 96], F32)
        RG = [SB("RG%d" % i, [96, 4, 96], BF16) for i in range(2)]
        cLt = SB("cLt", [128, 512], F32)
        cW0t = SB("cW0t", [128, 896], F32)
        cMst = SB("cMst", [128, MS_W], BF16)
        wqb_t = SB("wqb_t", [128, 2, 384], BF16)
        wkvn_t = SB("wkvn_t", [128, 4, 64], BF16)
        wkvv_t = SB("wkvv_t", [128, 4, 64], BF16)
        dummy = SB("dummy_t", [128, 8], F32)

        PS2 = [st.enter_context(nc.psum_tensor("psd%d" % i, [128, 1024], F32)) for i in range(4)]
        PS = [PS2[i // 2][:, (i % 2) * 512:(i % 2 + 1) * 512] for i in range(8)]
        psS = Ring([(PS[i], "ps%d" % i) for i in (0, 1, 2, 3)])
        psP = Ring([(PS2[i], ("ps%d" % (2 * i), "ps%d" % (2 * i + 1))) for i in (0, 1)])
        psO = Ring([(PS[i], "ps%d" % i) for i in (4, 5)])
        psB = Ring([(PS[i], "ps%d" % i) for i in (6, 7)])
        psF = Ring([(PS[i], "ps%d" % i) for i in range(8)])
        psM = Ring([(PS[i], "ps%d" % i) for i in (4, 6, 5, 7)])

        s.op("dve", I("memset", ones_bf[:], 1.0), writes=["ones_bf"])
        s.op("dve", I("memset", ones_f[:], 1.0), writes=["ones_f"])
        s.op("dve", I("memset", blk32[:], 0.0), writes=["blk32"])
        s.op("dve", I("memset", blk32[0:32, 0:32], 1.0), writes=["blk32"])
        s.op("dve", I("memset", blk32[32:64, 32:64], 1.0), writes=["blk32"])
        s.op("dve", I("memset", dummy[:], 0.0), writes=["dummy"])
        s.op("sp", I("dma_start", out=R0[:], in_=cR.rearrange("a k m -> k a m")), writes=["R0"], dmakey="c_R0")
        s.op("sp", I("dma_start", out=cLt[:], in_=cL[:, :]), writes=["cLt"], dmakey="c_L")
        s.op("sp", I("dma_start", out=cW0t[:], in_=cW0[:, :]), writes=["cW0t"], dmakey="c_W0")
        s.op("pool", I("dma_start", out=cMst[:], in_=cMs[:, :]), writes=["cMst"], dmakey="c_Ms")

        def phase_barrier():
            s.op("pool", I("memset", dummy[:, 0:1], 0.0), writes=[GUARD, "dummy"])

        def rstd_from_ssq(ssq_ap, ssq_name, d, inv_n, n=512):
            lt, ln_ = ln_r.next()
            rt, rn = rs_r.next()
            s.op("act", I("activation", out=lt[0:d, 0:n], in_=ssq_ap, func=AF.Ln, bias=EPS, scale=inv_n),
                 reads=[ssq_name], writes=[ln_])
            s.op("act", I("activation", out=rt[0:d, 0:n], in_=lt[0:d, 0:n], func=AF.Exp, scale=-0.5),
                 reads=[ln_], writes=[rn])
            return rt, rn

        def norm_rope(src, src_name, d, onesm, inv_n, g_ap, g_name, rg, rg_name, tabs, tabs_name, dest, dest_names,
                      ncol=512, ring=None):
            n = ncol
            src_names = list(src_name) if isinstance(src_name, (list, tuple)) else [src_name]
            ring = ring or psM
            sqt, sqn = sq_r.next()
            s.op("act", I("activation", out=sqt[0:d, 0:n], in_=src, func=AF.Square),
                 reads=src_names, writes=[sqn])
            if rg is not None:
                qt, qn = qs_r.next()
                s.op("act", I("activation", out=qt[0:d, 0:n], in_=src, func=AF.Copy),
                     reads=src_names, writes=[qn])
            pm, pmn = ring.next()
            s.op("pe", I("matmul", pm[0:d, 0:n], lhsT=onesm, rhs=sqt[0:d, 0:n], start=True, stop=True),
                 reads=[sqn, "ones_bf", "blk32"], writes=[pmn])
            lt, ln_ = ln_r.next()
            rt, rn = rs_r.next()
            s.op("act", I("activation", out=lt[0:d, 0:n], in_=pm[0:d, 0:n], func=AF.Ln, bias=EPS, scale=inv_n),
                 reads=[pmn], writes=[ln_])
            s.op("act", I("activation", out=rt[0:d, 0:n], in_=lt[0:d, 0:n], func=AF.Exp, scale=-0.5),
                 reads=[ln_], writes=[rn])
            if rg is None:
                s.op("dve", I("scalar_tensor_tensor", out=dest, in0=src, scalar=g_ap, in1=rt[0:d, 0:n],
                                                             op0=ALU.mult, op1=ALU.mult),
                     reads=src_names + [g_name, rn, GUARD], writes=dest_names)
                return
            pr, prn = ring.next()
            s.op("pe", I("matmul", pr[0:d, 0:n], lhsT=rg, rhs=qt[0:d, 0:n], start=True, stop=True),
                 reads=[qn, rg_name], writes=[prn])
            a1, a1n = t1_r.next()
            a2, a2n = t2_r.next()
            a3, a3n = t3_r.next()
            cos_ap, sin_ap = tabs
            s.op("dve", I("scalar_tensor_tensor", out=a1[0:d, 0:n], in0=src, scalar=g_ap, in1=cos_ap,
                                                         op0=ALU.mult, op1=ALU.mult),
                 reads=src_names + [g_name, tabs_name], writes=[a1n])
            s.op("dve", I("tensor_tensor", out=a2[0:d, 0:n], in0=pr[0:d, 0:n], in1=sin_ap, op=ALU.mult),
                 reads=[prn, tabs_name], writes=[a2n])
            s.op("dve", I("tensor_tensor", out=a3[0:d, 0:n], in0=a1[0:d, 0:n], in1=a2[0:d, 0:n], op=ALU.add),
                 reads=[a1n, a2n], writes=[a3n])
            s.op("dve", I("tensor_tensor", out=dest, in0=a3[0:d, 0:n], in1=rt[0:d, 0:n], op=ALU.mult),
                 reads=[a3n, rn, GUARD], writes=dest_names)

        def proj_fm(W, wname, c0, M, hbt, hbn):
            p, pn = psS.next()
            for kc in range(8):
                s.op("pe", I("matmul", p[0:M, :], lhsT=W[:, kc, c0:c0 + M], rhs=hbt[:, kc, :],
                             start=(kc == 0), stop=(kc == 7)),
                     reads=[wname, hbn, GUARD], writes=[pn])
            return p, pn

        def proj_v(W, wname, c0, nh, hbt, hbn, t):
            ncols = nh * 64
            for blk in range(4):
                p, pn = psS.next()
                for kc in range(8):
                    s.op("pe", I("matmul",
                        p[:, 0:ncols], lhsT=hbt[:, kc, blk * 128:(blk + 1) * 128], rhs=W[:, kc, c0:c0 + ncols],
                        start=(kc == 0), stop=(kc == 7)),
                        reads=[wname, hbn, GUARD], writes=[pn])
                b = t * 4 + blk
                s.op("act", I("activation",
                    out=VV[:, 0:nh, b, 0:64], in_=p[:, 0:ncols].rearrange("p (h d) -> p h d", h=nh), func=AF.Copy),
                    reads=[pn, GUARD], writes=["V:%d" % b])

        def proj_v1(W, wname, c0, nh, hbt, hbn, blk):
            ncols = nh * 64
            p, pn = psS.next()
            for kc in range(8):
                s.op("pe", I("matmul", p[:, 0:ncols], lhsT=hbt[:, kc, blk * 128:(blk + 1) * 128],
                             rhs=W[:, kc, c0:c0 + ncols], start=(kc == 0), stop=(kc == 7)),
                     reads=[wname, hbn, GUARD], writes=[pn])
            return p, pn

        def evac_v(r, nh, b):
            p, pn = r
            s.op("act", I("activation", out=VV[:, 0:nh, b, 0:64],
                          in_=p[:, 0:nh * 64].rearrange("p (h d) -> p h d", h=nh), func=AF.Copy),
                 reads=[pn, GUARD], writes=["V:%d" % b])

        def load_rope(tab_dram, d, t):
            rt_, rn_ = rp_r.next()
            s.op("sp", I("dma_start", out=rt_[0:d, :, :],
                                             in_=tab_dram[:, :, t * 512:(t + 1) * 512].rearrange("a d t -> d a t")),
                 writes=[rn_], dmakey=rn_)
            return (rt_[0:d, 0, :], rt_[0:d, 1, :]), rn_

        def layer(l, Xin, xin_name, Xout, xout_name):
            pt = ppt[l % 2]
            ptn = "ppt%d" % (l % 2)
            pd = pdt[l % 2]
            pdn = "pdt%d" % (l % 2)
            rgt = RG[l % 2]
            rgn = "RG%d" % (l % 2)

            def pcol(name, j=0, rows=128):
                o, w = PP[name]
                return pt[0:rows, o + j:o + j + 1]

            s.op("sp", I("dma_start", out=pt[:], in_=ppd[l]), writes=[ptn], dmakey=ptn)
            s.op("dve", I("tensor_tensor", out=pd[0:32, 2:3], in0=pcol("lq1", 0, 32), in1=pcol("lk1", 0, 32),
                                                  op=ALU.mult), reads=[ptn], writes=[pdn + "a"])
            s.op("dve", I("tensor_tensor", out=pd[0:32, 3:4], in0=pcol("lq2", 0, 32), in1=pcol("lk2", 0, 32),
                                                  op=ALU.mult), reads=[ptn], writes=[pdn + "a"])
            pm, pmn = psM.next()
            s.op("pe", I("matmul", pm[:, 0:2], lhsT=ones_f[0:32, :], rhs=pd[0:32, 2:4], start=True, stop=True),
                 reads=[pdn + "a", "ones_f"], writes=[pmn])
            s.op("act", I("activation", out=pd[:, 4:6], in_=pm[:, 0:2], func=AF.Exp),
                 reads=[pmn], writes=[pdn + "b"])
            s.op("dve", I("tensor_tensor", out=pd[:, 6:7], in0=pd[:, 5:6], in1=pd[:, 4:5], op=ALU.subtract),
                 reads=[pdn + "b"], writes=[pdn + "c"])
            s.op("dve", I("tensor_tensor", out=pd[:, 0:1], in0=pd[:, 6:7], in1=pcol("lam_init"), op=ALU.subtract),
                 reads=[pdn + "c", ptn], writes=[pdn])
            s.op("dve", I("tensor_tensor", out=pd[:, 1:2], in0=pcol("b_sub"), in1=pcol("oml"), op=ALU.mult),
                 reads=[ptn], writes=[pdn])
            for j, (gname, ri, d) in enumerate([("a_qn", 0, 64), ("a_kn", 0, 64), ("c_qn", 1, 96), ("c_kn", 1, 96)]):
                s.op("dve", I("tensor_scalar",
                    out=rgt[0:d, j, 0:d], in0=R0[0:d, ri, 0:d], scalar1=pcol(gname, 0, d), scalar2=None,
                    op0=ALU.mult), reads=[ptn, "R0"], writes=[rgn])

            xin_fm = Xin.rearrange("(kc p) t -> p kc t", p=128)
            ht_fm = HT.rearrange("(kc p) t -> p kc t", p=128)
            mix_fm = MIX.rearrange("(kc p) t -> p kc t", p=128)
            win_fm = w_in[l].rearrange("(kc p) n -> p kc n", p=128)

            def norm_chunk(Xsrc_fm, xname_fn, gname, tok0, ncol, col_lo, col_hi, dest_fn, dest_names, zero_pad=False):
                pm_, pmn_ = psM.next()
                for kc in range(8):
                    xt_, xn_ = xring.next()
                    if zero_pad:
                        s.op("dve", I("memset", xt_[:, 0:ncol], 0.0), writes=[xn_])
                    s.op("sp", I("dma_start", out=xt_[:, col_lo:col_hi], in_=Xsrc_fm[:, kc, tok0 + col_lo:tok0 + col_hi]),
                        reads=xname_fn(), writes=[xn_], dmakey=xn_)
                    sqt, sqn = sq_r.next()
                    s.op("act", I("activation", out=sqt[:, 0:ncol], in_=xt_[:, 0:ncol],
                                                                          func=AF.Square),
                         reads=[xn_], writes=[sqn])
                    s.op("pe", I("matmul", pm_[:, 0:ncol], lhsT=ones_bf[:, :], rhs=sqt[:, 0:ncol],
                                                                  start=(kc == 0), stop=(kc == 7)),
                         reads=[sqn, "ones_bf"], writes=[pmn_])
                rt, rn = rstd_from_ssq(pm_[:, 0:ncol], pmn_, 128, 1.0 / DM, ncol)
                for kc in range(8):
                    xt_, xn_ = xring.next()
                    if zero_pad:
                        s.op("dve", I("memset", xt_[:, 0:ncol], 0.0), writes=[xn_])
                    s.op("sp", I("dma_start", out=xt_[:, col_lo:col_hi], in_=Xsrc_fm[:, kc, tok0 + col_lo:tok0 + col_hi]),
                        reads=xname_fn(), writes=[xn_], dmakey=xn_)
                    s.op("dve", I("scalar_tensor_tensor", out=dest_fn(kc), in0=xt_[:, 0:ncol], scalar=pcol(gname, kc), in1=rt[:, 0:ncol],
                        op0=ALU.mult, op1=ALU.mult),
                        reads=[xn_, ptn, rn, GUARD], writes=dest_names)

            def load_w(slot, c0, c1):
                W = WSLOT[slot]
                wn = "wslot%d" % slot
                s.op("pool", I("dma_start", out=W[:, :, 0:c1 - c0], in_=win_fm[:, :, c0:c1]),
                     reads=[GUARD], writes=[wn], dmakey=wn)
                return W, wn

            wslot_i = [0]

            first_pass = True
            phase_barrier()
            s.op("dve", I("memset", VV[:, :, :, 64:65], 1.0), reads=[GUARD], writes=["VONES"])
            for mx in mixers:
                s.phase = "L%d proj%s" % (l, mx)
                c0, c1 = MIXER_COLS[mx]
                W, wn = load_w(wslot_i[0] % 2, c0, c1)
                wslot_i[0] += 1
                if mx == "C":
                    s.op("pool", I("dma_start", out=wqb_t[:], in_=c_wqb[l].rearrange("(kc p) n -> p kc n", p=128)),
                         writes=["wqb_t"], dmakey="wqb_t")
                    kvv = c_wkvb[l].rearrange("p (h two d) -> p h two d", h=4, two=2)
                    s.op("pool", I("dma_start", out=wkvn_t[:], in_=kvv[:, :, 0, :]), writes=["wkvn_t"],
                         dmakey="wkvn_t")
                    s.op("pool", I("dma_start", out=wkvv_t[:], in_=kvv[:, :, 1, :]), writes=["wkvv_t"],
                         dmakey="wkvv_t")
                for t in range(NT):
                    hbt, hbn = hring.next()
                    tsl = slice(t * 512, (t + 1) * 512)
                    if first_pass:
                        norm_chunk(xin_fm, lambda t=t: ["%s:%d" % (xin_name, t)], "n1g", t * 512, 512, 0, 512,
                                   lambda kc, hbt=hbt: hbt[:, kc, :], [hbn])
                        s.op("sp", I("dma_start", out=ht_fm[:, :, tsl], in_=hbt[:]),
                             reads=[hbn], writes=["HT:%d" % t], dmakey="st_" + hbn)
                    else:
                        s.op("sp", I("dma_start", out=hbt[:], in_=ht_fm[:, :, tsl]),
                             reads=["HT:%d" % t], writes=[hbn], dmakey="ld_" + hbn)
                    items = []
                    if mx == "A":
                        tabs, tabn = load_rope(ropeA, 64, t)
                        for h in range(4):
                            items.append((
                                lambda h=h: proj_fm(W, wn, A_Q + 64 * h, 64, hbt, hbn),
                                lambda r, h=h: norm_rope(r[0][0:64, :], r[1], 64, ones_bf[0:64, 0:64], 1.0 / 64,
                                                         pcol("a_qn", 0, 64), ptn, rgt[0:64, 0, 0:64], rgn, tabs, tabn,
                                                         QT[0:64, h, tsl], ["Q:%d:%d" % (h, t)])))
                        for h in range(2):
                            items.append((
                                lambda h=h: proj_fm(W, wn, A_K + 64 * h, 64, hbt, hbn),
                                lambda r, h=h: norm_rope(r[0][0:64, :], r[1], 64, ones_bf[0:64, 0:64], 1.0 / 64,
                                                         pcol("a_kn", 0, 64), ptn, rgt[0:64, 1, 0:64], rgn, tabs, tabn,
                                                         KT[0:64, h, tsl], ["K:%d:%d" % (h, t)])))
                        for blk in range(4):
                            items.append((lambda blk=blk: proj_v1(W, wn, A_V, 2, hbt, hbn, blk),
                                          lambda r, blk=blk: evac_v(r, 2, t * 4 + blk)))
                    elif mx in "BD":
                        qg, kg = ("b_qn", "b_kn") if mx == "B" else ("d_qn", "d_kn")
                        onesm = blk32[0:64, 0:64] if mx == "B" else ones_bf[0:64, 0:64]
                        inv_n = 1.0 / 32 if mx == "B" else 1.0 / 64
                        for h in range(4):
                            for (gname, coff, dst, tag) in ((qg, 0, QT, "Q"), (kg, 256, KT, "K")):
                                items.append((
                                    lambda h=h, coff=coff: proj_fm(W, wn, coff + 64 * h, 64, hbt, hbn),
                                    lambda r, h=h, gname=gname, dst=dst, tag=tag: norm_rope(
                                        r[0][0:64, :], r[1], 64, onesm, inv_n, pcol(gname, 0, 64), ptn, None, None, None,
                                        None, dst[0:64, h, tsl], ["%s:%d:%d" % (tag, h, t)])))
                        for blk in range(4):
                            items.append((lambda blk=blk: proj_v1(W, wn, 512, 4, hbt, hbn, blk),
                                          lambda r, blk=blk: evac_v(r, 4, t * 4 + blk)))
                    else:
                        tabs, tabn = load_rope(ropeC, 96, t)

                        def c_qlat1():
                            return proj_fm(W, wn, 0, 128, hbt, hbn), proj_fm(W, wn, 128, 128, hbt, hbn)

                        def c_qlat2(r):
                            pm_, pmn_ = psM.next()
                            for j, (pp_, ppn_) in enumerate(r):
                                sqt, sqn = sq_r.next()
                                s.op("act", I("activation", out=sqt[:, :], in_=pp_[:, :], func=AF.Square),
                                     reads=[ppn_], writes=[sqn])
                                s.op("pe", I("matmul", pm_[:, :], lhsT=ones_bf[:, :], rhs=sqt[:, :],
                                             start=(j == 0), stop=(j == 1)), reads=[sqn, "ones_bf"], writes=[pmn_])
                            rt, rn = rstd_from_ssq(pm_[:, :], pmn_, 128, 1.0 / 256)
                            for j, (pp_, ppn_) in enumerate(r):
                                s.op("dve", I("scalar_tensor_tensor", out=qa_t[:, j, :], in0=pp_[:, :],
                                              scalar=pcol("c_qa", j), in1=rt[:, :], op0=ALU.mult, op1=ALU.mult),
                                     reads=[ppn_, ptn, rn], writes=["qa_t"])

                        items.append((c_qlat1, c_qlat2))
                        items.append((lambda: proj_fm(W, wn, 256, 128, hbt, hbn),
                                      lambda r: norm_rope(r[0][:, :], r[1], 128, ones_bf[:, :], 1.0 / 128, pcol("c_kva"),
                                                          ptn, None, None, None, None, kvn_t[:, :], ["kvn_t"])))
                        items.append((lambda: proj_fm(W, wn, 384, 32, hbt, hbn),
                                      lambda r: s.op("act", I("activation", out=kcraw[64:96, :], in_=r[0][0:32, :],
                                                              func=AF.Copy), reads=[r[1]], writes=["kcraw_r"])))

                        def c_q1(h):
                            p, pn = psS.next()
                            for j in range(2):
                                s.op("pe", I("matmul", p[0:96, :], lhsT=wqb_t[:, j, 96 * h:96 * h + 96], rhs=qa_t[:, j, :],
                                             start=(j == 0), stop=(j == 1)), reads=["wqb_t", "qa_t"], writes=[pn])
                            return p, pn

                        def c_k1(h):
                            p, pn = psS.next()
                            s.op("pe", I("matmul", p[0:64, :], lhsT=wkvn_t[:, h, :], rhs=kvn_t[:, :], start=True, stop=True),
                                 reads=["wkvn_t", "kvn_t"], writes=[pn])
                            return p, pn

                        def c_k2(r, h):
                            s.op("act", I("activation", out=kcraw[0:64, :], in_=r[0][0:64, :], func=AF.Copy),
                                 reads=[r[1]], writes=["kcraw_n"])
                            norm_rope(kcraw[0:96, :], ["kcraw_n", "kcraw_r"], 96, ones_bf[0:96, 0:96], 1.0 / 96,
                                      pcol("c_kn", 0, 96), ptn, rgt[0:96, 3, 0:96], rgn, tabs, tabn,
                                      KT[0:96, h, tsl], ["K:%d:%d" % (h, t)])

                        for h in range(4):
                            items.append((lambda h=h: c_q1(h),
                                          lambda r, h=h: norm_rope(r[0][0:96, :], r[1], 96, ones_bf[0:96, 0:96], 1.0 / 96,
                                                                   pcol("c_qn", 0, 96), ptn, rgt[0:96, 2, 0:96], rgn, tabs,
                                                                   tabn, QT[0:96, h, tsl], ["Q:%d:%d" % (h, t)])))
                            items.append((lambda h=h: c_k1(h), lambda r, h=h: c_k2(r, h)))

                        def c_v1(blk):
                            p, pn = psS.next()
                            s.op("pe", I("matmul", p[:, 0:256], lhsT=kvn_t[:, blk * 128:(blk + 1) * 128],
                                         rhs=wkvv_t[:].rearrange("p h d -> p (h d)"), start=True, stop=True),
                                 reads=["kvn_t", "wkvv_t"], writes=[pn])
                            return p, pn

                        for blk in range(4):
                            items.append((lambda blk=blk: c_v1(blk), lambda r, blk=blk: evac_v(r, 4, t * 4 + blk)))
                    pq = []
                    for (st1, st2) in items:
                        pq.append((st2, st1()))
                        if len(pq) > PIPE_DEPTH:
                            f, r = pq.pop(0)
                            f(r)
                    for f, r in pq:
                        f(r)
                first_pass = False
                if mx == mixers[0] and do_ffn:
                    for r in range(8):
                        s.op("pool", I("dma_start", out=WUPB[r * 128:(r + 1) * 128, :], in_=w_up[l][r * 128:(r + 1) * 128, :]),
                             writes=["WUPB"], dmakey="castup")
                    for r in range(NCF):
                        s.op("pool", I("dma_start", out=WDNB[r * 128:(r + 1) * 128, :], in_=w_down[l][r * 128:(r + 1) * 128, :]),
                             writes=["WDNB"], dmakey="castdn")
                if do_attn:
                    s.phase = "L%d attn%s" % (l, mx)
                    attention(l, mx, pd, pdn, pt, ptn)

            s.phase = "L%d wout" % l
            wo_fm = w_out[l].rearrange("(kc p) n -> p kc n", p=128)
            for half in range(2):
                Wh = WSLOT[half]
                s.op("pool", I("dma_start", out=Wh[:, :, 0:512],
                                                                     in_=wo_fm[:, :, half * 512:(half + 1) * 512]),
                     reads=[GUARD], writes=["wslot%d" % half], dmakey="wslot%d" % half)
            for t in range(NT):
                hbt, hbn = hring.next()
                tsl = slice(t * 512, (t + 1) * 512)
                s.op("sp", I("dma_start", out=hbt[:], in_=mix_fm[:, :, tsl]),
                     reads=["MIX:%d" % t], writes=[hbn], dmakey="ld_" + hbn)
                xl = {}

                def xload(c):
                    xt_, xn_ = xring.next()
                    s.op("sp", I("dma_start", out=xt_[:, :], in_=xin_fm[:, c, tsl]),
                         reads=["%s:%d" % (xin_name, t)], writes=[xn_], dmakey=xn_)
                    xl[c] = (xt_, xn_)

                xload(0)
                xload(1)
                for c in range(8):
                    Wh = WSLOT[c // 4]
                    cc = (c % 4) * 128
                    p, pn = psS.next()
                    for kc in range(8):
                        s.op("pe", I("matmul", p[:, :], lhsT=Wh[:, kc, cc:cc + 128], rhs=hbt[:, kc, :],
                                     start=(kc == 0), stop=(kc == 7)),
                             reads=["wslot%d" % (c // 4), hbn, GUARD], writes=[pn])
                    if c + 2 < 8:
                        xload(c + 2)
                    xt_, xn_ = xl[c]
                    ot, on = xoring.next()
                    s.op("dve", I("tensor_tensor", out=ot[:, :], in0=p[:, :], in1=xt_[:, :], op=ALU.add),
                         reads=[pn, xn_], writes=[on])
                    s.op("sp", I("dma_start", out=XA[c * 128:(c + 1) * 128, tsl], in_=ot[:, :]),
                         reads=[on], writes=["XA:%d" % t], dmakey="st_" + on)

            if not do_ffn:
                return
            phase_barrier()
            s.phase = "L%d ffn" % l
            xa_fm = XA.rearrange("(kc p) t -> p kc t", p=128)
            wup_fm = WUPB.rearrange("(kc p) n -> p kc n", p=128)
            wdn_fm = WDNB.rearrange("(kc p) n -> p kc n", p=128)
            wins = []
            for w in range(9):
                tok0 = 510 * w - 1
                n_in = min(512, S + 1 - tok0)
                lo = 1 if w == 0 else 0
                hi = min(n_in, S - tok0)
                wins.append((w, tok0, n_in, lo, hi))
            wup_i = [0]
            wdn_i = [0]
            def ffn_norm(grp):
                for wi, (w, tok0, n_in, lo, hi) in enumerate(grp):
                    chunks = sorted(set([max(tok0, 0) // 512, min(tok0 + n_in - 1, S - 1) // 512]))
                    norm_chunk(xa_fm, lambda chunks=chunks: ["XA:%d" % c for c in chunks], "n2g", tok0, n_in, lo, hi,
                               lambda kc, wi=wi, n_in=n_in: H2[:, kc, wi, 0:n_in], ["H2:%d" % wi],
                               zero_pad=(lo > 0 or hi < n_in))

            def ffn_up(grp):
                for j in range(NCF):
                    wu = WUP[wup_i[0] % 3]
                    wun = "wup%d" % (wup_i[0] % 3)
                    wup_i[0] += 1
                    for half in range(2):
                        s.op("pool", I("dma_start", out=wu[:, :, half * 128:(half + 1) * 128],
                                       in_=wup_fm[:, :, half * DFF + j * 128:half * DFF + (j + 1) * 128]),
                             reads=[GUARD, "WUPB"], writes=[wun], dmakey=wun)
                    for wi, (w, tok0, n_in, lo, hi) in enumerate(grp):
                        no = n_in - 2
                        t3s = []
                        for half in range(2):
                            p, pn = psS.next()
                            for kc in range(8):
                                s.op("pe", I("matmul", p[:, 0:n_in], lhsT=wu[:, kc, half * 128:(half + 1) * 128], rhs=H2[:, kc, wi, 0:n_in],
                                    start=(kc == 0), stop=(kc == 7)),
                                    reads=[wun, "H2:%d" % wi, GUARD], writes=[pn])
                            cj = half * NCF + j
                            a1, a1n = t1_r.next()
                            a2, a2n = t2_r.next()
                            a3, a3n = (t3_r if half == 0 else tb_r).next()
                            s.op("act", I("activation", out=a1[:, 0:no], in_=p[:, 1:1 + no], func=AF.Identity,
                                bias=pt[:, PP["cb"][0] + cj:PP["cb"][0] + cj + 1],
                                scale=pt[:, PP["cw1"][0] + cj:PP["cw1"][0] + cj + 1]),
                                reads=[pn, ptn], writes=[a1n])
                            s.op("dve", I("scalar_tensor_tensor", out=a2[:, 0:no], in0=p[:, 0:no], scalar=pt[:, PP["cw0"][0] + cj:PP["cw0"][0] + cj + 1],
                                in1=a1[:, 0:no], op0=ALU.mult, op1=ALU.add), reads=[pn, a1n, ptn], writes=[a2n])
                            s.op("dve", I("scalar_tensor_tensor", out=a3[:, 0:no], in0=p[:, 2:2 + no], scalar=pt[:, PP["cw2"][0] + cj:PP["cw2"][0] + cj + 1],
                                in1=a2[:, 0:no], op0=ALU.mult, op1=ALU.add), reads=[pn, a2n, ptn], writes=[a3n])
                            t3s.append((a3, a3n))
                        (gt, gn), (vt, vn) = t3s
                        sg, sgn = ln_r.next()
                        s.op("act", I("activation", out=sg[:, 0:no], in_=gt[:, 0:no],
                                                                                func=AF.Silu),
                             reads=[gn], writes=[sgn])
                        s.op("dve", I("tensor_tensor", out=ACTT[:, j, wi, 0:no], in0=sg[:, 0:no], in1=vt[:, 0:no], op=ALU.mult),
                            reads=[sgn, vn, GUARD], writes=["ACT:%d" % wi])
            def ffn_down(grp):
                for c in range(8):
                    wd = WDN[wdn_i[0] % 2]
                    wdn = "wdn%d" % (wdn_i[0] % 2)
                    wdn_i[0] += 1
                    s.op("pool", I("dma_start", out=wd[:], in_=wdn_fm[:, :, c * 128:(c + 1) * 128]),
                         reads=[GUARD, "WDNB"], writes=[wdn], dmakey=wdn)
                    for wi, (w, tok0, n_in, lo, hi) in enumerate(grp):
                        no = n_in - 2
                        o0 = tok0 + 1
                        p, pn = psS.next()
                        for kc in range(NCF):
                            s.op("pe", I("matmul", p[:, 0:no], lhsT=wd[:, kc, :], rhs=ACTT[:, kc, wi, 0:no],
                                start=(kc == 0), stop=(kc == NCF - 1)),
                                reads=[wdn, "ACT:%d" % wi, GUARD], writes=[pn])
                        chunks = sorted(set([o0 // 512, (o0 + no - 1) // 512]))
                        xt_, xn_ = xring.next()
                        s.op("sp", I("dma_start", out=xt_[:, 0:no], in_=xa_fm[:, c, o0:o0 + no]),
                            reads=["XA:%d" % cc for cc in chunks], writes=[xn_], dmakey=xn_)
                        ot, on = xoring.next()
                        s.op("dve", I("tensor_tensor", out=ot[:, 0:no], in0=p[:, 0:no], in1=xt_[:, 0:no], op=ALU.add),
                            reads=[pn, xn_], writes=[on])
                        s.op("sp", I("dma_start", out=Xout[c * 128:(c + 1) * 128, o0:o0 + no], in_=ot[:, 0:no]),
                            reads=[on], writes=["%s:%d" % (xout_name, cc) for cc in chunks], dmakey="st_" + on)

            groups = [wins[g0:g0 + G] for g0 in range(0, 9, G)]
            ffn_norm(groups[0])
            for gi, grp in enumerate(groups):
                ffn_up(grp)
                if gi + 1 < len(groups):
                    ffn_norm(groups[gi + 1])
                ffn_down(grp)

        def attention(l, mx, pd, pdn, pt, ptn):
            base_head = {"A": 0, "B": 4, "C": 8, "D": 12}[mx]
            d = {"A": 64, "B": 32, "C": 96, "D": 64}[mx]
            scale = float(d) ** -0.5

            def kv_of(h):
                return h // 2 if mx == "A" else h

            def dist_min(k0, q0):
                if k0 + 127 < q0:
                    return q0 - (k0 + 127)
                if k0 > q0 + 511:
                    return k0 - (q0 + 511)
                return 0

            def finalize_one(po, pon):
                rd, rdn = ln_r.next()
                s.op("dve", I("reciprocal", out=rd[64:65, :], in_=po[64:65, :]), reads=[pon], writes=[rdn])
                pb, pbn = psB.next()
                s.op("pe", I("matmul", pb[0:64, :], lhsT=ones_f[64:65, 0:64], rhs=rd[64:65, :], start=True, stop=True),
                     reads=[rdn, "ones_f"], writes=[pbn])
                rc, rcn = fin_r.next()
                s.op("dve", I("tensor_copy", out=rc[:, :], in_=pb[0:64, :]), reads=[pbn], writes=[rcn])
                return rc, rcn

            deferred = []
            pend = []
            WIDE_KEEP = 2
            SLOPE_KEEP = 2

            def do_pv(item, last_flush=False):
                po, pon, Eap, En, kb, kvh, q0, first, last, on_last = item
                while deferred:
                    deferred.pop(0)()
                if mx == "D":
                    c_lo = MS_C0 - (kb * 128 - q0)
                    s.op("dve", I("tensor_tensor", out=Eap, in0=Eap, in1=cMst[:, c_lo:c_lo + 512], op=ALU.mult),
                         reads=[En, "cMst"], writes=[En])
                s.op("pe", I("matmul", po[0:65, :], lhsT=VV[:, kvh, kb, :], rhs=Eap, start=first, stop=last),
                     reads=[En, "V:%d" % kb, "VONES", GUARD], writes=[pon])
                if last and on_last is not None:
                    deferred.append(on_last)

            def flush_to(n):
                while len(pend) > n:
                    do_pv(pend.pop(0))

            for h in range(4):
                slope = (SLOPES[h] if mx == "B" else SLOPES[4 + h]) if mx in "BD" else None
                maps = [(0, 32), (32, 64)] if mx == "B" else [(0, d)]
                kvh = kv_of(h)
                for qc in range(8):
                    q0 = qc * 512
                    if mx == "D":
                        kbs = [kb for kb in range(32) if q0 - 1024 <= kb * 128 < q0 + 512 + 1024]
                    elif mx == "B":
                        kbs = [kb for kb in range(32) if slope * dist_min(kb * 128, q0) <= 60.0]
                    else:
                        kbs = list(range(32))
                    accs = []

                    def fin(accs=accs, h=h, q0=q0, qc=qc):
                        mo, mon = mo_r.next()
                        if mx != "B":
                            po, pon = accs[0]
                            rc, rcn = finalize_one(po, pon)
                            s.op("dve", I("tensor_tensor", out=mo[:, :], in0=po[0:64, :], in1=rc[:, :], op=ALU.mult),
                                 reads=[pon, rcn], writes=[mon])
                        else:
                            ns = []
                            for (po, pon) in accs:
                                rc, rcn = finalize_one(po, pon)
                                nn, nnn = fin_r.next()
                                s.op("dve", I("tensor_tensor", out=nn[:, :], in0=po[0:64, :], in1=rc[:, :], op=ALU.mult),
                                     reads=[pon, rcn], writes=[nnn])
                                ns.append((nn, nnn))
                            (n1, n1n), (n2, n2n) = ns
                            df, dfn = t1_r.next()
                            s.op("dve", I("scalar_tensor_tensor", out=df[0:64, :], in0=n2[:, :], scalar=pd[0:64, 0:1],
                                          in1=n1[:, :], op0=ALU.mult, op1=ALU.add), reads=[n1n, n2n, pdn], writes=[dfn])
                            norm_rope(df[0:64, :], dfn, 64, ones_bf[0:64, 0:64], 1.0 / 64, pd[0:64, 1:2], pdn, None,
                                      None, None, None, mo[:, :], [mon], ring=psB)
                        hh = base_head + h
                        s.op("sp", I("dma_start", out=MIX[hh * 64:(hh + 1) * 64, q0:q0 + 512], in_=mo[:, :]),
                             reads=[mon], writes=["MIX:%d" % qc], dmakey="st_" + mon)

                    for mi, (r0, r1) in enumerate(maps):
                        po, pon = psO.next()
                        accs.append((po, pon))
                        on_last = fin if mi == len(maps) - 1 else None
                        nk = len(kbs)
                        if slope is None:
                            i = 0
                            while i < nk:
                                grp = kbs[i:i + 2]
                                pp2, (pna, pnb) = psP.next()
                                names = [pna, pnb][:len(grp)]
                                for gi, kb in enumerate(grp):
                                    k0 = kb * 128
                                    s.op("pe", I("matmul", pp2[:, gi * 512:(gi + 1) * 512], lhsT=KT[r0:r1, kvh, k0:k0 + 128],
                                                 rhs=QT[r0:r1, h, q0:q0 + 512], start=True, stop=True),
                                         reads=["K:%d:%d" % (kvh, kb // 4), "Q:%d:%d" % (h, qc), GUARD], writes=[names[gi]])
                                flush_to(WIDE_KEEP)
                                Et, En = E_r.next()
                                w = 512 * len(grp)
                                s.op("act", I("activation", out=Et[:, 0:w], in_=pp2[:, 0:w], func=AF.Exp, scale=scale),
                                     reads=names, writes=[En])
                                for gi, kb in enumerate(grp):
                                    idx = i + gi
                                    pend.append((po, pon, Et[:, gi * 512:(gi + 1) * 512], En, kb, kvh, q0, idx == 0,
                                                 idx == nk - 1, on_last))
                                i += 2
                        else:
                            for i, kb in enumerate(kbs):
                                k0 = kb * 128
                                ps_, psn = psS.next()
                                s.op("pe", I("matmul", ps_[:, :], lhsT=KT[r0:r1, kvh, k0:k0 + 128],
                                             rhs=QT[r0:r1, h, q0:q0 + 512], start=True, stop=True),
                                     reads=["K:%d:%d" % (kvh, kb // 4), "Q:%d:%d" % (h, qc), GUARD], writes=[psn])
                                flush_to(SLOPE_KEEP)
                                Et, En = E_r.next()
                                tt, ttn = tb_r.next()
                                if k0 + 127 < q0:
                                    bias_ap, mul, cb = cLt[:, :], slope / scale, -slope * (q0 - k0)
                                elif k0 > q0 + 511:
                                    bias_ap, mul, cb = cLt[:, :], -slope / scale, -slope * (k0 - q0)
                                else:
                                    j = (k0 - q0) // 128
                                    bias_ap, mul, cb = cW0t[:, 384 - 128 * j:896 - 128 * j], slope / scale, 0.0
                                s.op("dve", I("scalar_tensor_tensor", out=tt[:, :], in0=bias_ap, scalar=float(mul),
                                              in1=ps_[:, :], op0=ALU.mult, op1=ALU.add),
                                     reads=[psn, "cLt", "cW0t"], writes=[ttn])
                                s.op("act", I("activation", out=Et[:, 0:512], in_=tt[:, :], func=AF.Exp, scale=scale,
                                              bias=float(cb)), reads=[ttn], writes=[En])
                                pend.append((po, pon, Et[:, 0:512], En, kb, kvh, q0, i == 0, i == nk - 1, on_last))
            while pend:
                do_pv(pend.pop(0))
            while deferred:
                deferred.pop(0)()

        for l in range(n_layers):
            Xin, xin_name = (xT, "X0") if l == 0 else (XB, "XB")
            last = (l == n_layers - 1)
            Xout, xout_name = (outT, "OUT") if last else (XB, "XB")
            if not do_ffn:
                pass
            layer(l, Xin, xin_name, Xout, xout_name)

        finals = [k for k in s.dma_ops if k.startswith("st_")]
        s.emit(st, final_waits=finals)
        counts = {e: len(s.ops[e]) for e in ENGS}
        _CACHE["sched"] = s
    return nc, counts


def _prep_inputs(inputs):
    f32 = np.float32
    consts = _host_constants()
    pp = _pack_pp(inputs)
    shared = {
        "w_in": np.ascontiguousarray(inputs["w_in"], dtype=f32),
        "w_out": np.ascontiguousarray(inputs["w_out"], dtype=f32),
        "w_up": np.ascontiguousarray(inputs["w_up"], dtype=f32),
        "w_down": np.ascontiguousarray(inputs["w_down"], dtype=f32),
        "c_wqb": np.ascontiguousarray(inputs["c_wqb"], dtype=f32),
        "c_wkvb": np.ascontiguousarray(inputs["c_wkvb"], dtype=f32),
        "pp": pp,
    }
    shared.update(consts)
    return shared


def kernel(**inputs):
    x = np.asarray(inputs["x"], dtype=np.float32)
    shared = _prep_inputs(inputs)
    if "nc" not in _CACHE:
        _CACHE["nc"] = build_program()[0]
    nc = _CACHE["nc"]
    in_maps = []
    for c in range(8):
        m = dict(shared)
        m["xT"] = np.ascontiguousarray(x[c].T)
        in_maps.append(m)
    res = run_bass_kernel_spmd(nc, in_maps, core_ids=list(range(8)))
    out = np.stack([np.asarray(r["outT"]).T for r in res.results], axis=0)
    return np.ascontiguousarray(out.astype(np.float32))
```

```python
import contextlib
import math
import numpy as np
import concourse.bass as bass
import concourse.mybir as mybir
from concourse.bass_utils import run_bass_kernel_spmd

F32 = mybir.dt.float32
BF16 = mybir.dt.bfloat16
ALU = mybir.AluOpType
AF = mybir.ActivationFunctionType

S = 4096
DM = 1024
DEPTH = 4
NT = 8
DFF = 2816
NCF = 22
EPS = 1e-6
INW = 2464
ENGS = ("pe", "act", "dve", "pool", "sp")
PIPE_DEPTH = 1


class Buf:
    __slots__ = ("name", "last_w", "readers")

    def __init__(self, name):
        self.name = name
        self.last_w = None
        self.readers = []


class Op:
    __slots__ = ("eng", "fn", "waits", "sig", "sigval", "dmakey", "phase", "fin")

    def __init__(self, eng, fn, dmakey=None):
        self.eng = eng
        self.fn = fn
        self.waits = {}
        self.sig = False
        self.sigval = 0
        self.dmakey = dmakey


class Sched:
    def __init__(self, nc):
        self.nc = nc
        self.ops = {e: [] for e in ENGS}
        self.known = {e: {} for e in ENGS}
        self.dma_ops = {}
        self.bufs = {}
        self.seq = []
        self.phase = "init"

    def buf(self, name):
        b = self.bufs.get(name)
        if b is None:
            b = Buf(name)
            self.bufs[name] = b
        return b

    def op(self, eng, fn, reads=(), writes=(), dmakey=None):
        o = Op(eng, fn, dmakey)
        lst = self.ops[eng]
        idx = len(lst)
        if dmakey is not None:
            dl = self.dma_ops.setdefault(dmakey, [])
            tok = ("dma:" + dmakey, len(dl))
            dl.append(o)
        else:
            tok = (eng, idx)
        deps = {}

        def need(t):
            if t is None:
                return
            k, i = t
            if k == "pe" and tok[0] == "pe":
                return
            if deps.get(k, -1) < i:
                deps[k] = i

        ps_reads = [x for x in reads if x.startswith("ps")]
        if ps_reads:
            reads = [x for x in reads if not x.startswith("ps")]
            writes = list(writes) + [x for x in ps_reads if x not in writes]
        rb = [self.buf(x) for x in reads]
        wb = [self.buf(x) for x in writes]
        for b in rb:
            need(b.last_w)
        for b in wb:
            need(b.last_w)
            for r in b.readers:
                if r[0] != tok[0] or tok[0].startswith("dma:"):
                    need(r)
        kn = self.known[eng]
        for k, i in deps.items():
            if kn.get(k, -1) < i:
                kn[k] = i
                o.waits[k] = i
                if k.startswith("dma:"):
                    self.dma_ops[k[4:]][i].sig = True
                else:
                    self.ops[k][i].sig = True
        for b in rb:
            b.readers.append(tok)
        for b in wb:
            b.last_w = tok
            b.readers = []
        lst.append(o)
        o.phase = self.phase
        self.seq.append(o)
        return o

    def estimate(self, ghz=1.95, sem_lat=150.0, dma_fixed=2000.0, dma_bpns=150.0):
        free = {e: 0.0 for e in ENGS}
        busy = {}
        span = {}
        dma_free = [0.0]

        def fs(ap):
            v = ap.free_size
            return v() if callable(v) else v

        def nb(ap):
            v = ap.nbytes
            return v() if callable(v) else v

        def cost(o):
            m = getattr(o.fn, "meta", None)
            if m is None:
                return 100.0, 0
            fn, args, kw = m
            if fn == "dma_start":
                ap = kw["out"]
                return 60.0, nb(ap)
            if fn == "matmul":
                n = fs(kw["rhs"])
                c = max(n, 64) / ghz + 8
                if kw["rhs"].dtype == F32:
                    c *= 4
                return c, 0
            out = kw.get("out", args[0] if args else None)
            n = fs(out)
            if fn == "activation":
                return 186 + 0.833 * n + 93, 0
            if fn in ("memset",):
                return (n / 2 + 100) / 0.96, 0
            srcs = [kw.get(k) for k in ("in0", "in1", "in_") if kw.get(k) is not None]
            psum = any(type(a.tensor).__name__.startswith("PSum") for a in srcs if hasattr(a, "tensor"))
            allbf = all(getattr(a, "dtype", None) == BF16 for a in srcs + [out] if hasattr(a, "dtype"))
            if fn in ("tensor_tensor", "scalar_tensor_tensor"):
                rate = 2.0 if (allbf and not psum) else 1.0
            else:
                rate = 1.0 if psum else 2.0
            return (n / rate + 151) / 0.96, 0

        for o in self.seq:
            st_ = free[o.eng]
            for k, i in o.waits.items():
                dep = self.dma_ops[k[4:]][i] if k.startswith("dma:") else self.ops[k][i]
                st_ = max(st_, dep.fin + (sem_lat if k != o.eng else 60.0))
            c, nbytes = cost(o)
            if o.dmakey is not None:
                free[o.eng] = st_ + c
                t0 = max(st_ + c, dma_free[0])
                dma_free[0] = t0 + nbytes / dma_bpns
                o.fin = max(st_ + dma_fixed, dma_free[0] + 500.0)
                b = busy.setdefault(o.phase, {}); b["dma"] = b.get("dma", 0.0) + nbytes / dma_bpns
            else:
                o.fin = st_ + c
                free[o.eng] = o.fin
                b = busy.setdefault(o.phase, {}); b[o.eng] = b.get(o.eng, 0.0) + c
            sp_ = span.setdefault(o.phase, [st_, o.fin])
            sp_[0] = min(sp_[0], st_); sp_[1] = max(sp_[1], o.fin)
        total = max(o.fin for o in self.seq)
        return total, span, busy

    def emit(self, st, final_waits=()):
        nc = self.nc
        sems = {}
        for e in ENGS:
            sems[e] = st.enter_context(nc.semaphore("s_" + e))
        for k in self.dma_ops:
            sems["dma:" + k] = st.enter_context(nc.semaphore("d_" + k))
        for e in ENGS:
            c = 0
            for o in self.ops[e]:
                if o.dmakey is None and o.sig:
                    c += 1
                o.sigval = c
        for k, dl in self.dma_ops.items():
            c = 0
            for o in dl:
                c += 16
                o.sigval = c
        block = st.enter_context(nc.Block())
        engmap = {"pe": "tensor", "act": "scalar", "dve": "vector", "pool": "gpsimd", "sp": "sync"}

        def make(e):
            def body(engine):
                for o in self.ops[e]:
                    for k, i in o.waits.items():
                        if k.startswith("dma:"):
                            v = self.dma_ops[k[4:]][i].sigval
                        else:
                            v = self.ops[k][i].sigval
                        engine.wait_ge(sems[k], v)
                    ins = o.fn(engine)
                    if o.dmakey is not None:
                        ins.then_inc(sems["dma:" + o.dmakey], 16)
                    elif o.sig:
                        ins.then_inc(sems[e], 1)
                if e == "sp":
                    for k in final_waits:
                        dl = self.dma_ops[k]
                        engine.wait_ge(sems["dma:" + k], dl[-1].sigval)
            return body

        for e in ENGS:
            getattr(block, engmap[e])(make(e))


class Ring:
    def __init__(self, items):
        self.items = items
        self.i = 0

    def next(self):
        it = self.items[self.i % len(self.items)]
        self.i += 1
        return it


def I(fn, *args, **kw):
    f = lambda e: getattr(e, fn)(*args, **kw)
    f.meta = (fn, args, kw)
    return f


PP = {}
_off = 0
for _n, _w in [("n1g", 8), ("n2g", 8), ("cw0", NCF * 2), ("cw1", NCF * 2), ("cw2", NCF * 2), ("cb", NCF * 2),
               ("a_qn", 1), ("a_kn", 1), ("b_qn", 1), ("b_kn", 1), ("b_sub", 1), ("c_qa", 2), ("c_kva", 1),
               ("c_qn", 1), ("c_kn", 1), ("d_qn", 1), ("d_kn", 1), ("lq1", 1), ("lk1", 1), ("lq2", 1),
               ("lk2", 1), ("lam_init", 1), ("oml", 1)]:
    PP[_n] = (_off, _w)
    _off += _w
NPP = _off

A_Q, A_K, A_V = 0, 256, 384
B_Q, B_K, B_V = 512, 768, 1024
C_QL, C_KVL, C_KR = 1280, 1536, 1664
D_Q, D_K, D_V = 1696, 1952, 2208
MIXER_COLS = {"A": (0, 512), "B": (512, 1280), "C": (1280, 1696), "D": (1696, 2464)}

SLOPES = [2.0 ** (-(i + 1)) for i in range(8)]
MS_C0 = 1408
MS_W = 2944


def _host_constants():
    f32 = np.float32
    pos = np.arange(S, dtype=f32)
    rows = np.floor(pos / 64).astype(f32)
    cols = (pos - rows * 64).astype(f32)

    def freqs(dim):
        return (1.0 / (f32(10000.0) ** (np.arange(0, dim, 2, dtype=f32) / f32(dim)))).astype(f32)

    fa = freqs(32)
    ang_row = (rows[:, None] * fa[None, :]).astype(f32)
    ang_col = (cols[:, None] * fa[None, :]).astype(f32)
    ang_mla = (pos[:, None] * freqs(32)[None, :]).astype(f32)
    ropeA = np.zeros((2, 64, S), f32)
    for m in range(64):
        ang = ang_row if m < 32 else ang_col
        f = (m % 32) % 16
        ropeA[0, m] = np.cos(ang[:, f])
        ropeA[1, m] = np.sin(ang[:, f])
    ropeC = np.zeros((2, 96, S), f32)
    ropeC[0, :64] = 1.0
    for m in range(64, 96):
        f = (m - 64) % 16
        ropeC[0, m] = np.cos(ang_mla[:, f])
        ropeC[1, m] = np.sin(ang_mla[:, f])
    cR = np.zeros((2, 96, 96), f32)
    for base in (0, 32):
        for j in range(16):
            cR[0, base + j + 16, base + j] = -1.0
            cR[0, base + j, base + j + 16] = 1.0
    for j in range(16):
        cR[1, 64 + j + 16, 64 + j] = -1.0
        cR[1, 64 + j, 64 + j + 16] = 1.0
    ki = np.arange(128, dtype=f32)[:, None]
    cL = (ki - np.arange(512, dtype=f32)[None, :]).astype(f32)
    cW0 = (-np.abs(np.arange(896, dtype=f32)[None, :] - ki - 384.0)).astype(f32)
    delta = (np.arange(MS_W)[None, :] - np.arange(128)[:, None] - MS_C0)
    ad = np.abs(delta)
    mult = np.zeros_like(delta, dtype=f32)
    for d in (1, 4, 16):
        mult += ((delta % d == 0) & (ad <= 64 * d)).astype(f32)
    cBD = np.zeros((3, 128, 128), f32)
    for a, b in ((0, 64), (64, 128)):
        cBD[0, a:b, a:b] = 1.0
    for a in range(0, 128, 32):
        cBD[1, a:a + 32, a:a + 32] = 1.0
    cBD[2, 0:96, 0:96] = 1.0
    cBD[2, 96:128, 96:128] = 1.0
    return {"ropeA": ropeA, "ropeC": ropeC, "cR": cR, "cL": cL, "cW0": cW0, "cMs": mult.astype(f32), "cBD": cBD}


def _pack_pp(inp):
    f32 = np.float32
    pp = np.zeros((DEPTH, 128, NPP), f32)

    def put(name, l, arr2d):
        o, w = PP[name]
        r = arr2d.shape[0]
        pp[l, :r, o:o + w] = arr2d

    for l in range(DEPTH):
        put("n1g", l, np.asarray(inp["norm1_g"][l]).reshape(8, 128).T)
        put("n2g", l, np.asarray(inp["norm2_g"][l]).reshape(8, 128).T)
        cw = np.asarray(inp["conv_w"][l])
        for t in range(3):
            put("cw%d" % t, l, cw[t].reshape(2 * NCF, 128).T)
        put("cb", l, np.asarray(inp["conv_b"][l]).reshape(2 * NCF, 128).T)
        put("a_qn", l, np.asarray(inp["a_qn_g"][l])[:, None])
        put("a_kn", l, np.asarray(inp["a_kn_g"][l])[:, None])
        put("b_qn", l, np.concatenate([inp["b_qn_g"][l], inp["b_qn_g"][l]])[:, None])
        put("b_kn", l, np.concatenate([inp["b_kn_g"][l], inp["b_kn_g"][l]])[:, None])
        put("b_sub", l, np.asarray(inp["b_sub_g"][l])[:, None])
        put("c_qa", l, np.asarray(inp["c_qa_g"][l]).reshape(2, 128).T)
        put("c_kva", l, np.asarray(inp["c_kva_g"][l])[:, None])
        put("c_qn", l, np.asarray(inp["c_qn_g"][l])[:, None])
        put("c_kn", l, np.asarray(inp["c_kn_g"][l])[:, None])
        put("d_qn", l, np.asarray(inp["d_qn_g"][l])[:, None])
        put("d_kn", l, np.asarray(inp["d_kn_g"][l])[:, None])
        put("lq1", l, np.asarray(inp["b_lam_q1"][l])[:, None])
        put("lk1", l, np.asarray(inp["b_lam_k1"][l])[:, None])
        put("lq2", l, np.asarray(inp["b_lam_q2"][l])[:, None])
        put("lk2", l, np.asarray(inp["b_lam_k2"][l])[:, None])
        lam_init = 0.8 - 0.6 * math.exp(-0.3 * l)
        put("lam_init", l, np.full((128, 1), lam_init, f32))
        put("oml", l, np.full((128, 1), 1.0 - lam_init, f32))
    return pp


_CACHE = {}


def build_program(n_layers=DEPTH, mixers="ABCD", do_ffn=True, dbg=False, do_attn=True):
    nc = bass.Bass("TRN2", target_bir_lowering=False)
    L = DEPTH

    def din(name, shape, dt=F32):
        return nc.dram_tensor(name, shape, dt, kind="ExternalInput").ap()

    xT = din("xT", [DM, S])
    w_in = din("w_in", [L, DM, INW])
    w_out = din("w_out", [L, DM, DM])
    w_up = din("w_up", [L, DM, 2 * DFF])
    w_down = din("w_down", [L, DFF, DM])
    c_wqb = din("c_wqb", [L, 256, 384])
    c_wkvb = din("c_wkvb", [L, 128, 512])
    ppd = din("pp", [L, 128, NPP])
    ropeA = din("ropeA", [2, 64, S])
    ropeC = din("ropeC", [2, 96, S])
    cR = din("cR", [2, 96, 96])
    cL = din("cL", [128, 512])
    cW0 = din("cW0", [128, 896])
    cMs = din("cMs", [128, MS_W])
    cBD = din("cBD", [3, 128, 128])
    outT = nc.dram_tensor("outT", [DM, S], F32, kind="ExternalOutput").ap()
    skind = "ExternalOutput" if dbg else "Internal"
    XA = nc.dram_tensor("XA", [DM, S], F32, kind=skind).ap()
    XB = nc.dram_tensor("XB", [DM, S], F32, kind=skind).ap()
    HT = nc.dram_tensor("HT", [DM, S], BF16, kind=skind).ap()
    MIX = nc.dram_tensor("MIX", [DM, S], BF16, kind=skind).ap()
    WUPB = nc.dram_tensor("WUPB", [DM, 2 * DFF], BF16).ap()
    WDNB = nc.dram_tensor("WDNB", [DFF, DM], BF16).ap()

    st = contextlib.ExitStack()
    with st:
        def SB(name, shape, dt):
            return st.enter_context(nc.sbuf_tensor(name, shape, dt))

        s = Sched(nc)
        GUARD = "GUARD"

        G = 2
        ARENA_N = 32768 + 4 * 32 * 65
        arena = SB("arena", [128, ARENA_N], BF16)
        QT = arena[:, 0:16384].rearrange("p (h t) -> p h t", h=4)
        KT = arena[:, 16384:32768].rearrange("p (h t) -> p h t", h=4)
        VV = arena[:, 32768:32768 + 4 * 32 * 65].rearrange("p (h b d) -> p h b d", h=4, b=32)
        H2 = arena[:, 0:8 * G * 512].rearrange("p (k w t) -> p k w t", k=8, w=G)
        ACTT = arena[:, 8 * G * 512:(8 + NCF) * G * 512].rearrange("p (k w t) -> p k w t", k=NCF, w=G)
        warena = SB("warena", [128, 12288], BF16)
        WSLOT = [warena[:, i * 6144:(i + 1) * 6144].rearrange("p (k n) -> p k n", k=8) for i in range(2)]
        WUP = [warena[:, i * 2048:(i + 1) * 2048].rearrange("p (k n) -> p k n", k=8) for i in range(3)]
        WDN = [warena[:, 6144 + i * 2816:6144 + (i + 1) * 2816].rearrange("p (k n) -> p k n", k=NCF) for i in range(2)]
        hb = [SB("hbuf%d" % i, [128, 8, 512], BF16) for i in range(2)]
        hring = Ring([(hb[i], "hbuf%d" % i) for i in range(2)])
        xp = [SB("xp%d" % i, [128, 512], F32) for i in range(3)]
        xring = Ring([(xp[i], "xp%d" % i) for i in range(3)])
        xo = [SB("xo%d" % i, [128, 512], F32) for i in range(2)]
        xoring = Ring([(xo[i], "xo%d" % i) for i in range(2)])

        def mkring(prefix, n, shape, dt):
            ts = [SB("%s%d" % (prefix, i), shape, dt) for i in range(n)]
            return Ring([(ts[i], "%s%d" % (prefix, i)) for i in range(n)])

        sq_r = mkring("sqb", 2, [128, 512], BF16)
        qs_r = mkring("qsb", 2, [128, 512], BF16)
        ln_r = mkring("lnv", 2, [128, 512], F32)
        rs_r = mkring("rstd", 2, [128, 512], F32)
        t1_r = mkring("t1_", 2, [128, 512], F32)
        t2_r = mkring("t2_", 2, [128, 512], F32)
        t3_r = mkring("t3_", 2, [128, 512], F32)
        E_r = mkring("E", 3, [128, 1024], BF16)
        tb_r = mkring("tb", 3, [128, 512], F32)
        rp_r = mkring("rope", 2, [96, 2, 512], F32)
        mo_r = mkring("mo", 2, [64, 512], BF16)
        fin_r = mkring("fin", 3, [64, 512], F32)
        qa_t = SB("qa_t", [128, 2, 512], BF16)
        kvn_t = SB("kvn_t", [128, 512], BF16)
        kcraw = SB("kcraw", [96, 512], F32)
        ppt = [SB("ppt%d" % i, [128, NPP], F32) for i in range(2)]
        pdt = [SB("pdt%d" % i, [128, 8], F32) for i in range(2)]
        ones_bf = SB("ones_bf", [128, 128], BF16)
        bdt = SB("bdt", [128, 3, 128], BF16)
        ones_f = SB("ones_f", [128, 128], F32)
        R0 = SB("R0", [96, 2, 96], F32)
        RG = [SB("RG%d" % i, [128, 4, 128], BF16) for i in range(2)]
        cLt = SB("cLt", [128, 512], F32)
        cW0t = SB("cW0t", [128, 896], F32)
        cMst = SB("cMst", [128, MS_W], BF16)
        wqb_t = SB("wqb_t", [128, 2, 416], BF16)
        wkvn_t = SB("wkvn_t", [128, 320], BF16)
        wkvv_t = SB("wkvv_t", [128, 4, 64], BF16)
        dummy = SB("dummy_t", [128, 8], F32)

        PS2 = [st.enter_context(nc.psum_tensor("psd%d" % i, [128, 1024], F32)) for i in range(4)]
        PS = [PS2[i // 2][:, (i % 2) * 512:(i % 2 + 1) * 512] for i in range(8)]
        psS = Ring([(PS[i], "ps%d" % i) for i in (0, 1, 2, 3)])
        psP = Ring([(PS2[i], ("ps%d" % (2 * i), "ps%d" % (2 * i + 1))) for i in (0, 1)])
        psO = Ring([(PS[i], "ps%d" % i) for i in (4, 5)])
        psB = Ring([(PS[i], "ps%d" % i) for i in (6, 7)])
        psF = Ring([(PS[i], "ps%d" % i) for i in range(8)])
        psM = Ring([(PS[i], "ps%d" % i) for i in (4, 6, 5, 7)])

        s.op("dve", I("memset", ones_bf[:], 1.0), writes=["ones_bf"])
        s.op("dve", I("memset", ones_f[:], 1.0), writes=["ones_f"])
        s.op("pool", I("dma_start", out=bdt[:], in_=cBD.rearrange("a k m -> k a m")), writes=["blk32"], dmakey="c_BD")
        for i in range(2):
            s.op("dve", I("memset", RG[i][:], 0.0), writes=["RG%d" % i])
        for (rt_, rn_) in sq_r.items + qs_r.items:
            s.op("dve", I("memset", rt_[:], 0.0), writes=[rn_])
        s.op("dve", I("memset", wqb_t[:], 0.0), writes=["wqb_t"])
        s.op("dve", I("memset", wkvn_t[:], 0.0), writes=["wkvn_t"])
        s.op("dve", I("memset", dummy[:], 0.0), writes=["dummy"])
        s.op("sp", I("dma_start", out=R0[:], in_=cR.rearrange("a k m -> k a m")), writes=["R0"], dmakey="c_R0")
        s.op("sp", I("dma_start", out=cLt[:], in_=cL[:, :]), writes=["cLt"], dmakey="c_L")
        s.op("sp", I("dma_start", out=cW0t[:], in_=cW0[:, :]), writes=["cW0t"], dmakey="c_W0")
        s.op("pool", I("dma_start", out=cMst[:], in_=cMs[:, :]), writes=["cMst"], dmakey="c_Ms")

        def phase_barrier():
            s.op("pool", I("memset", dummy[:, 0:1], 0.0), writes=[GUARD, "dummy"])

        def rstd_from_ssq(ssq_ap, ssq_name, d, inv_n, n=512):
            lt, ln_ = ln_r.next()
            rt, rn = rs_r.next()
            s.op("act", I("activation", out=lt[0:d, 0:n], in_=ssq_ap, func=AF.Ln, bias=EPS, scale=inv_n),
                 reads=[ssq_name], writes=[ln_])
            s.op("act", I("activation", out=rt[0:d, 0:n], in_=lt[0:d, 0:n], func=AF.Exp, scale=-0.5),
                 reads=[ln_], writes=[rn])
            return rt, rn

        def norm_rope(src, src_name, d, onesm, inv_n, g_ap, g_name, rg, rg_name, tabs, tabs_name, dest, dest_names,
                      ncol=512, ring=None):
            n = ncol
            src_names = list(src_name) if isinstance(src_name, (list, tuple)) else [src_name]
            ring = ring or psM
            sqt, sqn = sq_r.next()
            s.op("act", I("activation", out=sqt[0:d, 0:n], in_=src, func=AF.Square),
                 reads=src_names, writes=[sqn])
            if rg is not None:
                qt, qn = qs_r.next()
                s.op("act", I("activation", out=qt[0:d, 0:n], in_=src, func=AF.Copy),
                     reads=src_names, writes=[qn])
            pm, pmn = ring.next()
            s.op("pe", I("matmul", pm[:, 0:n], lhsT=onesm, rhs=sqt[:, 0:n], start=True, stop=True),
                 reads=[sqn, "ones_bf", "blk32"], writes=[pmn])
            lt, ln_ = ln_r.next()
            rt, rn = rs_r.next()
            s.op("act", I("activation", out=lt[0:d, 0:n], in_=pm[0:d, 0:n], func=AF.Ln, bias=EPS, scale=inv_n),
                 reads=[pmn], writes=[ln_])
            s.op("act", I("activation", out=rt[0:d, 0:n], in_=lt[0:d, 0:n], func=AF.Exp, scale=-0.5),
                 reads=[ln_], writes=[rn])
            if rg is None:
                s.op("dve", I("scalar_tensor_tensor", out=dest, in0=src, scalar=g_ap, in1=rt[0:d, 0:n],
                                                             op0=ALU.mult, op1=ALU.mult),
                     reads=src_names + [g_name, rn, GUARD], writes=dest_names)
                return
            pr, prn = ring.next()
            s.op("pe", I("matmul", pr[:, 0:n], lhsT=rg, rhs=qt[:, 0:n], start=True, stop=True),
                 reads=[qn, rg_name], writes=[prn])
            a1, a1n = t1_r.next()
            a2, a2n = t2_r.next()
            a3, a3n = t3_r.next()
            cos_ap, sin_ap = tabs
            s.op("dve", I("scalar_tensor_tensor", out=a1[0:d, 0:n], in0=src, scalar=g_ap, in1=cos_ap,
                                                         op0=ALU.mult, op1=ALU.mult),
                 reads=src_names + [g_name, tabs_name], writes=[a1n])
            s.op("dve", I("tensor_tensor", out=a2[0:d, 0:n], in0=pr[0:d, 0:n], in1=sin_ap, op=ALU.mult),
                 reads=[prn, tabs_name], writes=[a2n])
            s.op("dve", I("tensor_tensor", out=a3[0:d, 0:n], in0=a1[0:d, 0:n], in1=a2[0:d, 0:n], op=ALU.add),
                 reads=[a1n, a2n], writes=[a3n])
            s.op("dve", I("tensor_tensor", out=dest, in0=a3[0:d, 0:n], in1=rt[0:d, 0:n], op=ALU.mult),
                 reads=[a3n, rn, GUARD], writes=dest_names)

        def proj_fm(W, wname, c0, M, hbt, hbn):
            p, pn = psS.next()
            for kc in range(8):
                s.op("pe", I("matmul", p[:, :], lhsT=W[:, kc, c0:c0 + 128], rhs=hbt[:, kc, :],
                             start=(kc == 0), stop=(kc == 7)),
                     reads=[wname, hbn, GUARD], writes=[pn])
            return p, pn

        def proj_v(W, wname, c0, nh, hbt, hbn, t):
            ncols = nh * 64
            for blk in range(4):
                p, pn = psS.next()
                for kc in range(8):
                    s.op("pe", I("matmul",
                        p[:, 0:ncols], lhsT=hbt[:, kc, blk * 128:(blk + 1) * 128], rhs=W[:, kc, c0:c0 + ncols],
                        start=(kc == 0), stop=(kc == 7)),
                        reads=[wname, hbn, GUARD], writes=[pn])
                b = t * 4 + blk
                s.op("act", I("activation",
                    out=VV[:, 0:nh, b, 0:64], in_=p[:, 0:ncols].rearrange("p (h d) -> p h d", h=nh), func=AF.Copy),
                    reads=[pn, GUARD], writes=["V:%d" % b])

        def proj_v1(W, wname, c0, nh, hbt, hbn, blk):
            ncols = nh * 64
            p, pn = psS.next()
            for kc in range(8):
                s.op("pe", I("matmul", p[:, 0:ncols], lhsT=hbt[:, kc, blk * 128:(blk + 1) * 128],
                             rhs=W[:, kc, c0:c0 + ncols], start=(kc == 0), stop=(kc == 7)),
                     reads=[wname, hbn, GUARD], writes=[pn])
            return p, pn

        def evac_v(r, nh, b):
            p, pn = r
            s.op("act", I("activation", out=VV[:, 0:nh, b, 0:64],
                          in_=p[:, 0:nh * 64].rearrange("p (h d) -> p h d", h=nh), func=AF.Copy),
                 reads=[pn, GUARD], writes=["V:%d" % b])

        def load_rope(tab_dram, d, t):
            rt_, rn_ = rp_r.next()
            s.op("sp", I("dma_start", out=rt_[0:d, :, :],
                                             in_=tab_dram[:, :, t * 512:(t + 1) * 512].rearrange("a d t -> d a t")),
                 writes=[rn_], dmakey=rn_)
            return (rt_[0:d, 0, :], rt_[0:d, 1, :]), rn_

        def layer(l, Xin, xin_name, Xout, xout_name):
            pt = ppt[l % 2]
            ptn = "ppt%d" % (l % 2)
            pd = pdt[l % 2]
            pdn = "pdt%d" % (l % 2)
            rgt = RG[l % 2]
            rgn = "RG%d" % (l % 2)

            def pcol(name, j=0, rows=128):
                o, w = PP[name]
                return pt[0:rows, o + j:o + j + 1]

            s.op("sp", I("dma_start", out=pt[:], in_=ppd[l]), writes=[ptn], dmakey=ptn)
            s.op("dve", I("tensor_tensor", out=pd[0:32, 2:3], in0=pcol("lq1", 0, 32), in1=pcol("lk1", 0, 32),
                                                  op=ALU.mult), reads=[ptn], writes=[pdn + "a"])
            s.op("dve", I("tensor_tensor", out=pd[0:32, 3:4], in0=pcol("lq2", 0, 32), in1=pcol("lk2", 0, 32),
                                                  op=ALU.mult), reads=[ptn], writes=[pdn + "a"])
            pm, pmn = psM.next()
            s.op("pe", I("matmul", pm[:, 0:2], lhsT=ones_f[0:32, :], rhs=pd[0:32, 2:4], start=True, stop=True),
                 reads=[pdn + "a", "ones_f"], writes=[pmn])
            s.op("act", I("activation", out=pd[:, 4:6], in_=pm[:, 0:2], func=AF.Exp),
                 reads=[pmn], writes=[pdn + "b"])
            s.op("dve", I("tensor_tensor", out=pd[:, 6:7], in0=pd[:, 5:6], in1=pd[:, 4:5], op=ALU.subtract),
                 reads=[pdn + "b"], writes=[pdn + "c"])
            s.op("dve", I("tensor_tensor", out=pd[:, 0:1], in0=pd[:, 6:7], in1=pcol("lam_init"), op=ALU.subtract),
                 reads=[pdn + "c", ptn], writes=[pdn])
            s.op("dve", I("tensor_tensor", out=pd[:, 1:2], in0=pcol("b_sub"), in1=pcol("oml"), op=ALU.mult),
                 reads=[ptn], writes=[pdn])
            for j, (gname, ri, d) in enumerate([("a_qn", 0, 64), ("a_kn", 0, 64), ("c_qn", 1, 96), ("c_kn", 1, 96)]):
                s.op("dve", I("tensor_scalar",
                    out=rgt[0:d, j, 0:d], in0=R0[0:d, ri, 0:d], scalar1=pcol(gname, 0, d), scalar2=None,
                    op0=ALU.mult), reads=[ptn, "R0"], writes=[rgn])

            xin_fm = Xin.rearrange("(kc p) t -> p kc t", p=128)
            ht_fm = HT.rearrange("(kc p) t -> p kc t", p=128)
            mix_fm = MIX.rearrange("(kc p) t -> p kc t", p=128)
            win_fm = w_in[l].rearrange("(kc p) n -> p kc n", p=128)

            def norm_chunk(Xsrc_fm, xname_fn, gname, tok0, ncol, col_lo, col_hi, dest_fn, dest_names, zero_pad=False):
                pm_, pmn_ = psM.next()
                for kc in range(8):
                    xt_, xn_ = xring.next()
                    if zero_pad:
                        s.op("dve", I("memset", xt_[:, 0:ncol], 0.0), writes=[xn_])
                    s.op("sp", I("dma_start", out=xt_[:, col_lo:col_hi], in_=Xsrc_fm[:, kc, tok0 + col_lo:tok0 + col_hi]),
                        reads=xname_fn(), writes=[xn_], dmakey=xn_)
                    sqt, sqn = sq_r.next()
                    s.op("act", I("activation", out=sqt[:, 0:ncol], in_=xt_[:, 0:ncol],
                                                                          func=AF.Square),
                         reads=[xn_], writes=[sqn])
                    s.op("pe", I("matmul", pm_[:, 0:ncol], lhsT=ones_bf[:, :], rhs=sqt[:, 0:ncol],
                                                                  start=(kc == 0), stop=(kc == 7)),
                         reads=[sqn, "ones_bf"], writes=[pmn_])
                rt, rn = rstd_from_ssq(pm_[:, 0:ncol], pmn_, 128, 1.0 / DM, ncol)
                for kc in range(8):
                    xt_, xn_ = xring.next()
                    if zero_pad:
                        s.op("dve", I("memset", xt_[:, 0:ncol], 0.0), writes=[xn_])
                    s.op("sp", I("dma_start", out=xt_[:, col_lo:col_hi], in_=Xsrc_fm[:, kc, tok0 + col_lo:tok0 + col_hi]),
                        reads=xname_fn(), writes=[xn_], dmakey=xn_)
                    s.op("dve", I("scalar_tensor_tensor", out=dest_fn(kc), in0=xt_[:, 0:ncol], scalar=pcol(gname, kc), in1=rt[:, 0:ncol],
                        op0=ALU.mult, op1=ALU.mult),
                        reads=[xn_, ptn, rn, GUARD], writes=dest_names)

            def load_w(slot, c0, c1):
                W = WSLOT[slot]
                wn = "wslot%d" % slot
                s.op("pool", I("dma_start", out=W[:, :, 0:c1 - c0], in_=win_fm[:, :, c0:c1]),
                     reads=[GUARD], writes=[wn], dmakey=wn)
                return W, wn

            wslot_i = [0]

            first_pass = True
            phase_barrier()
            s.op("dve", I("memset", VV[:, :, :, 64:65], 1.0), reads=[GUARD], writes=["VONES"])
            allqk = ["%s:%d:%d" % (a, h, t) for a in "QK" for h in range(4) for t in range(NT)]

            def zero_pad_rows():
                s.op("dve", I("memset", QT[64:128, :, :], 0.0), reads=[GUARD], writes=allqk)
                s.op("dve", I("memset", KT[64:128, :, :], 0.0), reads=[GUARD], writes=allqk)

            zero_pad_rows()
            for mx in mixers:
                s.phase = "L%d proj%s" % (l, mx)
                if mx == "D" and "C" in mixers:
                    zero_pad_rows()
                c0, c1 = MIXER_COLS[mx]
                W, wn = load_w(wslot_i[0] % 2, c0, c1)
                wslot_i[0] += 1
                if mx == "C":
                    s.op("pool", I("dma_start", out=wqb_t[:, :, 0:384], in_=c_wqb[l].rearrange("(kc p) n -> p kc n", p=128)),
                         writes=["wqb_t"], dmakey="wqb_t")
                    kvv = c_wkvb[l].rearrange("p (h two d) -> p h two d", h=4, two=2)
                    s.op("pool", I("dma_start", out=wkvn_t[:, 0:256].rearrange("p (h d) -> p h d", h=4), in_=kvv[:, :, 0, :]), writes=["wkvn_t"],
                         dmakey="wkvn_t")
                    s.op("pool", I("dma_start", out=wkvv_t[:], in_=kvv[:, :, 1, :]), writes=["wkvv_t"],
                         dmakey="wkvv_t")
                for t in range(NT):
                    hbt, hbn = hring.next()
                    tsl = slice(t * 512, (t + 1) * 512)
                    if first_pass:
                        norm_chunk(xin_fm, lambda t=t: ["%s:%d" % (xin_name, t)], "n1g", t * 512, 512, 0, 512,
                                   lambda kc, hbt=hbt: hbt[:, kc, :], [hbn])
                        s.op("sp", I("dma_start", out=ht_fm[:, :, tsl], in_=hbt[:]),
                             reads=[hbn], writes=["HT:%d" % t], dmakey="st_" + hbn)
                    else:
                        s.op("sp", I("dma_start", out=hbt[:], in_=ht_fm[:, :, tsl]),
                             reads=["HT:%d" % t], writes=[hbn], dmakey="ld_" + hbn)
                    items = []
                    if mx == "A":
                        tabs, tabn = load_rope(ropeA, 64, t)
                        for h in range(4):
                            items.append((
                                lambda h=h: proj_fm(W, wn, A_Q + 64 * h, 64, hbt, hbn),
                                lambda r, h=h: norm_rope(r[0][0:64, :], r[1], 64, bdt[:, 0, :], 1.0 / 64,
                                                         pcol("a_qn", 0, 64), ptn, rgt[:, 0, :], rgn, tabs, tabn,
                                                         QT[0:64, h, tsl], ["Q:%d:%d" % (h, t)])))
                        for h in range(2):
                            items.append((
                                lambda h=h: proj_fm(W, wn, A_K + 64 * h, 64, hbt, hbn),
                                lambda r, h=h: norm_rope(r[0][0:64, :], r[1], 64, bdt[:, 0, :], 1.0 / 64,
                                                         pcol("a_kn", 0, 64), ptn, rgt[:, 1, :], rgn, tabs, tabn,
                                                         KT[0:64, h, tsl], ["K:%d:%d" % (h, t)])))
                        for blk in range(4):
                            items.append((lambda blk=blk: proj_v1(W, wn, A_V, 2, hbt, hbn, blk),
                                          lambda r, blk=blk: evac_v(r, 2, t * 4 + blk)))
                    elif mx in "BD":
                        qg, kg = ("b_qn", "b_kn") if mx == "B" else ("d_qn", "d_kn")
                        onesm = bdt[:, 1, :] if mx == "B" else bdt[:, 0, :]
                        inv_n = 1.0 / 32 if mx == "B" else 1.0 / 64
                        for h in range(4):
                            for (gname, coff, dst, tag) in ((qg, 0, QT, "Q"), (kg, 256, KT, "K")):
                                items.append((
                                    lambda h=h, coff=coff: proj_fm(W, wn, coff + 64 * h, 64, hbt, hbn),
                                    lambda r, h=h, gname=gname, dst=dst, tag=tag: norm_rope(
                                        r[0][0:64, :], r[1], 64, onesm, inv_n, pcol(gname, 0, 64), ptn, None, None, None,
                                        None, dst[0:64, h, tsl], ["%s:%d:%d" % (tag, h, t)])))
                        for blk in range(4):
                            items.append((lambda blk=blk: proj_v1(W, wn, 512, 4, hbt, hbn, blk),
                                          lambda r, blk=blk: evac_v(r, 4, t * 4 + blk)))
                    else:
                        tabs, tabn = load_rope(ropeC, 96, t)

                        def c_qlat1():
                            return proj_fm(W, wn, 0, 128, hbt, hbn), proj_fm(W, wn, 128, 128, hbt, hbn)

                        def c_qlat2(r):
                            pm_, pmn_ = psM.next()
                            for j, (pp_, ppn_) in enumerate(r):
                                sqt, sqn = sq_r.next()
                                s.op("act", I("activation", out=sqt[:, :], in_=pp_[:, :], func=AF.Square),
                                     reads=[ppn_], writes=[sqn])
                                s.op("pe", I("matmul", pm_[:, :], lhsT=ones_bf[:, :], rhs=sqt[:, :],
                                             start=(j == 0), stop=(j == 1)), reads=[sqn, "ones_bf"], writes=[pmn_])
                            rt, rn = rstd_from_ssq(pm_[:, :], pmn_, 128, 1.0 / 256)
                            for j, (pp_, ppn_) in enumerate(r):
                                s.op("dve", I("scalar_tensor_tensor", out=qa_t[:, j, :], in0=pp_[:, :],
                                              scalar=pcol("c_qa", j), in1=rt[:, :], op0=ALU.mult, op1=ALU.mult),
                                     reads=[ppn_, ptn, rn], writes=["qa_t"])

                        items.append((c_qlat1, c_qlat2))
                        items.append((lambda: proj_fm(W, wn, 256, 128, hbt, hbn),
                                      lambda r: norm_rope(r[0][:, :], r[1], 128, ones_bf[:, :], 1.0 / 128, pcol("c_kva"),
                                                          ptn, None, None, None, None, kvn_t[:, :], ["kvn_t"])))
                        items.append((lambda: proj_fm(W, wn, 384, 32, hbt, hbn),
                                      lambda r: s.op("act", I("activation", out=kcraw[64:96, :], in_=r[0][0:32, :],
                                                              func=AF.Copy), reads=[r[1]], writes=["kcraw_r"])))

                        def c_q1(h):
                            p, pn = psS.next()
                            for j in range(2):
                                s.op("pe", I("matmul", p[:, :], lhsT=wqb_t[:, j, 96 * h:96 * h + 128], rhs=qa_t[:, j, :],
                                             start=(j == 0), stop=(j == 1)), reads=["wqb_t", "qa_t"], writes=[pn])
                            return p, pn

                        def c_k1(h):
                            p, pn = psS.next()
                            s.op("pe", I("matmul", p[:, :], lhsT=wkvn_t[:, 64 * h:64 * h + 128], rhs=kvn_t[:, :], start=True, stop=True),
                                 reads=["wkvn_t", "kvn_t"], writes=[pn])
                            return p, pn

                        def c_k2(r, h):
                            s.op("act", I("activation", out=kcraw[0:64, :], in_=r[0][0:64, :], func=AF.Copy),
                                 reads=[r[1]], writes=["kcraw_n"])
                            norm_rope(kcraw[0:96, :], ["kcraw_n", "kcraw_r"], 96, bdt[:, 2, :], 1.0 / 96,
                                      pcol("c_kn", 0, 96), ptn, rgt[:, 3, :], rgn, tabs, tabn,
                                      KT[0:96, h, tsl], ["K:%d:%d" % (h, t)])

                        for h in range(4):
                            items.append((lambda h=h: c_q1(h),
                                          lambda r, h=h: norm_rope(r[0][0:96, :], r[1], 96, bdt[:, 2, :], 1.0 / 96,
                                                                   pcol("c_qn", 0, 96), ptn, rgt[:, 2, :], rgn, tabs,
                                                                   tabn, QT[0:96, h, tsl], ["Q:%d:%d" % (h, t)])))
                            items.append((lambda h=h: c_k1(h), lambda r, h=h: c_k2(r, h)))

                        def c_v1(blk):
                            p, pn = psS.next()
                            s.op("pe", I("matmul", p[:, 0:256], lhsT=kvn_t[:, blk * 128:(blk + 1) * 128],
                                         rhs=wkvv_t[:].rearrange("p h d -> p (h d)"), start=True, stop=True),
                                 reads=["kvn_t", "wkvv_t"], writes=[pn])
                            return p, pn

                        for blk in range(4):
                            items.append((lambda blk=blk: c_v1(blk), lambda r, blk=blk: evac_v(r, 4, t * 4 + blk)))
                    pq = []
                    for (st1, st2) in items:
                        pq.append((st2, st1()))
                        if len(pq) > PIPE_DEPTH:
                            f, r = pq.pop(0)
                            f(r)
                    for f, r in pq:
                        f(r)
                first_pass = False
                if mx == mixers[0] and do_ffn:
                    for r in range(8):
                        s.op("pool", I("dma_start", out=WUPB[r * 128:(r + 1) * 128, :], in_=w_up[l][r * 128:(r + 1) * 128, :]),
                             writes=["WUPB"], dmakey="castup")
                    for r in range(NCF):
                        s.op("pool", I("dma_start", out=WDNB[r * 128:(r + 1) * 128, :], in_=w_down[l][r * 128:(r + 1) * 128, :]),
                             writes=["WDNB"], dmakey="castdn")
                if do_attn:
                    s.phase = "L%d attn%s" % (l, mx)
                    attention(l, mx, pd, pdn, pt, ptn)

            s.phase = "L%d wout" % l
            wo_fm = w_out[l].rearrange("(kc p) n -> p kc n", p=128)
            for half in range(2):
                Wh = WSLOT[half]
                s.op("pool", I("dma_start", out=Wh[:, :, 0:512],
                                                                     in_=wo_fm[:, :, half * 512:(half + 1) * 512]),
                     reads=[GUARD], writes=["wslot%d" % half], dmakey="wslot%d" % half)
            for t in range(NT):
                hbt, hbn = hring.next()
                tsl = slice(t * 512, (t + 1) * 512)
                s.op("sp", I("dma_start", out=hbt[:], in_=mix_fm[:, :, tsl]),
                     reads=["MIX:%d" % t], writes=[hbn], dmakey="ld_" + hbn)
                xl = {}

                def xload(c):
                    xt_, xn_ = xring.next()
                    s.op("sp", I("dma_start", out=xt_[:, :], in_=xin_fm[:, c, tsl]),
                         reads=["%s:%d" % (xin_name, t)], writes=[xn_], dmakey=xn_)
                    xl[c] = (xt_, xn_)

                xload(0)
                xload(1)
                for c in range(8):
                    Wh = WSLOT[c // 4]
                    cc = (c % 4) * 128
                    p, pn = psS.next()
                    for kc in range(8):
                        s.op("pe", I("matmul", p[:, :], lhsT=Wh[:, kc, cc:cc + 128], rhs=hbt[:, kc, :],
                                     start=(kc == 0), stop=(kc == 7)),
                             reads=["wslot%d" % (c // 4), hbn, GUARD], writes=[pn])
                    if c + 2 < 8:
                        xload(c + 2)
                    xt_, xn_ = xl[c]
                    ot, on = xoring.next()
                    s.op("dve", I("tensor_tensor", out=ot[:, :], in0=p[:, :], in1=xt_[:, :], op=ALU.add),
                         reads=[pn, xn_], writes=[on])
                    s.op("sp", I("dma_start", out=XA[c * 128:(c + 1) * 128, tsl], in_=ot[:, :]),
                         reads=[on], writes=["XA:%d" % t], dmakey="st_" + on)

            if not do_ffn:
                return
            phase_barrier()
            s.phase = "L%d ffn" % l
            xa_fm = XA.rearrange("(kc p) t -> p kc t", p=128)
            wup_fm = WUPB.rearrange("(kc p) n -> p kc n", p=128)
            wdn_fm = WDNB.rearrange("(kc p) n -> p kc n", p=128)
            wins = []
            for w in range(9):
                tok0 = 510 * w - 1
                n_in = min(512, S + 1 - tok0)
                lo = 1 if w == 0 else 0
                hi = min(n_in, S - tok0)
                wins.append((w, tok0, n_in, lo, hi))
            wup_i = [0]
            wdn_i = [0]
            def ffn_norm(grp):
                for wi, (w, tok0, n_in, lo, hi) in enumerate(grp):
                    chunks = sorted(set([max(tok0, 0) // 512, min(tok0 + n_in - 1, S - 1) // 512]))
                    norm_chunk(xa_fm, lambda chunks=chunks: ["XA:%d" % c for c in chunks], "n2g", tok0, n_in, lo, hi,
                               lambda kc, wi=wi, n_in=n_in: H2[:, kc, wi, 0:n_in], ["H2:%d" % wi],
                               zero_pad=(lo > 0 or hi < n_in))

            def ffn_up(grp):
                for j in range(NCF):
                    wu = WUP[wup_i[0] % 3]
                    wun = "wup%d" % (wup_i[0] % 3)
                    wup_i[0] += 1
                    for half in range(2):
                        s.op("pool", I("dma_start", out=wu[:, :, half * 128:(half + 1) * 128],
                                       in_=wup_fm[:, :, half * DFF + j * 128:half * DFF + (j + 1) * 128]),
                             reads=[GUARD, "WUPB"], writes=[wun], dmakey=wun)
                    for wi, (w, tok0, n_in, lo, hi) in enumerate(grp):
                        no = n_in - 2
                        t3s = []
                        for half in range(2):
                            p, pn = psS.next()
                            for kc in range(8):
                                s.op("pe", I("matmul", p[:, 0:n_in], lhsT=wu[:, kc, half * 128:(half + 1) * 128], rhs=H2[:, kc, wi, 0:n_in],
                                    start=(kc == 0), stop=(kc == 7)),
                                    reads=[wun, "H2:%d" % wi, GUARD], writes=[pn])
                            cj = half * NCF + j
                            a1, a1n = t1_r.next()
                            a2, a2n = t2_r.next()
                            a3, a3n = (t3_r if half == 0 else tb_r).next()
                            s.op("act", I("activation", out=a1[:, 0:no], in_=p[:, 1:1 + no], func=AF.Identity,
                                bias=pt[:, PP["cb"][0] + cj:PP["cb"][0] + cj + 1],
                                scale=pt[:, PP["cw1"][0] + cj:PP["cw1"][0] + cj + 1]),
                                reads=[pn, ptn], writes=[a1n])
                            s.op("dve", I("scalar_tensor_tensor", out=a2[:, 0:no], in0=p[:, 0:no], scalar=pt[:, PP["cw0"][0] + cj:PP["cw0"][0] + cj + 1],
                                in1=a1[:, 0:no], op0=ALU.mult, op1=ALU.add), reads=[pn, a1n, ptn], writes=[a2n])
                            s.op("dve", I("scalar_tensor_tensor", out=a3[:, 0:no], in0=p[:, 2:2 + no], scalar=pt[:, PP["cw2"][0] + cj:PP["cw2"][0] + cj + 1],
                                in1=a2[:, 0:no], op0=ALU.mult, op1=ALU.add), reads=[pn, a2n, ptn], writes=[a3n])
                            t3s.append((a3, a3n))
                        (gt, gn), (vt, vn) = t3s
                        sg, sgn = ln_r.next()
                        s.op("act", I("activation", out=sg[:, 0:no], in_=gt[:, 0:no],
                                                                                func=AF.Silu),
                             reads=[gn], writes=[sgn])
                        s.op("dve", I("tensor_tensor", out=ACTT[:, j, wi, 0:no], in0=sg[:, 0:no], in1=vt[:, 0:no], op=ALU.mult),
                            reads=[sgn, vn, GUARD], writes=["ACT:%d" % wi])
            def ffn_down(grp):
                for c in range(8):
                    wd = WDN[wdn_i[0] % 2]
                    wdn = "wdn%d" % (wdn_i[0] % 2)
                    wdn_i[0] += 1
                    s.op("pool", I("dma_start", out=wd[:], in_=wdn_fm[:, :, c * 128:(c + 1) * 128]),
                         reads=[GUARD, "WDNB"], writes=[wdn], dmakey=wdn)
                    for wi, (w, tok0, n_in, lo, hi) in enumerate(grp):
                        no = n_in - 2
                        o0 = tok0 + 1
                        p, pn = psS.next()
                        for kc in range(NCF):
                            s.op("pe", I("matmul", p[:, 0:no], lhsT=wd[:, kc, :], rhs=ACTT[:, kc, wi, 0:no],
                                start=(kc == 0), stop=(kc == NCF - 1)),
                                reads=[wdn, "ACT:%d" % wi, GUARD], writes=[pn])
                        chunks = sorted(set([o0 // 512, (o0 + no - 1) // 512]))
                        xt_, xn_ = xring.next()
                        s.op("sp", I("dma_start", out=xt_[:, 0:no], in_=xa_fm[:, c, o0:o0 + no]),
                            reads=["XA:%d" % cc for cc in chunks], writes=[xn_], dmakey=xn_)
                        ot, on = xoring.next()
                        s.op("dve", I("tensor_tensor", out=ot[:, 0:no], in0=p[:, 0:no], in1=xt_[:, 0:no], op=ALU.add),
                            reads=[pn, xn_], writes=[on])
                        s.op("sp", I("dma_start", out=Xout[c * 128:(c + 1) * 128, o0:o0 + no], in_=ot[:, 0:no]),
                            reads=[on], writes=["%s:%d" % (xout_name, cc) for cc in chunks], dmakey="st_" + on)

            groups = [wins[g0:g0 + G] for g0 in range(0, 9, G)]
            ffn_norm(groups[0])
            for gi, grp in enumerate(groups):
                ffn_up(grp)
                if gi + 1 < len(groups):
                    ffn_norm(groups[gi + 1])
                ffn_down(grp)

        def attention(l, mx, pd, pdn, pt, ptn):
            base_head = {"A": 0, "B": 4, "C": 8, "D": 12}[mx]
            d = {"A": 64, "B": 32, "C": 96, "D": 64}[mx]
            scale = float(d) ** -0.5

            def kv_of(h):
                return h // 2 if mx == "A" else h

            def dist_min(k0, q0):
                if k0 + 127 < q0:
                    return q0 - (k0 + 127)
                if k0 > q0 + 511:
                    return k0 - (q0 + 511)
                return 0

            def finalize_one(po, pon):
                rd, rdn = ln_r.next()
                s.op("dve", I("reciprocal", out=rd[64:65, :], in_=po[64:65, :]), reads=[pon], writes=[rdn])
                pb, pbn = psB.next()
                s.op("pe", I("matmul", pb[0:64, :], lhsT=ones_f[64:65, 0:64], rhs=rd[64:65, :], start=True, stop=True),
                     reads=[rdn, "ones_f"], writes=[pbn])
                rc, rcn = fin_r.next()
                s.op("dve", I("tensor_copy", out=rc[:, :], in_=pb[0:64, :]), reads=[pbn], writes=[rcn])
                return rc, rcn

            deferred = []
            pend = []
            WIDE_KEEP = 2
            SLOPE_KEEP = 2

            def do_pv(item, last_flush=False):
                po, pon, Eap, En, kb, kvh, q0, first, last, on_last = item
                while deferred:
                    deferred.pop(0)()
                if mx == "D":
                    c_lo = MS_C0 - (kb * 128 - q0)
                    s.op("dve", I("tensor_tensor", out=Eap, in0=Eap, in1=cMst[:, c_lo:c_lo + 512], op=ALU.mult),
                         reads=[En, "cMst"], writes=[En])
                s.op("pe", I("matmul", po[0:65, :], lhsT=VV[:, kvh, kb, :], rhs=Eap, start=first, stop=last),
                     reads=[En, "V:%d" % kb, "VONES", GUARD], writes=[pon])
                if last and on_last is not None:
                    deferred.append(on_last)

            def flush_to(n):
                while len(pend) > n:
                    do_pv(pend.pop(0))

            for h in range(4):
                slope = (SLOPES[h] if mx == "B" else SLOPES[4 + h]) if mx in "BD" else None
                maps = [(0, 32), (32, 64)] if mx == "B" else [(0, d)]
                kvh = kv_of(h)
                for qc in range(8):
                    q0 = qc * 512
                    if mx == "D":
                        kbs = [kb for kb in range(32) if q0 - 1024 <= kb * 128 < q0 + 512 + 1024]
                    elif mx == "B":
                        kbs = [kb for kb in range(32) if slope * dist_min(kb * 128, q0) <= 60.0]
                    else:
                        kbs = list(range(32))
                    accs = []

                    def fin(accs=accs, h=h, q0=q0, qc=qc):
                        mo, mon = mo_r.next()
                        if mx != "B":
                            po, pon = accs[0]
                            rc, rcn = finalize_one(po, pon)
                            s.op("dve", I("tensor_tensor", out=mo[:, :], in0=po[0:64, :], in1=rc[:, :], op=ALU.mult),
                                 reads=[pon, rcn], writes=[mon])
                        else:
                            ns = []
                            for (po, pon) in accs:
                                rc, rcn = finalize_one(po, pon)
                                nn, nnn = fin_r.next()
                                s.op("dve", I("tensor_tensor", out=nn[:, :], in0=po[0:64, :], in1=rc[:, :], op=ALU.mult),
                                     reads=[pon, rcn], writes=[nnn])
                                ns.append((nn, nnn))
                            (n1, n1n), (n2, n2n) = ns
                            df, dfn = t1_r.next()
                            s.op("dve", I("scalar_tensor_tensor", out=df[0:64, :], in0=n2[:, :], scalar=pd[0:64, 0:1],
                                          in1=n1[:, :], op0=ALU.mult, op1=ALU.add), reads=[n1n, n2n, pdn], writes=[dfn])
                            norm_rope(df[0:64, :], dfn, 64, bdt[:, 0, :], 1.0 / 64, pd[0:64, 1:2], pdn, None,
                                      None, None, None, mo[:, :], [mon], ring=psB)
                        hh = base_head + h
                        s.op("sp", I("dma_start", out=MIX[hh * 64:(hh + 1) * 64, q0:q0 + 512], in_=mo[:, :]),
                             reads=[mon], writes=["MIX:%d" % qc], dmakey="st_" + mon)

                    for mi, (r0, r1) in enumerate(maps):
                        po, pon = psO.next()
                        accs.append((po, pon))
                        on_last = fin if mi == len(maps) - 1 else None
                        nk = len(kbs)
                        if slope is None:
                            i = 0
                            while i < nk:
                                grp = kbs[i:i + 2]
                                pp2, (pna, pnb) = psP.next()
                                names = [pna, pnb][:len(grp)]
                                for gi, kb in enumerate(grp):
                                    k0 = kb * 128
                                    s.op("pe", I("matmul", pp2[:, gi * 512:(gi + 1) * 512], lhsT=KT[:, kvh, k0:k0 + 128],
                                                 rhs=QT[:, h, q0:q0 + 512], start=True, stop=True),
                                         reads=["K:%d:%d" % (kvh, kb // 4), "Q:%d:%d" % (h, qc), GUARD], writes=[names[gi]])
                                flush_to(WIDE_KEEP)
                                Et, En = E_r.next()
                                w = 512 * len(grp)
                                s.op("act", I("activation", out=Et[:, 0:w], in_=pp2[:, 0:w], func=AF.Exp, scale=scale),
                                     reads=names, writes=[En])
                                for gi, kb in enumerate(grp):
                                    idx = i + gi
                                    pend.append((po, pon, Et[:, gi * 512:(gi + 1) * 512], En, kb, kvh, q0, idx == 0,
                                                 idx == nk - 1, on_last))
                                i += 2
                        else:
                            for i, kb in enumerate(kbs):
                                k0 = kb * 128
                                ps_, psn = psS.next()
                                rr0, rr1 = (r0, r1) if mx == "B" else (0, 128)
                                s.op("pe", I("matmul", ps_[:, :], lhsT=KT[rr0:rr1, kvh, k0:k0 + 128],
                                             rhs=QT[rr0:rr1, h, q0:q0 + 512], start=True, stop=True),
                                     reads=["K:%d:%d" % (kvh, kb // 4), "Q:%d:%d" % (h, qc), GUARD], writes=[psn])
                                flush_to(SLOPE_KEEP)
                                Et, En = E_r.next()
                                tt, ttn = tb_r.next()
                                if k0 + 127 < q0:
                                    bias_ap, mul, cb = cLt[:, :], slope / scale, -slope * (q0 - k0)
                                elif k0 > q0 + 511:
                                    bias_ap, mul, cb = cLt[:, :], -slope / scale, -slope * (k0 - q0)
                                else:
                                    j = (k0 - q0) // 128
                                    bias_ap, mul, cb = cW0t[:, 384 - 128 * j:896 - 128 * j], slope / scale, 0.0
                                s.op("dve", I("scalar_tensor_tensor", out=tt[:, :], in0=bias_ap, scalar=float(mul),
                                              in1=ps_[:, :], op0=ALU.mult, op1=ALU.add),
                                     reads=[psn, "cLt", "cW0t"], writes=[ttn])
                                s.op("act", I("activation", out=Et[:, 0:512], in_=tt[:, :], func=AF.Exp, scale=scale,
                                              bias=float(cb)), reads=[ttn], writes=[En])
                                pend.append((po, pon, Et[:, 0:512], En, kb, kvh, q0, i == 0, i == nk - 1, on_last))
            while pend:
                do_pv(pend.pop(0))
            while deferred:
                deferred.pop(0)()

        for l in range(n_layers):
            Xin, xin_name = (xT, "X0") if l == 0 else (XB, "XB")
            last = (l == n_layers - 1)
            Xout, xout_name = (outT, "OUT") if last else (XB, "XB")
            if not do_ffn:
                pass
            layer(l, Xin, xin_name, Xout, xout_name)

        finals = [k for k in s.dma_ops if k.startswith("st_")]
        s.emit(st, final_waits=finals)
        counts = {e: len(s.ops[e]) for e in ENGS}
        _CACHE["sched"] = s
    return nc, counts


def _prep_inputs(inputs):
    f32 = np.float32
    consts = _host_constants()
    pp = _pack_pp(inputs)
    shared = {
        "w_in": np.ascontiguousarray(inputs["w_in"], dtype=f32),
        "w_out": np.ascontiguousarray(inputs["w_out"], dtype=f32),
        "w_up": np.ascontiguousarray(inputs["w_up"], dtype=f32),
        "w_down": np.ascontiguousarray(inputs["w_down"], dtype=f32),
        "c_wqb": np.ascontiguousarray(inputs["c_wqb"], dtype=f32),
        "c_wkvb": np.ascontiguousarray(inputs["c_wkvb"], dtype=f32),
        "pp": pp,
    }
    shared.update(consts)
    return shared


def kernel(**inputs):
    x = np.asarray(inputs["x"], dtype=np.float32)
    shared = _prep_inputs(inputs)
    if "nc" not in _CACHE:
        _CACHE["nc"] = build_program()[0]
    nc = _CACHE["nc"]
    in_maps = []
    for c in range(8):
        m = dict(shared)
        m["xT"] = np.ascontiguousarray(x[c].T)
        in_maps.append(m)
    res = run_bass_kernel_spmd(nc, in_maps, core_ids=list(range(8)))
    out = np.stack([np.asarray(r["outT"]).T for r in res.results], axis=0)
    return np.ascontiguousarray(out.astype(np.float32))
```

```python
import contextlib
import math
import numpy as np
import concourse.bass as bass
import concourse.mybir as mybir
from concourse.bass_utils import run_bass_kernel_spmd

F32 = mybir.dt.float32
BF16 = mybir.dt.bfloat16
ALU = mybir.AluOpType
AF = mybir.ActivationFunctionType

S = 4096
DM = 1024
DEPTH = 4
NT = 8
DFF = 2816
NCF = 22
EPS = 1e-6
INW = 2464
ENGS = ("pe", "act", "dve", "pool", "sp")
PIPE_DEPTH = 1


class Buf:
    __slots__ = ("name", "last_w", "readers")

    def __init__(self, name):
        self.name = name
        self.last_w = None
        self.readers = []


class Op:
    __slots__ = ("eng", "fn", "waits", "sig", "sigval", "dmakey", "phase", "fin")

    def __init__(self, eng, fn, dmakey=None):
        self.eng = eng
        self.fn = fn
        self.waits = {}
        self.sig = False
        self.sigval = 0
        self.dmakey = dmakey


class Sched:
    def __init__(self, nc):
        self.nc = nc
        self.ops = {e: [] for e in ENGS}
        self.known = {e: {} for e in ENGS}
        self.dma_ops = {}
        self.bufs = {}
        self.seq = []
        self.phase = "init"

    def buf(self, name):
        b = self.bufs.get(name)
        if b is None:
            b = Buf(name)
            self.bufs[name] = b
        return b

    def op(self, eng, fn, reads=(), writes=(), dmakey=None):
        o = Op(eng, fn, dmakey)
        lst = self.ops[eng]
        idx = len(lst)
        if dmakey is not None:
            dl = self.dma_ops.setdefault(dmakey, [])
            tok = ("dma:" + dmakey, len(dl))
            dl.append(o)
        else:
            tok = (eng, idx)
        deps = {}

        def need(t):
            if t is None:
                return
            k, i = t
            if k == "pe" and tok[0] == "pe":
                return
            if deps.get(k, -1) < i:
                deps[k] = i

        ps_reads = [x for x in reads if x.startswith("ps")]
        if ps_reads:
            reads = [x for x in reads if not x.startswith("ps")]
            writes = list(writes) + [x for x in ps_reads if x not in writes]
        rb = [self.buf(x) for x in reads]
        wb = [self.buf(x) for x in writes]
        for b in rb:
            need(b.last_w)
        for b in wb:
            need(b.last_w)
            for r in b.readers:
                if r[0] != tok[0] or tok[0].startswith("dma:"):
                    need(r)
        kn = self.known[eng]
        for k, i in deps.items():
            if kn.get(k, -1) < i:
                kn[k] = i
                o.waits[k] = i
                if k.startswith("dma:"):
                    self.dma_ops[k[4:]][i].sig = True
                else:
                    self.ops[k][i].sig = True
        for b in rb:
            b.readers.append(tok)
        for b in wb:
            b.last_w = tok
            b.readers = []
        lst.append(o)
        o.phase = self.phase
        self.seq.append(o)
        return o

    def estimate(self, ghz=1.95, sem_lat=150.0, dma_fixed=2000.0, dma_bpns=150.0):
        free = {e: 0.0 for e in ENGS}
        busy = {}
        span = {}
        dma_free = [0.0]

        def fs(ap):
            v = ap.free_size
            return v() if callable(v) else v

        def nb(ap):
            v = ap.nbytes
            return v() if callable(v) else v

        def cost(o):
            m = getattr(o.fn, "meta", None)
            if m is None:
                return 100.0, 0
            fn, args, kw = m
            if fn == "dma_start":
                ap = kw["out"]
                return 60.0, nb(ap)
            if fn == "matmul":
                n = fs(kw["rhs"])
                c = max(n, 64) / ghz + 8
                if kw["rhs"].dtype == F32:
                    c *= 4
                return c, 0
            out = kw.get("out", args[0] if args else None)
            n = fs(out)
            if fn == "activation":
                return 186 + 0.833 * n + 93, 0
            if fn in ("memset",):
                return (n / 2 + 100) / 0.96, 0
            srcs = [kw.get(k) for k in ("in0", "in1", "in_") if kw.get(k) is not None]
            psum = any(type(a.tensor).__name__.startswith("PSum") for a in srcs if hasattr(a, "tensor"))
            allbf = all(getattr(a, "dtype", None) == BF16 for a in srcs + [out] if hasattr(a, "dtype"))
            if fn in ("tensor_tensor", "scalar_tensor_tensor"):
                rate = 2.0 if (allbf and not psum) else 1.0
            else:
                rate = 1.0 if psum else 2.0
            return (n / rate + 151) / 0.96, 0

        for o in self.seq:
            st_ = free[o.eng]
            for k, i in o.waits.items():
                dep = self.dma_ops[k[4:]][i] if k.startswith("dma:") else self.ops[k][i]
                st_ = max(st_, dep.fin + (sem_lat if k != o.eng else 60.0))
            c, nbytes = cost(o)
            if o.dmakey is not None:
                free[o.eng] = st_ + c
                t0 = max(st_ + c, dma_free[0])
                dma_free[0] = t0 + nbytes / dma_bpns
                o.fin = max(st_ + dma_fixed, dma_free[0] + 500.0)
                b = busy.setdefault(o.phase, {}); b["dma"] = b.get("dma", 0.0) + nbytes / dma_bpns
            else:
                o.fin = st_ + c
                free[o.eng] = o.fin
                b = busy.setdefault(o.phase, {}); b[o.eng] = b.get(o.eng, 0.0) + c
            sp_ = span.setdefault(o.phase, [st_, o.fin])
            sp_[0] = min(sp_[0], st_); sp_[1] = max(sp_[1], o.fin)
        total = max(o.fin for o in self.seq)
        return total, span, busy

    def emit(self, st, final_waits=()):
        nc = self.nc
        sems = {}
        for e in ENGS:
            sems[e] = st.enter_context(nc.semaphore("s_" + e))
        for k in self.dma_ops:
            sems["dma:" + k] = st.enter_context(nc.semaphore("d_" + k))
        for e in ENGS:
            c = 0
            for o in self.ops[e]:
                if o.dmakey is None and o.sig:
                    c += 1
                o.sigval = c
        for k, dl in self.dma_ops.items():
            c = 0
            for o in dl:
                c += 16
                o.sigval = c
        block = st.enter_context(nc.Block())
        engmap = {"pe": "tensor", "act": "scalar", "dve": "vector", "pool": "gpsimd", "sp": "sync"}

        def make(e):
            def body(engine):
                for o in self.ops[e]:
                    for k, i in o.waits.items():
                        if k.startswith("dma:"):
                            v = self.dma_ops[k[4:]][i].sigval
                        else:
                            v = self.ops[k][i].sigval
                        engine.wait_ge(sems[k], v)
                    ins = o.fn(engine)
                    if o.dmakey is not None:
                        ins.then_inc(sems["dma:" + o.dmakey], 16)
                    elif o.sig:
                        ins.then_inc(sems[e], 1)
                if e == "sp":
                    for k in final_waits:
                        dl = self.dma_ops[k]
                        engine.wait_ge(sems["dma:" + k], dl[-1].sigval)
            return body

        for e in ENGS:
            getattr(block, engmap[e])(make(e))


class Ring:
    def __init__(self, items):
        self.items = items
        self.i = 0

    def next(self):
        it = self.items[self.i % len(self.items)]
        self.i += 1
        return it


def I(fn, *args, **kw):
    f = lambda e: getattr(e, fn)(*args, **kw)
    f.meta = (fn, args, kw)
    return f


PP = {}
_off = 0
for _n, _w in [("n1g", 8), ("n2g", 8), ("cw0", NCF * 2), ("cw1", NCF * 2), ("cw2", NCF * 2), ("cb", NCF * 2),
               ("a_qn", 1), ("a_kn", 1), ("b_qn", 1), ("b_kn", 1), ("b_sub", 1), ("c_qa", 2), ("c_kva", 1),
               ("c_qn", 1), ("c_kn", 1), ("d_qn", 1), ("d_kn", 1), ("lq1", 1), ("lk1", 1), ("lq2", 1),
               ("lk2", 1), ("lam_init", 1), ("oml", 1)]:
    PP[_n] = (_off, _w)
    _off += _w
NPP = _off

A_Q, A_K, A_V = 0, 256, 384
B_Q, B_K, B_V = 512, 768, 1024
C_QL, C_KVL, C_KR = 1280, 1536, 1664
D_Q, D_K, D_V = 1696, 1952, 2208
MIXER_COLS = {"A": (0, 512), "B": (512, 1280), "C": (1280, 1696), "D": (1696, 2464)}

SLOPES = [2.0 ** (-(i + 1)) for i in range(8)]
MS_C0 = 1408
MS_W = 2944


def _host_constants():
    f32 = np.float32
    pos = np.arange(S, dtype=f32)
    rows = np.floor(pos / 64).astype(f32)
    cols = (pos - rows * 64).astype(f32)

    def freqs(dim):
        return (1.0 / (f32(10000.0) ** (np.arange(0, dim, 2, dtype=f32) / f32(dim)))).astype(f32)

    fa = freqs(32)
    ang_row = (rows[:, None] * fa[None, :]).astype(f32)
    ang_col = (cols[:, None] * fa[None, :]).astype(f32)
    ang_mla = (pos[:, None] * freqs(32)[None, :]).astype(f32)
    ropeA = np.zeros((2, 64, S), f32)
    for m in range(64):
        ang = ang_row if m < 32 else ang_col
        f = (m % 32) % 16
        ropeA[0, m] = np.cos(ang[:, f])
        ropeA[1, m] = np.sin(ang[:, f])
    ropeC = np.zeros((2, 96, S), f32)
    ropeC[0, :64] = 1.0
    for m in range(64, 96):
        f = (m - 64) % 16
        ropeC[0, m] = np.cos(ang_mla[:, f])
        ropeC[1, m] = np.sin(ang_mla[:, f])
    cR = np.zeros((2, 96, 96), f32)
    for base in (0, 32):
        for j in range(16):
            cR[0, base + j + 16, base + j] = -1.0
            cR[0, base + j, base + j + 16] = 1.0
    for j in range(16):
        cR[1, 64 + j + 16, 64 + j] = -1.0
        cR[1, 64 + j, 64 + j + 16] = 1.0
    ki = np.arange(128, dtype=f32)[:, None]
    cL = (ki - np.arange(512, dtype=f32)[None, :]).astype(f32)
    cW0 = (-np.abs(np.arange(896, dtype=f32)[None, :] - ki - 384.0)).astype(f32)
    delta = (np.arange(MS_W)[None, :] - np.arange(128)[:, None] - MS_C0)
    ad = np.abs(delta)
    mult = np.zeros_like(delta, dtype=f32)
    for d in (1, 4, 16):
        mult += ((delta % d == 0) & (ad <= 64 * d)).astype(f32)
    cBD = np.zeros((4, 128, 128), f32)
    for a, b in ((0, 64), (64, 128)):
        cBD[0, a:b, a:b] = 1.0
    for a in range(0, 128, 32):
        cBD[1, a:a + 32, a:a + 32] = 1.0
    cBD[2, 0:96, 0:96] = 1.0
    cBD[2, 96:128, 96:128] = 1.0
    cBD[3, 64, 0:64] = 1.0
    return {"ropeA": ropeA, "ropeC": ropeC, "cR": cR, "cL": cL, "cW0": cW0, "cMs": mult.astype(f32), "cBD": cBD}


def _pack_pp(inp):
    f32 = np.float32
    pp = np.zeros((DEPTH, 128, NPP), f32)

    def put(name, l, arr2d):
        o, w = PP[name]
        r = arr2d.shape[0]
        pp[l, :r, o:o + w] = arr2d

    for l in range(DEPTH):
        put("n1g", l, np.asarray(inp["norm1_g"][l]).reshape(8, 128).T)
        put("n2g", l, np.asarray(inp["norm2_g"][l]).reshape(8, 128).T)
        cw = np.asarray(inp["conv_w"][l])
        for t in range(3):
            put("cw%d" % t, l, cw[t].reshape(2 * NCF, 128).T)
        put("cb", l, np.asarray(inp["conv_b"][l]).reshape(2 * NCF, 128).T)
        put("a_qn", l, np.asarray(inp["a_qn_g"][l])[:, None])
        put("a_kn", l, np.asarray(inp["a_kn_g"][l])[:, None])
        put("b_qn", l, np.concatenate([inp["b_qn_g"][l], inp["b_qn_g"][l]])[:, None])
        put("b_kn", l, np.concatenate([inp["b_kn_g"][l], inp["b_kn_g"][l]])[:, None])
        put("b_sub", l, np.asarray(inp["b_sub_g"][l])[:, None])
        put("c_qa", l, np.asarray(inp["c_qa_g"][l]).reshape(2, 128).T)
        put("c_kva", l, np.asarray(inp["c_kva_g"][l])[:, None])
        put("c_qn", l, np.asarray(inp["c_qn_g"][l])[:, None])
        put("c_kn", l, np.asarray(inp["c_kn_g"][l])[:, None])
        put("d_qn", l, np.asarray(inp["d_qn_g"][l])[:, None])
        put("d_kn", l, np.asarray(inp["d_kn_g"][l])[:, None])
        put("lq1", l, np.asarray(inp["b_lam_q1"][l])[:, None])
        put("lk1", l, np.asarray(inp["b_lam_k1"][l])[:, None])
        put("lq2", l, np.asarray(inp["b_lam_q2"][l])[:, None])
        put("lk2", l, np.asarray(inp["b_lam_k2"][l])[:, None])
        lam_init = 0.8 - 0.6 * math.exp(-0.3 * l)
        put("lam_init", l, np.full((128, 1), lam_init, f32))
        put("oml", l, np.full((128, 1), 1.0 - lam_init, f32))
    return pp


_CACHE = {}


def build_program(n_layers=DEPTH, mixers="ABCD", do_ffn=True, dbg=False, do_attn=True):
    nc = bass.Bass("TRN2", target_bir_lowering=False)
    L = DEPTH

    def din(name, shape, dt=F32):
        return nc.dram_tensor(name, shape, dt, kind="ExternalInput").ap()

    xT = din("xT", [DM, S])
    w_in = din("w_in", [L, DM, INW])
    w_out = din("w_out", [L, DM, DM])
    w_up = din("w_up", [L, DM, 2 * DFF])
    w_down = din("w_down", [L, DFF, DM])
    c_wqb = din("c_wqb", [L, 256, 384])
    c_wkvb = din("c_wkvb", [L, 128, 512])
    ppd = din("pp", [L, 128, NPP])
    ropeA = din("ropeA", [2, 64, S])
    ropeC = din("ropeC", [2, 96, S])
    cR = din("cR", [2, 96, 96])
    cL = din("cL", [128, 512])
    cW0 = din("cW0", [128, 896])
    cMs = din("cMs", [128, MS_W])
    cBD = din("cBD", [4, 128, 128])
    outT = nc.dram_tensor("outT", [DM, S], F32, kind="ExternalOutput").ap()
    skind = "ExternalOutput" if dbg else "Internal"
    XA = nc.dram_tensor("XA", [DM, S], F32, kind=skind).ap()
    XB = nc.dram_tensor("XB", [DM, S], F32, kind=skind).ap()
    HT = nc.dram_tensor("HT", [DM, S], BF16, kind=skind).ap()
    MIX = nc.dram_tensor("MIX", [DM, S], BF16, kind=skind).ap()
    WUPB = nc.dram_tensor("WUPB", [DM, 2 * DFF], BF16).ap()
    WDNB = nc.dram_tensor("WDNB", [DFF, DM], BF16).ap()

    st = contextlib.ExitStack()
    with st:
        def SB(name, shape, dt):
            return st.enter_context(nc.sbuf_tensor(name, shape, dt))

        s = Sched(nc)
        GUARD = "GUARD"

        G = 2
        ARENA_N = 32768 + 4 * 32 * 65 + 64
        arena = SB("arena", [128, ARENA_N], BF16)
        QT = arena[:, 0:16384].rearrange("p (h t) -> p h t", h=4)
        KT = arena[:, 16384:32768].rearrange("p (h t) -> p h t", h=4)
        VV = arena[:, 32768:32768 + 4 * 32 * 65].rearrange("p (h b d) -> p h b d", h=4, b=32)
        H2 = arena[:, 0:8 * G * 512].rearrange("p (k w t) -> p k w t", k=8, w=G)
        ACTT = arena[:, 8 * G * 512:(8 + NCF) * G * 512].rearrange("p (k w t) -> p k w t", k=NCF, w=G)
        warena = SB("warena", [128, 12288], BF16)
        WSLOT = [warena[:, i * 6144:(i + 1) * 6144].rearrange("p (k n) -> p k n", k=8) for i in range(2)]
        WUP = [warena[:, i * 2048:(i + 1) * 2048].rearrange("p (k n) -> p k n", k=8) for i in range(3)]
        WDN = [warena[:, 6144 + i * 2816:6144 + (i + 1) * 2816].rearrange("p (k n) -> p k n", k=NCF) for i in range(2)]
        hb = [SB("hbuf%d" % i, [128, 8, 512], BF16) for i in range(2)]
        hring = Ring([(hb[i], "hbuf%d" % i) for i in range(2)])
        xp = [SB("xp%d" % i, [128, 512], F32) for i in range(3)]
        xring = Ring([(xp[i], "xp%d" % i) for i in range(3)])
        xo = [SB("xo%d" % i, [128, 512], F32) for i in range(2)]
        xoring = Ring([(xo[i], "xo%d" % i) for i in range(2)])

        def mkring(prefix, n, shape, dt):
            ts = [SB("%s%d" % (prefix, i), shape, dt) for i in range(n)]
            return Ring([(ts[i], "%s%d" % (prefix, i)) for i in range(n)])

        sq_r = mkring("sqb", 2, [128, 512], BF16)
        qs_r = mkring("qsb", 2, [128, 512], BF16)
        ln_r = mkring("lnv", 2, [128, 512], F32)
        rs_r = mkring("rstd", 2, [128, 512], F32)
        t1_r = mkring("t1_", 2, [128, 512], F32)
        t2_r = mkring("t2_", 2, [128, 512], F32)
        t3_r = mkring("t3_", 2, [128, 512], F32)
        E_r = mkring("E", 3, [128, 1024], BF16)
        tb_r = mkring("tb", 3, [128, 512], F32)
        rp_r = mkring("rope", 2, [96, 2, 512], F32)
        mo_r = mkring("mo", 2, [64, 512], BF16)
        fin_r = mkring("fin", 3, [64, 512], F32)
        qa_t = SB("qa_t", [128, 2, 512], BF16)
        kvn_t = SB("kvn_t", [128, 512], BF16)
        kcraw = SB("kcraw", [96, 512], F32)
        ppt = [SB("ppt%d" % i, [128, NPP], F32) for i in range(2)]
        pdt = [SB("pdt%d" % i, [128, 8], F32) for i in range(2)]
        ones_bf = SB("ones_bf", [128, 128], BF16)
        bdt = SB("bdt", [128, 4, 128], BF16)
        ones_f = SB("ones_f", [128, 128], F32)
        R0 = SB("R0", [96, 2, 96], F32)
        RG = [SB("RG%d" % i, [128, 4, 128], BF16) for i in range(2)]
        cLt = SB("cLt", [128, 512], F32)
        cW0t = SB("cW0t", [128, 896], F32)
        cMst = SB("cMst", [128, MS_W], BF16)
        wqb_t = SB("wqb_t", [128, 2, 416], BF16)
        wkvn_t = SB("wkvn_t", [128, 320], BF16)
        wkvv_t = SB("wkvv_t", [128, 4, 64], BF16)
        dummy = SB("dummy_t", [128, 8], F32)

        PS2 = [st.enter_context(nc.psum_tensor("psd%d" % i, [128, 1024], F32)) for i in range(4)]
        PS = [PS2[i // 2][:, (i % 2) * 512:(i % 2 + 1) * 512] for i in range(8)]
        psS = Ring([(PS[i], "ps%d" % i) for i in (0, 1, 2, 3)])
        psP = Ring([(PS2[i], ("ps%d" % (2 * i), "ps%d" % (2 * i + 1))) for i in (0, 1)])
        psO = Ring([(PS[i], "ps%d" % i) for i in (4, 5)])
        psB = Ring([(PS[i], "ps%d" % i) for i in (6, 7)])
        psF = Ring([(PS[i], "ps%d" % i) for i in range(8)])
        psM = Ring([(PS[i], "ps%d" % i) for i in (4, 6, 5, 7)])

        s.op("dve", I("memset", ones_bf[:], 1.0), writes=["ones_bf"])
        s.op("dve", I("memset", ones_f[:], 1.0), writes=["ones_f"])
        s.op("pool", I("dma_start", out=bdt[:], in_=cBD.rearrange("a k m -> k a m")), writes=["blk32"], dmakey="c_BD")
        for i in range(2):
            s.op("dve", I("memset", RG[i][:], 0.0), writes=["RG%d" % i])
        for (rt_, rn_) in sq_r.items + qs_r.items:
            s.op("dve", I("memset", rt_[:], 0.0), writes=[rn_])
        s.op("dve", I("memset", arena[:, 32768 + 4 * 32 * 65:ARENA_N], 0.0), writes=["VTAIL"])
        s.op("dve", I("memset", wqb_t[:], 0.0), writes=["wqb_t"])
        s.op("dve", I("memset", wkvn_t[:], 0.0), writes=["wkvn_t"])
        s.op("dve", I("memset", dummy[:], 0.0), writes=["dummy"])
        s.op("sp", I("dma_start", out=R0[:], in_=cR.rearrange("a k m -> k a m")), writes=["R0"], dmakey="c_R0")
        s.op("sp", I("dma_start", out=cLt[:], in_=cL[:, :]), writes=["cLt"], dmakey="c_L")
        s.op("sp", I("dma_start", out=cW0t[:], in_=cW0[:, :]), writes=["cW0t"], dmakey="c_W0")
        s.op("pool", I("dma_start", out=cMst[:], in_=cMs[:, :]), writes=["cMst"], dmakey="c_Ms")

        def phase_barrier():
            s.op("pool", I("memset", dummy[:, 0:1], 0.0), writes=[GUARD, "dummy"])

        def rstd_from_ssq(ssq_ap, ssq_name, d, inv_n, n=512):
            lt, ln_ = ln_r.next()
            rt, rn = rs_r.next()
            s.op("act", I("activation", out=lt[0:d, 0:n], in_=ssq_ap, func=AF.Ln, bias=EPS, scale=inv_n),
                 reads=[ssq_name], writes=[ln_])
            s.op("act", I("activation", out=rt[0:d, 0:n], in_=lt[0:d, 0:n], func=AF.Exp, scale=-0.5),
                 reads=[ln_], writes=[rn])
            return rt, rn

        def norm_rope(src, src_name, d, onesm, inv_n, g_ap, g_name, rg, rg_name, tabs, tabs_name, dest, dest_names,
                      ncol=512, ring=None):
            n = ncol
            src_names = list(src_name) if isinstance(src_name, (list, tuple)) else [src_name]
            ring = ring or psM
            sqt, sqn = sq_r.next()
            s.op("act", I("activation", out=sqt[0:d, 0:n], in_=src, func=AF.Square),
                 reads=src_names, writes=[sqn])
            if rg is not None:
                qt, qn = qs_r.next()
                s.op("act", I("activation", out=qt[0:d, 0:n], in_=src, func=AF.Copy),
                     reads=src_names, writes=[qn])
            pm, pmn = ring.next()
            s.op("pe", I("matmul", pm[:, 0:n], lhsT=onesm, rhs=sqt[:, 0:n], start=True, stop=True),
                 reads=[sqn, "ones_bf", "blk32"], writes=[pmn])
            lt, ln_ = ln_r.next()
            rt, rn = rs_r.next()
            s.op("act", I("activation", out=lt[0:d, 0:n], in_=pm[0:d, 0:n], func=AF.Ln, bias=EPS, scale=inv_n),
                 reads=[pmn], writes=[ln_])
            s.op("act", I("activation", out=rt[0:d, 0:n], in_=lt[0:d, 0:n], func=AF.Exp, scale=-0.5),
                 reads=[ln_], writes=[rn])
            if rg is None:
                s.op("dve", I("scalar_tensor_tensor", out=dest, in0=src, scalar=g_ap, in1=rt[0:d, 0:n],
                                                             op0=ALU.mult, op1=ALU.mult),
                     reads=src_names + [g_name, rn, GUARD], writes=dest_names)
                return
            pr, prn = ring.next()
            s.op("pe", I("matmul", pr[:, 0:n], lhsT=rg, rhs=qt[:, 0:n], start=True, stop=True),
                 reads=[qn, rg_name], writes=[prn])
            a1, a1n = t1_r.next()
            a2, a2n = t2_r.next()
            a3, a3n = t3_r.next()
            cos_ap, sin_ap = tabs
            s.op("dve", I("scalar_tensor_tensor", out=a1[0:d, 0:n], in0=src, scalar=g_ap, in1=cos_ap,
                                                         op0=ALU.mult, op1=ALU.mult),
                 reads=src_names + [g_name, tabs_name], writes=[a1n])
            s.op("dve", I("tensor_tensor", out=a2[0:d, 0:n], in0=pr[0:d, 0:n], in1=sin_ap, op=ALU.mult),
                 reads=[prn, tabs_name], writes=[a2n])
            s.op("dve", I("tensor_tensor", out=a3[0:d, 0:n], in0=a1[0:d, 0:n], in1=a2[0:d, 0:n], op=ALU.add),
                 reads=[a1n, a2n], writes=[a3n])
            s.op("dve", I("tensor_tensor", out=dest, in0=a3[0:d, 0:n], in1=rt[0:d, 0:n], op=ALU.mult),
                 reads=[a3n, rn, GUARD], writes=dest_names)

        def proj_fm(W, wname, c0, M, hbt, hbn):
            p, pn = psS.next()
            for kc in range(8):
                s.op("pe", I("matmul", p[:, :], lhsT=W[:, kc, c0:c0 + 128], rhs=hbt[:, kc, :],
                             start=(kc == 0), stop=(kc == 7)),
                     reads=[wname, hbn, GUARD], writes=[pn])
            return p, pn

        def proj_v(W, wname, c0, nh, hbt, hbn, t):
            ncols = nh * 64
            for blk in range(4):
                p, pn = psS.next()
                for kc in range(8):
                    s.op("pe", I("matmul",
                        p[:, 0:ncols], lhsT=hbt[:, kc, blk * 128:(blk + 1) * 128], rhs=W[:, kc, c0:c0 + ncols],
                        start=(kc == 0), stop=(kc == 7)),
                        reads=[wname, hbn, GUARD], writes=[pn])
                b = t * 4 + blk
                s.op("act", I("activation",
                    out=VV[:, 0:nh, b, 0:64], in_=p[:, 0:ncols].rearrange("p (h d) -> p h d", h=nh), func=AF.Copy),
                    reads=[pn, GUARD], writes=["V:%d" % b])

        def proj_v1(W, wname, c0, nh, hbt, hbn, blk):
            ncols = nh * 64
            p, pn = psS.next()
            for kc in range(8):
                s.op("pe", I("matmul", p[:, 0:ncols], lhsT=hbt[:, kc, blk * 128:(blk + 1) * 128],
                             rhs=W[:, kc, c0:c0 + ncols], start=(kc == 0), stop=(kc == 7)),
                     reads=[wname, hbn, GUARD], writes=[pn])
            return p, pn

        def evac_v(r, nh, b):
            p, pn = r
            s.op("act", I("activation", out=VV[:, 0:nh, b, 0:64],
                          in_=p[:, 0:nh * 64].rearrange("p (h d) -> p h d", h=nh), func=AF.Copy),
                 reads=[pn, GUARD], writes=["V:%d" % b])

        def load_rope(tab_dram, d, t):
            rt_, rn_ = rp_r.next()
            s.op("sp", I("dma_start", out=rt_[0:d, :, :],
                                             in_=tab_dram[:, :, t * 512:(t + 1) * 512].rearrange("a d t -> d a t")),
                 writes=[rn_], dmakey=rn_)
            return (rt_[0:d, 0, :], rt_[0:d, 1, :]), rn_

        def layer(l, Xin, xin_name, Xout, xout_name):
            pt = ppt[l % 2]
            ptn = "ppt%d" % (l % 2)
            pd = pdt[l % 2]
            pdn = "pdt%d" % (l % 2)
            rgt = RG[l % 2]
            rgn = "RG%d" % (l % 2)

            def pcol(name, j=0, rows=128):
                o, w = PP[name]
                return pt[0:rows, o + j:o + j + 1]

            s.op("sp", I("dma_start", out=pt[:], in_=ppd[l]), writes=[ptn], dmakey=ptn)
            s.op("dve", I("tensor_tensor", out=pd[0:32, 2:3], in0=pcol("lq1", 0, 32), in1=pcol("lk1", 0, 32),
                                                  op=ALU.mult), reads=[ptn], writes=[pdn + "a"])
            s.op("dve", I("tensor_tensor", out=pd[0:32, 3:4], in0=pcol("lq2", 0, 32), in1=pcol("lk2", 0, 32),
                                                  op=ALU.mult), reads=[ptn], writes=[pdn + "a"])
            pm, pmn = psM.next()
            s.op("pe", I("matmul", pm[:, 0:2], lhsT=ones_f[0:32, :], rhs=pd[0:32, 2:4], start=True, stop=True),
                 reads=[pdn + "a", "ones_f"], writes=[pmn])
            s.op("act", I("activation", out=pd[:, 4:6], in_=pm[:, 0:2], func=AF.Exp),
                 reads=[pmn], writes=[pdn + "b"])
            s.op("dve", I("tensor_tensor", out=pd[:, 6:7], in0=pd[:, 5:6], in1=pd[:, 4:5], op=ALU.subtract),
                 reads=[pdn + "b"], writes=[pdn + "c"])
            s.op("dve", I("tensor_tensor", out=pd[:, 0:1], in0=pd[:, 6:7], in1=pcol("lam_init"), op=ALU.subtract),
                 reads=[pdn + "c", ptn], writes=[pdn])
            s.op("dve", I("tensor_tensor", out=pd[:, 1:2], in0=pcol("b_sub"), in1=pcol("oml"), op=ALU.mult),
                 reads=[ptn], writes=[pdn])
            for j, (gname, ri, d) in enumerate([("a_qn", 0, 64), ("a_kn", 0, 64), ("c_qn", 1, 96), ("c_kn", 1, 96)]):
                s.op("dve", I("tensor_scalar",
                    out=rgt[0:d, j, 0:d], in0=R0[0:d, ri, 0:d], scalar1=pcol(gname, 0, d), scalar2=None,
                    op0=ALU.mult), reads=[ptn, "R0"], writes=[rgn])

            xin_fm = Xin.rearrange("(kc p) t -> p kc t", p=128)
            ht_fm = HT.rearrange("(kc p) t -> p kc t", p=128)
            mix_fm = MIX.rearrange("(kc p) t -> p kc t", p=128)
            win_fm = w_in[l].rearrange("(kc p) n -> p kc n", p=128)

            def norm_chunk(Xsrc_fm, xname_fn, gname, tok0, ncol, col_lo, col_hi, dest_fn, dest_names, zero_pad=False):
                pm_, pmn_ = psM.next()
                for kc in range(8):
                    xt_, xn_ = xring.next()
                    if zero_pad:
                        s.op("dve", I("memset", xt_[:, 0:ncol], 0.0), writes=[xn_])
                    s.op("sp", I("dma_start", out=xt_[:, col_lo:col_hi], in_=Xsrc_fm[:, kc, tok0 + col_lo:tok0 + col_hi]),
                        reads=xname_fn(), writes=[xn_], dmakey=xn_)
                    sqt, sqn = sq_r.next()
                    s.op("act", I("activation", out=sqt[:, 0:ncol], in_=xt_[:, 0:ncol],
                                                                          func=AF.Square),
                         reads=[xn_], writes=[sqn])
                    s.op("pe", I("matmul", pm_[:, 0:ncol], lhsT=ones_bf[:, :], rhs=sqt[:, 0:ncol],
                                                                  start=(kc == 0), stop=(kc == 7)),
                         reads=[sqn, "ones_bf"], writes=[pmn_])
                rt, rn = rstd_from_ssq(pm_[:, 0:ncol], pmn_, 128, 1.0 / DM, ncol)
                for kc in range(8):
                    xt_, xn_ = xring.next()
                    if zero_pad:
                        s.op("dve", I("memset", xt_[:, 0:ncol], 0.0), writes=[xn_])
                    s.op("sp", I("dma_start", out=xt_[:, col_lo:col_hi], in_=Xsrc_fm[:, kc, tok0 + col_lo:tok0 + col_hi]),
                        reads=xname_fn(), writes=[xn_], dmakey=xn_)
                    s.op("dve", I("scalar_tensor_tensor", out=dest_fn(kc), in0=xt_[:, 0:ncol], scalar=pcol(gname, kc), in1=rt[:, 0:ncol],
                        op0=ALU.mult, op1=ALU.mult),
                        reads=[xn_, ptn, rn, GUARD], writes=dest_names)

            def load_w(slot, c0, c1):
                W = WSLOT[slot]
                wn = "wslot%d" % slot
                s.op("pool", I("dma_start", out=W[:, :, 0:c1 - c0], in_=win_fm[:, :, c0:c1]),
                     reads=[GUARD], writes=[wn], dmakey=wn)
                return W, wn

            wslot_i = [0]

            first_pass = True
            phase_barrier()
            s.op("dve", I("memset", VV[:, :, :, 64:65], 1.0), reads=[GUARD], writes=["VONES"])
            allqk = ["%s:%d:%d" % (a, h, t) for a in "QK" for h in range(4) for t in range(NT)]

            def zero_pad_rows():
                s.op("dve", I("memset", QT[64:128, :, :], 0.0), reads=[GUARD], writes=allqk)
                s.op("dve", I("memset", KT[64:128, :, :], 0.0), reads=[GUARD], writes=allqk)

            zero_pad_rows()
            for mx in mixers:
                s.phase = "L%d proj%s" % (l, mx)
                if mx == "D" and "C" in mixers:
                    zero_pad_rows()
                c0, c1 = MIXER_COLS[mx]
                W, wn = load_w(wslot_i[0] % 2, c0, c1)
                wslot_i[0] += 1
                if mx == "C":
                    s.op("pool", I("dma_start", out=wqb_t[:, :, 0:384], in_=c_wqb[l].rearrange("(kc p) n -> p kc n", p=128)),
                         writes=["wqb_t"], dmakey="wqb_t")
                    kvv = c_wkvb[l].rearrange("p (h two d) -> p h two d", h=4, two=2)
                    s.op("pool", I("dma_start", out=wkvn_t[:, 0:256].rearrange("p (h d) -> p h d", h=4), in_=kvv[:, :, 0, :]), writes=["wkvn_t"],
                         dmakey="wkvn_t")
                    s.op("pool", I("dma_start", out=wkvv_t[:], in_=kvv[:, :, 1, :]), writes=["wkvv_t"],
                         dmakey="wkvv_t")
                for t in range(NT):
                    hbt, hbn = hring.next()
                    tsl = slice(t * 512, (t + 1) * 512)
                    if first_pass:
                        norm_chunk(xin_fm, lambda t=t: ["%s:%d" % (xin_name, t)], "n1g", t * 512, 512, 0, 512,
                                   lambda kc, hbt=hbt: hbt[:, kc, :], [hbn])
                        s.op("sp", I("dma_start", out=ht_fm[:, :, tsl], in_=hbt[:]),
                             reads=[hbn], writes=["HT:%d" % t], dmakey="st_" + hbn)
                    else:
                        s.op("sp", I("dma_start", out=hbt[:], in_=ht_fm[:, :, tsl]),
                             reads=["HT:%d" % t], writes=[hbn], dmakey="ld_" + hbn)
                    items = []
                    if mx == "A":
                        tabs, tabn = load_rope(ropeA, 64, t)
                        for h in range(4):
                            items.append((
                                lambda h=h: proj_fm(W, wn, A_Q + 64 * h, 64, hbt, hbn),
                                lambda r, h=h: norm_rope(r[0][0:64, :], r[1], 64, bdt[:, 0, :], 1.0 / 64,
                                                         pcol("a_qn", 0, 64), ptn, rgt[:, 0, :], rgn, tabs, tabn,
                                                         QT[0:64, h, tsl], ["Q:%d:%d" % (h, t)])))
                        for h in range(2):
                            items.append((
                                lambda h=h: proj_fm(W, wn, A_K + 64 * h, 64, hbt, hbn),
                                lambda r, h=h: norm_rope(r[0][0:64, :], r[1], 64, bdt[:, 0, :], 1.0 / 64,
                                                         pcol("a_kn", 0, 64), ptn, rgt[:, 1, :], rgn, tabs, tabn,
                                                         KT[0:64, h, tsl], ["K:%d:%d" % (h, t)])))
                        for blk in range(4):
                            items.append((lambda blk=blk: proj_v1(W, wn, A_V, 2, hbt, hbn, blk),
                                          lambda r, blk=blk: evac_v(r, 2, t * 4 + blk)))
                    elif mx in "BD":
                        qg, kg = ("b_qn", "b_kn") if mx == "B" else ("d_qn", "d_kn")
                        onesm = bdt[:, 1, :] if mx == "B" else bdt[:, 0, :]
                        inv_n = 1.0 / 32 if mx == "B" else 1.0 / 64
                        for h in range(4):
                            for (gname, coff, dst, tag) in ((qg, 0, QT, "Q"), (kg, 256, KT, "K")):
                                items.append((
                                    lambda h=h, coff=coff: proj_fm(W, wn, coff + 64 * h, 64, hbt, hbn),
                                    lambda r, h=h, gname=gname, dst=dst, tag=tag: norm_rope(
                                        r[0][0:64, :], r[1], 64, onesm, inv_n, pcol(gname, 0, 64), ptn, None, None, None,
                                        None, dst[0:64, h, tsl], ["%s:%d:%d" % (tag, h, t)])))
                        for blk in range(4):
                            items.append((lambda blk=blk: proj_v1(W, wn, 512, 4, hbt, hbn, blk),
                                          lambda r, blk=blk: evac_v(r, 4, t * 4 + blk)))
                    else:
                        tabs, tabn = load_rope(ropeC, 96, t)

                        def c_qlat1():
                            return proj_fm(W, wn, 0, 128, hbt, hbn), proj_fm(W, wn, 128, 128, hbt, hbn)

                        def c_qlat2(r):
                            pm_, pmn_ = psM.next()
                            for j, (pp_, ppn_) in enumerate(r):
                                sqt, sqn = sq_r.next()
                                s.op("act", I("activation", out=sqt[:, :], in_=pp_[:, :], func=AF.Square),
                                     reads=[ppn_], writes=[sqn])
                                s.op("pe", I("matmul", pm_[:, :], lhsT=ones_bf[:, :], rhs=sqt[:, :],
                                             start=(j == 0), stop=(j == 1)), reads=[sqn, "ones_bf"], writes=[pmn_])
                            rt, rn = rstd_from_ssq(pm_[:, :], pmn_, 128, 1.0 / 256)
                            for j, (pp_, ppn_) in enumerate(r):
                                s.op("dve", I("scalar_tensor_tensor", out=qa_t[:, j, :], in0=pp_[:, :],
                                              scalar=pcol("c_qa", j), in1=rt[:, :], op0=ALU.mult, op1=ALU.mult),
                                     reads=[ppn_, ptn, rn], writes=["qa_t"])

                        items.append((c_qlat1, c_qlat2))
                        items.append((lambda: proj_fm(W, wn, 256, 128, hbt, hbn),
                                      lambda r: norm_rope(r[0][:, :], r[1], 128, ones_bf[:, :], 1.0 / 128, pcol("c_kva"),
                                                          ptn, None, None, None, None, kvn_t[:, :], ["kvn_t"])))
                        items.append((lambda: proj_fm(W, wn, 384, 32, hbt, hbn),
                                      lambda r: s.op("act", I("activation", out=kcraw[64:96, :], in_=r[0][0:32, :],
                                                              func=AF.Copy), reads=[r[1]], writes=["kcraw_r"])))

                        def c_q1(h):
                            p, pn = psS.next()
                            for j in range(2):
                                s.op("pe", I("matmul", p[:, :], lhsT=wqb_t[:, j, 96 * h:96 * h + 128], rhs=qa_t[:, j, :],
                                             start=(j == 0), stop=(j == 1)), reads=["wqb_t", "qa_t"], writes=[pn])
                            return p, pn

                        def c_k1(h):
                            p, pn = psS.next()
                            s.op("pe", I("matmul", p[:, :], lhsT=wkvn_t[:, 64 * h:64 * h + 128], rhs=kvn_t[:, :], start=True, stop=True),
                                 reads=["wkvn_t", "kvn_t"], writes=[pn])
                            return p, pn

                        def c_k2(r, h):
                            s.op("act", I("activation", out=kcraw[0:64, :], in_=r[0][0:64, :], func=AF.Copy),
                                 reads=[r[1]], writes=["kcraw_n"])
                            norm_rope(kcraw[0:96, :], ["kcraw_n", "kcraw_r"], 96, bdt[:, 2, :], 1.0 / 96,
                                      pcol("c_kn", 0, 96), ptn, rgt[:, 3, :], rgn, tabs, tabn,
                                      KT[0:96, h, tsl], ["K:%d:%d" % (h, t)])

                        for h in range(4):
                            items.append((lambda h=h: c_q1(h),
                                          lambda r, h=h: norm_rope(r[0][0:96, :], r[1], 96, bdt[:, 2, :], 1.0 / 96,
                                                                   pcol("c_qn", 0, 96), ptn, rgt[:, 2, :], rgn, tabs,
                                                                   tabn, QT[0:96, h, tsl], ["Q:%d:%d" % (h, t)])))
                            items.append((lambda h=h: c_k1(h), lambda r, h=h: c_k2(r, h)))

                        def c_v1(blk):
                            p, pn = psS.next()
                            s.op("pe", I("matmul", p[:, 0:256], lhsT=kvn_t[:, blk * 128:(blk + 1) * 128],
                                         rhs=wkvv_t[:].rearrange("p h d -> p (h d)"), start=True, stop=True),
                                 reads=["kvn_t", "wkvv_t"], writes=[pn])
                            return p, pn

                        for blk in range(4):
                            items.append((lambda blk=blk: c_v1(blk), lambda r, blk=blk: evac_v(r, 4, t * 4 + blk)))
                    pq = []
                    for (st1, st2) in items:
                        pq.append((st2, st1()))
                        if len(pq) > PIPE_DEPTH:
                            f, r = pq.pop(0)
                            f(r)
                    for f, r in pq:
                        f(r)
                first_pass = False
                if mx == mixers[0] and do_ffn:
                    for r in range(8):
                        s.op("pool", I("dma_start", out=WUPB[r * 128:(r + 1) * 128, :], in_=w_up[l][r * 128:(r + 1) * 128, :]),
                             writes=["WUPB"], dmakey="castup")
                    for r in range(NCF):
                        s.op("pool", I("dma_start", out=WDNB[r * 128:(r + 1) * 128, :], in_=w_down[l][r * 128:(r + 1) * 128, :]),
                             writes=["WDNB"], dmakey="castdn")
                if do_attn:
                    s.phase = "L%d attn%s" % (l, mx)
                    attention(l, mx, pd, pdn, pt, ptn)

            s.phase = "L%d wout" % l
            wo_fm = w_out[l].rearrange("(kc p) n -> p kc n", p=128)
            for half in range(2):
                Wh = WSLOT[half]
                s.op("pool", I("dma_start", out=Wh[:, :, 0:512],
                                                                     in_=wo_fm[:, :, half * 512:(half + 1) * 512]),
                     reads=[GUARD], writes=["wslot%d" % half], dmakey="wslot%d" % half)
            for t in range(NT):
                hbt, hbn = hring.next()
                tsl = slice(t * 512, (t + 1) * 512)
                s.op("sp", I("dma_start", out=hbt[:], in_=mix_fm[:, :, tsl]),
                     reads=["MIX:%d" % t], writes=[hbn], dmakey="ld_" + hbn)
                xl = {}

                def xload(c):
                    xt_, xn_ = xring.next()
                    s.op("sp", I("dma_start", out=xt_[:, :], in_=xin_fm[:, c, tsl]),
                         reads=["%s:%d" % (xin_name, t)], writes=[xn_], dmakey=xn_)
                    xl[c] = (xt_, xn_)

                xload(0)
                xload(1)
                for c in range(8):
                    Wh = WSLOT[c // 4]
                    cc = (c % 4) * 128
                    p, pn = psS.next()
                    for kc in range(8):
                        s.op("pe", I("matmul", p[:, :], lhsT=Wh[:, kc, cc:cc + 128], rhs=hbt[:, kc, :],
                                     start=(kc == 0), stop=(kc == 7)),
                             reads=["wslot%d" % (c // 4), hbn, GUARD], writes=[pn])
                    if c + 2 < 8:
                        xload(c + 2)
                    xt_, xn_ = xl[c]
                    ot, on = xoring.next()
                    s.op("dve", I("tensor_tensor", out=ot[:, :], in0=p[:, :], in1=xt_[:, :], op=ALU.add),
                         reads=[pn, xn_], writes=[on])
                    s.op("sp", I("dma_start", out=XA[c * 128:(c + 1) * 128, tsl], in_=ot[:, :]),
                         reads=[on], writes=["XA:%d" % t], dmakey="st_" + on)

            if not do_ffn:
                return
            phase_barrier()
            s.phase = "L%d ffn" % l
            xa_fm = XA.rearrange("(kc p) t -> p kc t", p=128)
            wup_fm = WUPB.rearrange("(kc p) n -> p kc n", p=128)
            wdn_fm = WDNB.rearrange("(kc p) n -> p kc n", p=128)
            wins = []
            for w in range(9):
                tok0 = 510 * w - 1
                n_in = min(512, S + 1 - tok0)
                lo = 1 if w == 0 else 0
                hi = min(n_in, S - tok0)
                wins.append((w, tok0, n_in, lo, hi))
            wup_i = [0]
            wdn_i = [0]
            def ffn_norm(grp):
                for wi, (w, tok0, n_in, lo, hi) in enumerate(grp):
                    chunks = sorted(set([max(tok0, 0) // 512, min(tok0 + n_in - 1, S - 1) // 512]))
                    norm_chunk(xa_fm, lambda chunks=chunks: ["XA:%d" % c for c in chunks], "n2g", tok0, n_in, lo, hi,
                               lambda kc, wi=wi, n_in=n_in: H2[:, kc, wi, 0:n_in], ["H2:%d" % wi],
                               zero_pad=(lo > 0 or hi < n_in))

            def ffn_up(grp):
                for j in range(NCF):
                    wu = WUP[wup_i[0] % 3]
                    wun = "wup%d" % (wup_i[0] % 3)
                    wup_i[0] += 1
                    for half in range(2):
                        s.op("pool", I("dma_start", out=wu[:, :, half * 128:(half + 1) * 128],
                                       in_=wup_fm[:, :, half * DFF + j * 128:half * DFF + (j + 1) * 128]),
                             reads=[GUARD, "WUPB"], writes=[wun], dmakey=wun)
                    for wi, (w, tok0, n_in, lo, hi) in enumerate(grp):
                        no = n_in - 2
                        t3s = []
                        for half in range(2):
                            p, pn = psS.next()
                            for kc in range(8):
                                s.op("pe", I("matmul", p[:, 0:n_in], lhsT=wu[:, kc, half * 128:(half + 1) * 128], rhs=H2[:, kc, wi, 0:n_in],
                                    start=(kc == 0), stop=(kc == 7)),
                                    reads=[wun, "H2:%d" % wi, GUARD], writes=[pn])
                            cj = half * NCF + j
                            a1, a1n = t1_r.next()
                            a2, a2n = t2_r.next()
                            a3, a3n = (t3_r if half == 0 else tb_r).next()
                            s.op("act", I("activation", out=a1[:, 0:no], in_=p[:, 1:1 + no], func=AF.Identity,
                                bias=pt[:, PP["cb"][0] + cj:PP["cb"][0] + cj + 1],
                                scale=pt[:, PP["cw1"][0] + cj:PP["cw1"][0] + cj + 1]),
                                reads=[pn, ptn], writes=[a1n])
                            s.op("dve", I("scalar_tensor_tensor", out=a2[:, 0:no], in0=p[:, 0:no], scalar=pt[:, PP["cw0"][0] + cj:PP["cw0"][0] + cj + 1],
                                in1=a1[:, 0:no], op0=ALU.mult, op1=ALU.add), reads=[pn, a1n, ptn], writes=[a2n])
                            s.op("dve", I("scalar_tensor_tensor", out=a3[:, 0:no], in0=p[:, 2:2 + no], scalar=pt[:, PP["cw2"][0] + cj:PP["cw2"][0] + cj + 1],
                                in1=a2[:, 0:no], op0=ALU.mult, op1=ALU.add), reads=[pn, a2n, ptn], writes=[a3n])
                            t3s.append((a3, a3n))
                        (gt, gn), (vt, vn) = t3s
                        sg, sgn = ln_r.next()
                        s.op("act", I("activation", out=sg[:, 0:no], in_=gt[:, 0:no],
                                                                                func=AF.Silu),
                             reads=[gn], writes=[sgn])
                        s.op("dve", I("tensor_tensor", out=ACTT[:, j, wi, 0:no], in0=sg[:, 0:no], in1=vt[:, 0:no], op=ALU.mult),
                            reads=[sgn, vn, GUARD], writes=["ACT:%d" % wi])
            def ffn_down(grp):
                for c in range(8):
                    wd = WDN[wdn_i[0] % 2]
                    wdn = "wdn%d" % (wdn_i[0] % 2)
                    wdn_i[0] += 1
                    s.op("pool", I("dma_start", out=wd[:], in_=wdn_fm[:, :, c * 128:(c + 1) * 128]),
                         reads=[GUARD, "WDNB"], writes=[wdn], dmakey=wdn)
                    for wi, (w, tok0, n_in, lo, hi) in enumerate(grp):
                        no = n_in - 2
                        o0 = tok0 + 1
                        p, pn = psS.next()
                        for kc in range(NCF):
                            s.op("pe", I("matmul", p[:, 0:no], lhsT=wd[:, kc, :], rhs=ACTT[:, kc, wi, 0:no],
                                start=(kc == 0), stop=(kc == NCF - 1)),
                                reads=[wdn, "ACT:%d" % wi, GUARD], writes=[pn])
                        chunks = sorted(set([o0 // 512, (o0 + no - 1) // 512]))
                        xt_, xn_ = xring.next()
                        s.op("sp", I("dma_start", out=xt_[:, 0:no], in_=xa_fm[:, c, o0:o0 + no]),
                            reads=["XA:%d" % cc for cc in chunks], writes=[xn_], dmakey=xn_)
                        ot, on = xoring.next()
                        s.op("dve", I("tensor_tensor", out=ot[:, 0:no], in0=p[:, 0:no], in1=xt_[:, 0:no], op=ALU.add),
                            reads=[pn, xn_], writes=[on])
                        s.op("sp", I("dma_start", out=Xout[c * 128:(c + 1) * 128, o0:o0 + no], in_=ot[:, 0:no]),
                            reads=[on], writes=["%s:%d" % (xout_name, cc) for cc in chunks], dmakey="st_" + on)

            groups = [wins[g0:g0 + G] for g0 in range(0, 9, G)]
            ffn_norm(groups[0])
            for gi, grp in enumerate(groups):
                ffn_up(grp)
                if gi + 1 < len(groups):
                    ffn_norm(groups[gi + 1])
                ffn_down(grp)

        def attention(l, mx, pd, pdn, pt, ptn):
            base_head = {"A": 0, "B": 4, "C": 8, "D": 12}[mx]
            d = {"A": 64, "B": 32, "C": 96, "D": 64}[mx]
            scale = float(d) ** -0.5

            def kv_of(h):
                return h // 2 if mx == "A" else h

            def dist_min(k0, q0):
                if k0 + 127 < q0:
                    return q0 - (k0 + 127)
                if k0 > q0 + 511:
                    return k0 - (q0 + 511)
                return 0

            def finalize_one(po, pon):
                rd, rdn = ln_r.next()
                s.op("dve", I("reciprocal", out=rd[64:65, :], in_=po[64:65, :]), reads=[pon], writes=[rdn])
                rh, rhn = qs_r.next()
                rl, rln = qs_r.next()
                s.op("dve", I("tensor_copy", out=rh[64:65, :], in_=rd[64:65, :]), reads=[rdn], writes=[rhn])
                s.op("dve", I("tensor_tensor", out=rl[64:65, :], in0=rd[64:65, :], in1=rh[64:65, :], op=ALU.subtract),
                     reads=[rdn, rhn], writes=[rln])
                pb, pbn = psB.next()
                s.op("pe", I("matmul", pb[:, :], lhsT=bdt[:, 3, :], rhs=rh[:, :], start=True, stop=False),
                     reads=[rhn, "blk32"], writes=[pbn])
                s.op("pe", I("matmul", pb[:, :], lhsT=bdt[:, 3, :], rhs=rl[:, :], start=False, stop=True),
                     reads=[rln, "blk32"], writes=[pbn])
                rc, rcn = fin_r.next()
                s.op("dve", I("tensor_copy", out=rc[:, :], in_=pb[0:64, :]), reads=[pbn], writes=[rcn])
                return rc, rcn

            deferred = []
            pend = []
            WIDE_KEEP = 2
            SLOPE_KEEP = 2

            def do_pv(item, last_flush=False):
                po, pon, Eap, En, kb, kvh, q0, first, last, on_last = item
                while deferred:
                    deferred.pop(0)()
                if mx == "D":
                    c_lo = MS_C0 - (kb * 128 - q0)
                    s.op("dve", I("tensor_tensor", out=Eap, in0=Eap, in1=cMst[:, c_lo:c_lo + 512], op=ALU.mult),
                         reads=[En, "cMst"], writes=[En])
                voff = 32768 + (kvh * 32 + kb) * 65
                s.op("pe", I("matmul", po[:, :], lhsT=arena[:, voff:voff + 128], rhs=Eap, start=first, stop=last),
                     reads=[En, "V:%d" % kb, "VONES", GUARD], writes=[pon])
                if last and on_last is not None:
                    deferred.append(on_last)

            def flush_to(n):
                while len(pend) > n:
                    do_pv(pend.pop(0))

            for h in range(4):
                slope = (SLOPES[h] if mx == "B" else SLOPES[4 + h]) if mx in "BD" else None
                maps = [(0, 32), (32, 64)] if mx == "B" else [(0, d)]
                kvh = kv_of(h)
                for qc in range(8):
                    q0 = qc * 512
                    if mx == "D":
                        kbs = [kb for kb in range(32) if q0 - 1024 <= kb * 128 < q0 + 512 + 1024]
                    elif mx == "B":
                        kbs = [kb for kb in range(32) if slope * dist_min(kb * 128, q0) <= 60.0]
                    else:
                        kbs = list(range(32))
                    accs = []

                    def fin(accs=accs, h=h, q0=q0, qc=qc):
                        mo, mon = mo_r.next()
                        if mx != "B":
                            po, pon = accs[0]
                            rc, rcn = finalize_one(po, pon)
                            s.op("dve", I("tensor_tensor", out=mo[:, :], in0=po[0:64, :], in1=rc[:, :], op=ALU.mult),
                                 reads=[pon, rcn], writes=[mon])
                        else:
                            ns = []
                            for (po, pon) in accs:
                                rc, rcn = finalize_one(po, pon)
                                nn, nnn = fin_r.next()
                                s.op("dve", I("tensor_tensor", out=nn[:, :], in0=po[0:64, :], in1=rc[:, :], op=ALU.mult),
                                     reads=[pon, rcn], writes=[nnn])
                                ns.append((nn, nnn))
                            (n1, n1n), (n2, n2n) = ns
                            df, dfn = t1_r.next()
                            s.op("dve", I("scalar_tensor_tensor", out=df[0:64, :], in0=n2[:, :], scalar=pd[0:64, 0:1],
                                          in1=n1[:, :], op0=ALU.mult, op1=ALU.add), reads=[n1n, n2n, pdn], writes=[dfn])
                            norm_rope(df[0:64, :], dfn, 64, bdt[:, 0, :], 1.0 / 64, pd[0:64, 1:2], pdn, None,
                                      None, None, None, mo[:, :], [mon], ring=psB)
                        hh = base_head + h
                        s.op("sp", I("dma_start", out=MIX[hh * 64:(hh + 1) * 64, q0:q0 + 512], in_=mo[:, :]),
                             reads=[mon], writes=["MIX:%d" % qc], dmakey="st_" + mon)

                    for mi, (r0, r1) in enumerate(maps):
                        po, pon = psO.next()
                        accs.append((po, pon))
                        on_last = fin if mi == len(maps) - 1 else None
                        nk = len(kbs)
                        if slope is None:
                            i = 0
                            while i < nk:
                                grp = kbs[i:i + 2]
                                pp2, (pna, pnb) = psP.next()
                                names = [pna, pnb][:len(grp)]
                                for gi, kb in enumerate(grp):
                                    k0 = kb * 128
                                    s.op("pe", I("matmul", pp2[:, gi * 512:(gi + 1) * 512], lhsT=KT[:, kvh, k0:k0 + 128],
                                                 rhs=QT[:, h, q0:q0 + 512], start=True, stop=True),
                                         reads=["K:%d:%d" % (kvh, kb // 4), "Q:%d:%d" % (h, qc), GUARD], writes=[names[gi]])
                                flush_to(WIDE_KEEP)
                                Et, En = E_r.next()
                                w = 512 * len(grp)
                                s.op("act", I("activation", out=Et[:, 0:w], in_=pp2[:, 0:w], func=AF.Exp, scale=scale),
                                     reads=names, writes=[En])
                                for gi, kb in enumerate(grp):
                                    idx = i + gi
                                    pend.append((po, pon, Et[:, gi * 512:(gi + 1) * 512], En, kb, kvh, q0, idx == 0,
                                                 idx == nk - 1, on_last))
                                i += 2
                        else:
                            for i, kb in enumerate(kbs):
                                k0 = kb * 128
                                ps_, psn = psS.next()
                                rr0, rr1 = (r0, r1) if mx == "B" else (0, 128)
                                s.op("pe", I("matmul", ps_[:, :], lhsT=KT[rr0:rr1, kvh, k0:k0 + 128],
                                             rhs=QT[rr0:rr1, h, q0:q0 + 512], start=True, stop=True),
                                     reads=["K:%d:%d" % (kvh, kb // 4), "Q:%d:%d" % (h, qc), GUARD], writes=[psn])
                                flush_to(SLOPE_KEEP)
                                Et, En = E_r.next()
                                tt, ttn = tb_r.next()
                                if k0 + 127 < q0:
                                    bias_ap, mul, cb = cLt[:, :], slope / scale, -slope * (q0 - k0)
                                elif k0 > q0 + 511:
                                    bias_ap, mul, cb = cLt[:, :], -slope / scale, -slope * (k0 - q0)
                                else:
                                    j = (k0 - q0) // 128
                                    bias_ap, mul, cb = cW0t[:, 384 - 128 * j:896 - 128 * j], slope / scale, 0.0
                                s.op("dve", I("scalar_tensor_tensor", out=tt[:, :], in0=bias_ap, scalar=float(mul),
                                              in1=ps_[:, :], op0=ALU.mult, op1=ALU.add),
                                     reads=[psn, "cLt", "cW0t"], writes=[ttn])
                                s.op("act", I("activation", out=Et[:, 0:512], in_=tt[:, :], func=AF.Exp, scale=scale,
                                              bias=float(cb)), reads=[ttn], writes=[En])
                                pend.append((po, pon, Et[:, 0:512], En, kb, kvh, q0, i == 0, i == nk - 1, on_last))
            while pend:
                do_pv(pend.pop(0))
            while deferred:
                deferred.pop(0)()

        for l in range(n_layers):
            Xin, xin_name = (xT, "X0") if l == 0 else (XB, "XB")
            last = (l == n_layers - 1)
            Xout, xout_name = (outT, "OUT") if last else (XB, "XB")
            if not do_ffn:
                pass
            layer(l, Xin, xin_name, Xout, xout_name)

        finals = [k for k in s.dma_ops if k.startswith("st_")]
        s.emit(st, final_waits=finals)
        counts = {e: len(s.ops[e]) for e in ENGS}
        _CACHE["sched"] = s
    return nc, counts


def _prep_inputs(inputs):
    f32 = np.float32
    consts = _host_constants()
    pp = _pack_pp(inputs)
    shared = {
        "w_in": np.ascontiguousarray(inputs["w_in"], dtype=f32),
        "w_out": np.ascontiguousarray(inputs["w_out"], dtype=f32),
        "w_up": np.ascontiguousarray(inputs["w_up"], dtype=f32),
        "w_down": np.ascontiguousarray(inputs["w_down"], dtype=f32),
        "c_wqb": np.ascontiguousarray(inputs["c_wqb"], dtype=f32),
        "c_wkvb": np.ascontiguousarray(inputs["c_wkvb"], dtype=f32),
        "pp": pp,
    }
    shared.update(consts)
    return shared


def kernel(**inputs):
    x = np.asarray(inputs["x"], dtype=np.float32)
    shared = _prep_inputs(inputs)
    if "nc" not in _CACHE:
        _CACHE["nc"] = build_program()[0]
    nc = _CACHE["nc"]
    in_maps = []
    for c in range(8):
        m = dict(shared)
        m["xT"] = np.ascontiguousarray(x[c].T)
        in_maps.append(m)
    res = run_bass_kernel_spmd(nc, in_maps, core_ids=list(range(8)))
    out = np.stack([np.asarray(r["outT"]).T for r in res.results], axis=0)
    return np.ascontiguousarray(out.astype(np.float32))
```

```python
import contextlib
import math
import numpy as np
import concourse.bass as bass
import concourse.mybir as mybir
from concourse.bass_utils import run_bass_kernel_spmd

F32 = mybir.dt.float32
BF16 = mybir.dt.bfloat16
ALU = mybir.AluOpType
AF = mybir.ActivationFunctionType

S = 4096
DM = 1024
DEPTH = 4
NT = 8
DFF = 2816
NCF = 22
EPS = 1e-6
INW = 2464
ENGS = ("pe", "act", "dve", "pool", "sp")
PIPE_DEPTH = 1


class Buf:
    __slots__ = ("name", "last_w", "readers")

    def __init__(self, name):
        self.name = name
        self.last_w = None
        self.readers = []


class Op:
    __slots__ = ("eng", "fn", "waits", "sig", "sigval", "dmakey", "phase", "fin")

    def __init__(self, eng, fn, dmakey=None):
        self.eng = eng
        self.fn = fn
        self.waits = {}
        self.sig = False
        self.sigval = 0
        self.dmakey = dmakey


class Sched:
    def __init__(self, nc):
        self.nc = nc
        self.ops = {e: [] for e in ENGS}
        self.known = {e: {} for e in ENGS}
        self.dma_ops = {}
        self.bufs = {}
        self.seq = []
        self.phase = "init"

    def buf(self, name):
        b = self.bufs.get(name)
        if b is None:
            b = Buf(name)
            self.bufs[name] = b
        return b

    def op(self, eng, fn, reads=(), writes=(), dmakey=None):
        o = Op(eng, fn, dmakey)
        lst = self.ops[eng]
        idx = len(lst)
        if dmakey is not None:
            dl = self.dma_ops.setdefault(dmakey, [])
            tok = ("dma:" + dmakey, len(dl))
            dl.append(o)
        else:
            tok = (eng, idx)
        deps = {}

        def need(t):
            if t is None:
                return
            k, i = t
            if k == "pe" and tok[0] == "pe":
                return
            if deps.get(k, -1) < i:
                deps[k] = i

        ps_reads = [x for x in reads if x.startswith("ps")]
        if ps_reads:
            reads = [x for x in reads if not x.startswith("ps")]
            writes = list(writes) + [x for x in ps_reads if x not in writes]
        rb = [self.buf(x) for x in reads]
        wb = [self.buf(x) for x in writes]
        for b in rb:
            need(b.last_w)
        for b in wb:
            need(b.last_w)
            for r in b.readers:
                if r[0] != tok[0] or tok[0].startswith("dma:"):
                    need(r)
        kn = self.known[eng]
        for k, i in deps.items():
            if kn.get(k, -1) < i:
                kn[k] = i
                o.waits[k] = i
                if k.startswith("dma:"):
                    self.dma_ops[k[4:]][i].sig = True
                else:
                    self.ops[k][i].sig = True
        for b in rb:
            b.readers.append(tok)
        for b in wb:
            b.last_w = tok
            b.readers = []
        lst.append(o)
        o.phase = self.phase
        self.seq.append(o)
        return o

    def estimate(self, ghz=1.95, sem_lat=150.0, dma_fixed=2000.0, dma_bpns=150.0):
        free = {e: 0.0 for e in ENGS}
        busy = {}
        span = {}
        dma_free = [0.0]

        def fs(ap):
            v = ap.free_size
            return v() if callable(v) else v

        def nb(ap):
            v = ap.nbytes
            return v() if callable(v) else v

        def cost(o):
            m = getattr(o.fn, "meta", None)
            if m is None:
                return 100.0, 0
            fn, args, kw = m
            if fn == "dma_start":
                ap = kw["out"]
                return 60.0, nb(ap)
            if fn == "matmul":
                n = fs(kw["rhs"])
                c = max(n, 64) / ghz + 8
                if kw["rhs"].dtype == F32:
                    c *= 4
                return c, 0
            out = kw.get("out", args[0] if args else None)
            n = fs(out)
            if fn == "activation":
                return 186 + 0.833 * n + 93, 0
            if fn in ("memset",):
                return (n / 2 + 100) / 0.96, 0
            srcs = [kw.get(k) for k in ("in0", "in1", "in_") if kw.get(k) is not None]
            psum = any(type(a.tensor).__name__.startswith("PSum") for a in srcs if hasattr(a, "tensor"))
            allbf = all(getattr(a, "dtype", None) == BF16 for a in srcs + [out] if hasattr(a, "dtype"))
            if fn in ("tensor_tensor", "scalar_tensor_tensor"):
                rate = 2.0 if (allbf and not psum) else 1.0
            else:
                rate = 1.0 if psum else 2.0
            return (n / rate + 151) / 0.96, 0

        for o in self.seq:
            st_ = free[o.eng]
            for k, i in o.waits.items():
                dep = self.dma_ops[k[4:]][i] if k.startswith("dma:") else self.ops[k][i]
                st_ = max(st_, dep.fin + (sem_lat if k != o.eng else 60.0))
            c, nbytes = cost(o)
            if o.dmakey is not None:
                free[o.eng] = st_ + c
                t0 = max(st_ + c, dma_free[0])
                dma_free[0] = t0 + nbytes / dma_bpns
                o.fin = max(st_ + dma_fixed, dma_free[0] + 500.0)
                b = busy.setdefault(o.phase, {}); b["dma"] = b.get("dma", 0.0) + nbytes / dma_bpns
            else:
                o.fin = st_ + c
                free[o.eng] = o.fin
                b = busy.setdefault(o.phase, {}); b[o.eng] = b.get(o.eng, 0.0) + c
            sp_ = span.setdefault(o.phase, [st_, o.fin])
            sp_[0] = min(sp_[0], st_); sp_[1] = max(sp_[1], o.fin)
        total = max(o.fin for o in self.seq)
        return total, span, busy

    def emit(self, st, final_waits=()):
        nc = self.nc
        sems = {}
        for e in ENGS:
            sems[e] = st.enter_context(nc.semaphore("s_" + e))
        for k in self.dma_ops:
            sems["dma:" + k] = st.enter_context(nc.semaphore("d_" + k))
        for e in ENGS:
            c = 0
            for o in self.ops[e]:
                if o.dmakey is None and o.sig:
                    c += 1
                o.sigval = c
        for k, dl in self.dma_ops.items():
            c = 0
            for o in dl:
                c += 16
                o.sigval = c
        block = st.enter_context(nc.Block())
        engmap = {"pe": "tensor", "act": "scalar", "dve": "vector", "pool": "gpsimd", "sp": "sync"}

        def make(e):
            def body(engine):
                for o in self.ops[e]:
                    for k, i in o.waits.items():
                        if k.startswith("dma:"):
                            v = self.dma_ops[k[4:]][i].sigval
                        else:
                            v = self.ops[k][i].sigval
                        engine.wait_ge(sems[k], v)
                    ins = o.fn(engine)
                    if o.dmakey is not None:
                        ins.then_inc(sems["dma:" + o.dmakey], 16)
                    elif o.sig:
                        ins.then_inc(sems[e], 1)
                if e == "sp":
                    for k in final_waits:
                        dl = self.dma_ops[k]
                        engine.wait_ge(sems["dma:" + k], dl[-1].sigval)
            return body

        for e in ENGS:
            getattr(block, engmap[e])(make(e))


class Ring:
    def __init__(self, items):
        self.items = items
        self.i = 0

    def next(self):
        it = self.items[self.i % len(self.items)]
        self.i += 1
        return it


def I(fn, *args, **kw):
    f = lambda e: getattr(e, fn)(*args, **kw)
    f.meta = (fn, args, kw)
    return f


PP = {}
_off = 0
for _n, _w in [("n1g", 8), ("n2g", 8), ("cw0", NCF * 2), ("cw1", NCF * 2), ("cw2", NCF * 2), ("cb", NCF * 2),
               ("a_qn", 1), ("a_kn", 1), ("b_qn", 1), ("b_kn", 1), ("b_sub", 1), ("c_qa", 2), ("c_kva", 1),
               ("c_qn", 1), ("c_kn", 1), ("d_qn", 1), ("d_kn", 1), ("lq1", 1), ("lk1", 1), ("lq2", 1),
               ("lk2", 1), ("lam_init", 1), ("oml", 1)]:
    PP[_n] = (_off, _w)
    _off += _w
NPP = _off

A_Q, A_K, A_V = 0, 256, 384
B_Q, B_K, B_V = 512, 768, 1024
C_QL, C_KVL, C_KR = 1280, 1536, 1664
D_Q, D_K, D_V = 1696, 1952, 2208
MIXER_COLS = {"A": (0, 512), "B": (512, 1280), "C": (1280, 1696), "D": (1696, 2464)}

SLOPES = [2.0 ** (-(i + 1)) for i in range(8)]
MS_C0 = 1408
MS_W = 2944


def _host_constants():
    f32 = np.float32
    pos = np.arange(S, dtype=f32)
    rows = np.floor(pos / 64).astype(f32)
    cols = (pos - rows * 64).astype(f32)

    def freqs(dim):
        return (1.0 / (f32(10000.0) ** (np.arange(0, dim, 2, dtype=f32) / f32(dim)))).astype(f32)

    fa = freqs(32)
    ang_row = (rows[:, None] * fa[None, :]).astype(f32)
    ang_col = (cols[:, None] * fa[None, :]).astype(f32)
    ang_mla = (pos[:, None] * freqs(32)[None, :]).astype(f32)
    ropeA = np.zeros((2, 64, S), f32)
    for m in range(64):
        ang = ang_row if m < 32 else ang_col
        f = (m % 32) % 16
        ropeA[0, m] = np.cos(ang[:, f])
        ropeA[1, m] = np.sin(ang[:, f])
    ropeC = np.zeros((2, 96, S), f32)
    ropeC[0, :64] = 1.0
    for m in range(64, 96):
        f = (m - 64) % 16
        ropeC[0, m] = np.cos(ang_mla[:, f])
        ropeC[1, m] = np.sin(ang_mla[:, f])
    cR = np.zeros((2, 96, 96), f32)
    for base in (0, 32):
        for j in range(16):
            cR[0, base + j + 16, base + j] = -1.0
            cR[0, base + j, base + j + 16] = 1.0
    for j in range(16):
        cR[1, 64 + j + 16, 64 + j] = -1.0
        cR[1, 64 + j, 64 + j + 16] = 1.0
    ki = np.arange(128, dtype=f32)[:, None]
    cL = (ki - np.arange(512, dtype=f32)[None, :]).astype(f32)
    cW0 = (-np.abs(np.arange(896, dtype=f32)[None, :] - ki - 384.0)).astype(f32)
    delta = (np.arange(MS_W)[None, :] - np.arange(128)[:, None] - MS_C0)
    ad = np.abs(delta)
    mult = np.zeros_like(delta, dtype=f32)
    for d in (1, 4, 16):
        mult += ((delta % d == 0) & (ad <= 64 * d)).astype(f32)
    cBD = np.zeros((4, 128, 128), f32)
    for a, b in ((0, 64), (64, 128)):
        cBD[0, a:b, a:b] = 1.0
    for a in range(0, 128, 32):
        cBD[1, a:a + 32, a:a + 32] = 1.0
    cBD[2, 0:96, 0:96] = 1.0
    cBD[2, 96:128, 96:128] = 1.0
    cBD[3, 64, 0:64] = 1.0
    return {"ropeA": ropeA, "ropeC": ropeC, "cR": cR, "cL": cL, "cW0": cW0, "cMs": mult.astype(f32), "cBD": cBD}


def _pack_pp(inp):
    f32 = np.float32
    pp = np.zeros((DEPTH, 128, NPP), f32)

    def put(name, l, arr2d):
        o, w = PP[name]
        r = arr2d.shape[0]
        pp[l, :r, o:o + w] = arr2d

    for l in range(DEPTH):
        put("n1g", l, np.asarray(inp["norm1_g"][l]).reshape(8, 128).T)
        put("n2g", l, np.asarray(inp["norm2_g"][l]).reshape(8, 128).T)
        cw = np.asarray(inp["conv_w"][l])
        for t in range(3):
            put("cw%d" % t, l, cw[t].reshape(2 * NCF, 128).T)
        put("cb", l, np.asarray(inp["conv_b"][l]).reshape(2 * NCF, 128).T)
        put("a_qn", l, np.asarray(inp["a_qn_g"][l])[:, None])
        put("a_kn", l, np.asarray(inp["a_kn_g"][l])[:, None])
        put("b_qn", l, np.concatenate([inp["b_qn_g"][l], inp["b_qn_g"][l]])[:, None])
        put("b_kn", l, np.concatenate([inp["b_kn_g"][l], inp["b_kn_g"][l]])[:, None])
        put("b_sub", l, np.asarray(inp["b_sub_g"][l])[:, None])
        put("c_qa", l, np.asarray(inp["c_qa_g"][l]).reshape(2, 128).T)
        put("c_kva", l, np.asarray(inp["c_kva_g"][l])[:, None])
        put("c_qn", l, np.asarray(inp["c_qn_g"][l])[:, None])
        put("c_kn", l, np.asarray(inp["c_kn_g"][l])[:, None])
        put("d_qn", l, np.asarray(inp["d_qn_g"][l])[:, None])
        put("d_kn", l, np.asarray(inp["d_kn_g"][l])[:, None])
        put("lq1", l, np.asarray(inp["b_lam_q1"][l])[:, None])
        put("lk1", l, np.asarray(inp["b_lam_k1"][l])[:, None])
        put("lq2", l, np.asarray(inp["b_lam_q2"][l])[:, None])
        put("lk2", l, np.asarray(inp["b_lam_k2"][l])[:, None])
        lam_init = 0.8 - 0.6 * math.exp(-0.3 * l)
        put("lam_init", l, np.full((128, 1), lam_init, f32))
        put("oml", l, np.full((128, 1), 1.0 - lam_init, f32))
    return pp


_CACHE = {}


def build_program(n_layers=DEPTH, mixers="ABCD", do_ffn=True, dbg=False, do_attn=True):
    nc = bass.Bass("TRN2", target_bir_lowering=False)
    L = DEPTH

    def din(name, shape, dt=F32):
        return nc.dram_tensor(name, shape, dt, kind="ExternalInput").ap()

    xT = din("xT", [DM, S])
    w_in = din("w_in", [L, DM, INW])
    w_out = din("w_out", [L, DM, DM])
    w_up = din("w_up", [L, DM, 2 * DFF])
    w_down = din("w_down", [L, DFF, DM])
    c_wqb = din("c_wqb", [L, 256, 384])
    c_wkvb = din("c_wkvb", [L, 128, 512])
    ppd = din("pp", [L, 128, NPP])
    ropeA = din("ropeA", [2, 64, S])
    ropeC = din("ropeC", [2, 96, S])
    cR = din("cR", [2, 96, 96])
    cL = din("cL", [128, 512])
    cW0 = din("cW0", [128, 896])
    cMs = din("cMs", [128, MS_W])
    cBD = din("cBD", [4, 128, 128])
    outT = nc.dram_tensor("outT", [DM, S], F32, kind="ExternalOutput").ap()
    skind = "ExternalOutput" if dbg else "Internal"
    XA = nc.dram_tensor("XA", [DM, S], F32, kind=skind).ap()
    XB = nc.dram_tensor("XB", [DM, S], F32, kind=skind).ap()
    HT = nc.dram_tensor("HT", [DM, S], BF16, kind=skind).ap()
    MIX = nc.dram_tensor("MIX", [DM, S], BF16, kind=skind).ap()
    WUPB = nc.dram_tensor("WUPB", [DM, 2 * DFF], BF16).ap()
    WDNB = nc.dram_tensor("WDNB", [DFF, DM], BF16).ap()

    st = contextlib.ExitStack()
    with st:
        def SB(name, shape, dt):
            return st.enter_context(nc.sbuf_tensor(name, shape, dt))

        s = Sched(nc)
        GUARD = "GUARD"

        G = 2
        ARENA_N = 32768 + 4 * 32 * 65 + 64
        arena = SB("arena", [128, ARENA_N], BF16)
        QT = arena[:, 0:16384].rearrange("p (h t) -> p h t", h=4)
        KT = arena[:, 16384:32768].rearrange("p (h t) -> p h t", h=4)
        VV = arena[:, 32768:32768 + 4 * 32 * 65].rearrange("p (h b d) -> p h b d", h=4, b=32)
        H2 = arena[:, 0:8 * G * 512].rearrange("p (k w t) -> p k w t", k=8, w=G)
        ACTT = arena[:, 8 * G * 512:(8 + NCF) * G * 512].rearrange("p (k w t) -> p k w t", k=NCF, w=G)
        warena = SB("warena", [128, 12288], BF16)
        WSLOT = [warena[:, i * 6144:(i + 1) * 6144].rearrange("p (k n) -> p k n", k=8) for i in range(2)]
        WUP = [warena[:, i * 2048:(i + 1) * 2048].rearrange("p (k n) -> p k n", k=8) for i in range(3)]
        WDN = [warena[:, 6144 + i * 2816:6144 + (i + 1) * 2816].rearrange("p (k n) -> p k n", k=NCF) for i in range(2)]
        hb = [SB("hbuf%d" % i, [128, 8, 512], BF16) for i in range(2)]
        hring = Ring([(hb[i], "hbuf%d" % i) for i in range(2)])
        xp = [SB("xp%d" % i, [128, 512], F32) for i in range(3)]
        xring = Ring([(xp[i], "xp%d" % i) for i in range(3)])
        xo = [SB("xo%d" % i, [128, 512], F32) for i in range(2)]
        xoring = Ring([(xo[i], "xo%d" % i) for i in range(2)])

        def mkring(prefix, n, shape, dt):
            ts = [SB("%s%d" % (prefix, i), shape, dt) for i in range(n)]
            return Ring([(ts[i], "%s%d" % (prefix, i)) for i in range(n)])

        sq_r = mkring("sqb", 2, [128, 512], BF16)
        qs_r = mkring("qsb", 2, [128, 512], BF16)
        ln_r = mkring("lnv", 2, [128, 512], F32)
        rs_r = mkring("rstd", 2, [128, 512], F32)
        t1_r = mkring("t1_", 2, [128, 512], F32)
        t2_r = mkring("t2_", 2, [128, 512], F32)
        t3_r = mkring("t3_", 2, [128, 512], F32)
        E_r = mkring("E", 3, [128, 1024], BF16)
        tb_r = mkring("tb", 3, [128, 512], F32)
        rp_r = mkring("rope", 2, [96, 2, 512], F32)
        mo_r = mkring("mo", 2, [64, 512], BF16)
        fin_r = mkring("fin", 3, [64, 512], F32)
        qa_t = SB("qa_t", [128, 2, 512], BF16)
        kvn_t = SB("kvn_t", [128, 512], BF16)
        kcraw = SB("kcraw", [96, 512], F32)
        ppt = [SB("ppt%d" % i, [128, NPP], F32) for i in range(2)]
        pdt = [SB("pdt%d" % i, [128, 8], F32) for i in range(2)]
        ones_bf = SB("ones_bf", [128, 128], BF16)
        bdt = SB("bdt", [128, 4, 128], BF16)
        ones_f = SB("ones_f", [128, 128], F32)
        R0 = SB("R0", [96, 2, 96], F32)
        RG = [SB("RG%d" % i, [128, 4, 128], BF16) for i in range(2)]
        cLt = SB("cLt", [128, 512], F32)
        cW0t = SB("cW0t", [128, 896], F32)
        cMst = SB("cMst", [128, MS_W], BF16)
        wqb_t = SB("wqb_t", [128, 2, 416], BF16)
        wkvn_t = SB("wkvn_t", [128, 320], BF16)
        wkvv_t = SB("wkvv_t", [128, 4, 64], BF16)
        dummy = SB("dummy_t", [128, 8], F32)

        PS2 = [st.enter_context(nc.psum_tensor("psd%d" % i, [128, 1024], F32)) for i in range(4)]
        PS = [PS2[i // 2][:, (i % 2) * 512:(i % 2 + 1) * 512] for i in range(8)]
        psS = Ring([(PS[i], "ps%d" % i) for i in (0, 1, 2, 3)])
        psP = Ring([(PS2[i], ("ps%d" % (2 * i), "ps%d" % (2 * i + 1))) for i in (0, 1)])
        psO = Ring([(PS[i], "ps%d" % i) for i in (4, 5)])
        psB = Ring([(PS[i], "ps%d" % i) for i in (6, 7)])
        psF = Ring([(PS[i], "ps%d" % i) for i in range(8)])
        psM = Ring([(PS[i], "ps%d" % i) for i in (4, 6, 5, 7)])

        s.op("dve", I("memset", ones_bf[:], 1.0), writes=["ones_bf"])
        s.op("dve", I("memset", ones_f[:], 1.0), writes=["ones_f"])
        s.op("pool", I("dma_start", out=bdt[:], in_=cBD.rearrange("a k m -> k a m")), writes=["blk32"], dmakey="c_BD")
        for i in range(2):
            s.op("dve", I("memset", RG[i][:], 0.0), writes=["RG%d" % i])
        for (rt_, rn_) in sq_r.items + qs_r.items:
            s.op("dve", I("memset", rt_[:], 0.0), writes=[rn_])
        s.op("dve", I("memset", arena[:, 32768 + 4 * 32 * 65:ARENA_N], 0.0), writes=["VTAIL"])
        s.op("dve", I("memset", wqb_t[:], 0.0), writes=["wqb_t"])
        s.op("dve", I("memset", wkvn_t[:], 0.0), writes=["wkvn_t"])
        s.op("dve", I("memset", dummy[:], 0.0), writes=["dummy"])
        s.op("sp", I("dma_start", out=R0[:], in_=cR.rearrange("a k m -> k a m")), writes=["R0"], dmakey="c_R0")
        s.op("sp", I("dma_start", out=cLt[:], in_=cL[:, :]), writes=["cLt"], dmakey="c_L")
        s.op("sp", I("dma_start", out=cW0t[:], in_=cW0[:, :]), writes=["cW0t"], dmakey="c_W0")
        s.op("pool", I("dma_start", out=cMst[:], in_=cMs[:, :]), writes=["cMst"], dmakey="c_Ms")

        def phase_barrier():
            s.op("pool", I("memset", dummy[:, 0:1], 0.0), writes=[GUARD, "dummy"])

        def rstd_from_ssq(ssq_ap, ssq_name, d, inv_n, n=512):
            lt, ln_ = ln_r.next()
            rt, rn = rs_r.next()
            s.op("act", I("activation", out=lt[0:d, 0:n], in_=ssq_ap, func=AF.Ln, bias=EPS, scale=inv_n),
                 reads=[ssq_name], writes=[ln_])
            s.op("act", I("activation", out=rt[0:d, 0:n], in_=lt[0:d, 0:n], func=AF.Exp, scale=-0.5),
                 reads=[ln_], writes=[rn])
            return rt, rn

        def norm_rope(src, src_name, d, onesm, inv_n, g_ap, g_name, rg, rg_name, tabs, tabs_name, dest, dest_names,
                      ncol=512, ring=None):
            n = ncol
            src_names = list(src_name) if isinstance(src_name, (list, tuple)) else [src_name]
            ring = ring or psM
            sqt, sqn = sq_r.next()
            s.op("act", I("activation", out=sqt[0:d, 0:n], in_=src, func=AF.Square),
                 reads=src_names, writes=[sqn])
            if rg is not None:
                qt, qn = qs_r.next()
                s.op("act", I("activation", out=qt[0:d, 0:n], in_=src, func=AF.Copy),
                     reads=src_names, writes=[qn])
            pm, pmn = ring.next()
            s.op("pe", I("matmul", pm[:, 0:n], lhsT=onesm, rhs=sqt[:, 0:n], start=True, stop=True),
                 reads=[sqn, "ones_bf", "blk32"], writes=[pmn])
            lt, ln_ = ln_r.next()
            rt, rn = rs_r.next()
            s.op("act", I("activation", out=lt[0:d, 0:n], in_=pm[0:d, 0:n], func=AF.Ln, bias=EPS, scale=inv_n),
                 reads=[pmn], writes=[ln_])
            s.op("act", I("activation", out=rt[0:d, 0:n], in_=lt[0:d, 0:n], func=AF.Exp, scale=-0.5),
                 reads=[ln_], writes=[rn])
            if rg is None:
                s.op("dve", I("scalar_tensor_tensor", out=dest, in0=src, scalar=g_ap, in1=rt[0:d, 0:n],
                                                             op0=ALU.mult, op1=ALU.mult),
                     reads=src_names + [g_name, rn, GUARD], writes=dest_names)
                return
            pr, prn = ring.next()
            s.op("pe", I("matmul", pr[:, 0:n], lhsT=rg, rhs=qt[:, 0:n], start=True, stop=True),
                 reads=[qn, rg_name], writes=[prn])
            a1, a1n = t1_r.next()
            a2, a2n = t2_r.next()
            a3, a3n = t3_r.next()
            cos_ap, sin_ap = tabs
            s.op("dve", I("scalar_tensor_tensor", out=a1[0:d, 0:n], in0=src, scalar=g_ap, in1=cos_ap,
                                                         op0=ALU.mult, op1=ALU.mult),
                 reads=src_names + [g_name, tabs_name], writes=[a1n])
            s.op("dve", I("tensor_tensor", out=a2[0:d, 0:n], in0=pr[0:d, 0:n], in1=sin_ap, op=ALU.mult),
                 reads=[prn, tabs_name], writes=[a2n])
            s.op("dve", I("tensor_tensor", out=a3[0:d, 0:n], in0=a1[0:d, 0:n], in1=a2[0:d, 0:n], op=ALU.add),
                 reads=[a1n, a2n], writes=[a3n])
            s.op("dve", I("tensor_tensor", out=dest, in0=a3[0:d, 0:n], in1=rt[0:d, 0:n], op=ALU.mult),
                 reads=[a3n, rn, GUARD], writes=dest_names)

        def proj_fm(W, wname, c0, M, hbt, hbn):
            p, pn = psS.next()
            for kc in range(8):
                s.op("pe", I("matmul", p[:, :], lhsT=W[:, kc, c0:c0 + 128], rhs=hbt[:, kc, :],
                             start=(kc == 0), stop=(kc == 7)),
                     reads=[wname, hbn, GUARD], writes=[pn])
            return p, pn

        def proj_v(W, wname, c0, nh, hbt, hbn, t):
            ncols = nh * 64
            for blk in range(4):
                p, pn = psS.next()
                for kc in range(8):
                    s.op("pe", I("matmul",
                        p[:, 0:ncols], lhsT=hbt[:, kc, blk * 128:(blk + 1) * 128], rhs=W[:, kc, c0:c0 + ncols],
                        start=(kc == 0), stop=(kc == 7)),
                        reads=[wname, hbn, GUARD], writes=[pn])
                b = t * 4 + blk
                s.op("act", I("activation",
                    out=VV[:, 0:nh, b, 0:64], in_=p[:, 0:ncols].rearrange("p (h d) -> p h d", h=nh), func=AF.Copy),
                    reads=[pn, GUARD], writes=["V:%d" % b])

        def proj_v1(W, wname, c0, nh, hbt, hbn, blk):
            ncols = nh * 64
            p, pn = psS.next()
            for kc in range(8):
                s.op("pe", I("matmul", p[:, 0:ncols], lhsT=hbt[:, kc, blk * 128:(blk + 1) * 128],
                             rhs=W[:, kc, c0:c0 + ncols], start=(kc == 0), stop=(kc == 7)),
                     reads=[wname, hbn, GUARD], writes=[pn])
            return p, pn

        def evac_v(r, nh, b):
            p, pn = r
            s.op("act", I("activation", out=VV[:, 0:nh, b, 0:64],
                          in_=p[:, 0:nh * 64].rearrange("p (h d) -> p h d", h=nh), func=AF.Copy),
                 reads=[pn, GUARD], writes=["V:%d" % b])

        def load_rope(tab_dram, d, t):
            rt_, rn_ = rp_r.next()
            s.op("sp", I("dma_start", out=rt_[0:d, :, :],
                                             in_=tab_dram[:, :, t * 512:(t + 1) * 512].rearrange("a d t -> d a t")),
                 writes=[rn_], dmakey=rn_)
            return (rt_[0:d, 0, :], rt_[0:d, 1, :]), rn_

        def layer(l, Xin, xin_name, Xout, xout_name):
            pt = ppt[l % 2]
            ptn = "ppt%d" % (l % 2)
            pd = pdt[l % 2]
            pdn = "pdt%d" % (l % 2)
            rgt = RG[l % 2]
            rgn = "RG%d" % (l % 2)

            def pcol(name, j=0, rows=128):
                o, w = PP[name]
                return pt[0:rows, o + j:o + j + 1]

            s.op("sp", I("dma_start", out=pt[:], in_=ppd[l]), writes=[ptn], dmakey=ptn)
            s.op("dve", I("tensor_tensor", out=pd[0:32, 2:3], in0=pcol("lq1", 0, 32), in1=pcol("lk1", 0, 32),
                                                  op=ALU.mult), reads=[ptn], writes=[pdn + "a"])
            s.op("dve", I("tensor_tensor", out=pd[0:32, 3:4], in0=pcol("lq2", 0, 32), in1=pcol("lk2", 0, 32),
                                                  op=ALU.mult), reads=[ptn], writes=[pdn + "a"])
            pm, pmn = psM.next()
            s.op("pe", I("matmul", pm[:, 0:2], lhsT=ones_f[0:32, :], rhs=pd[0:32, 2:4], start=True, stop=True),
                 reads=[pdn + "a", "ones_f"], writes=[pmn])
            s.op("act", I("activation", out=pd[:, 4:6], in_=pm[:, 0:2], func=AF.Exp),
                 reads=[pmn], writes=[pdn + "b"])
            s.op("dve", I("tensor_tensor", out=pd[:, 6:7], in0=pd[:, 5:6], in1=pd[:, 4:5], op=ALU.subtract),
                 reads=[pdn + "b"], writes=[pdn + "c"])
            s.op("dve", I("tensor_tensor", out=pd[:, 0:1], in0=pd[:, 6:7], in1=pcol("lam_init"), op=ALU.subtract),
                 reads=[pdn + "c", ptn], writes=[pdn])
            s.op("dve", I("tensor_tensor", out=pd[:, 1:2], in0=pcol("b_sub"), in1=pcol("oml"), op=ALU.mult),
                 reads=[ptn], writes=[pdn])
            for j, (gname, ri, d) in enumerate([("a_qn", 0, 64), ("a_kn", 0, 64), ("c_qn", 1, 96), ("c_kn", 1, 96)]):
                s.op("dve", I("tensor_scalar",
                    out=rgt[0:d, j, 0:d], in0=R0[0:d, ri, 0:d], scalar1=pcol(gname, 0, d), scalar2=None,
                    op0=ALU.mult), reads=[ptn, "R0"], writes=[rgn])

            xin_fm = Xin.rearrange("(kc p) t -> p kc t", p=128)
            ht_fm = HT.rearrange("(kc p) t -> p kc t", p=128)
            mix_fm = MIX.rearrange("(kc p) t -> p kc t", p=128)
            win_fm = w_in[l].rearrange("(kc p) n -> p kc n", p=128)

            def norm_chunk(Xsrc_fm, xname_fn, gname, tok0, ncol, col_lo, col_hi, dest_fn, dest_names, zero_pad=False):
                pm_, pmn_ = psM.next()
                for kc in range(8):
                    xt_, xn_ = xring.next()
                    if zero_pad:
                        s.op("dve", I("memset", xt_[:, 0:ncol], 0.0), writes=[xn_])
                    s.op("sp", I("dma_start", out=xt_[:, col_lo:col_hi], in_=Xsrc_fm[:, kc, tok0 + col_lo:tok0 + col_hi]),
                        reads=xname_fn(), writes=[xn_], dmakey=xn_)
                    sqt, sqn = sq_r.next()
                    s.op("act", I("activation", out=sqt[:, 0:ncol], in_=xt_[:, 0:ncol],
                                                                          func=AF.Square),
                         reads=[xn_], writes=[sqn])
                    s.op("pe", I("matmul", pm_[:, 0:ncol], lhsT=ones_bf[:, :], rhs=sqt[:, 0:ncol],
                                                                  start=(kc == 0), stop=(kc == 7)),
                         reads=[sqn, "ones_bf"], writes=[pmn_])
                rt, rn = rstd_from_ssq(pm_[:, 0:ncol], pmn_, 128, 1.0 / DM, ncol)
                for kc in range(8):
                    xt_, xn_ = xring.next()
                    if zero_pad:
                        s.op("dve", I("memset", xt_[:, 0:ncol], 0.0), writes=[xn_])
                    s.op("sp", I("dma_start", out=xt_[:, col_lo:col_hi], in_=Xsrc_fm[:, kc, tok0 + col_lo:tok0 + col_hi]),
                        reads=xname_fn(), writes=[xn_], dmakey=xn_)
                    s.op("dve", I("scalar_tensor_tensor", out=dest_fn(kc), in0=xt_[:, 0:ncol], scalar=pcol(gname, kc), in1=rt[:, 0:ncol],
                        op0=ALU.mult, op1=ALU.mult),
                        reads=[xn_, ptn, rn, GUARD], writes=dest_names)

            def load_w(slot, c0, c1):
                W = WSLOT[slot]
                wn = "wslot%d" % slot
                s.op("pool", I("dma_start", out=W[:, :, 0:c1 - c0], in_=win_fm[:, :, c0:c1]),
                     reads=[GUARD], writes=[wn], dmakey=wn)
                return W, wn

            wslot_i = [0]

            first_pass = True
            phase_barrier()
            s.op("dve", I("memset", VV[:, :, :, 64:65], 1.0), reads=[GUARD], writes=["VONES"])
            allqk = ["%s:%d:%d" % (a, h, t) for a in "QK" for h in range(4) for t in range(NT)]

            def zero_pad_rows():
                s.op("dve", I("memset", QT[64:128, :, :], 0.0), reads=[GUARD], writes=allqk)
                s.op("dve", I("memset", KT[64:128, :, :], 0.0), reads=[GUARD], writes=allqk)

            zero_pad_rows()
            for mx in mixers:
                s.phase = "L%d proj%s" % (l, mx)
                if mx == "D" and "C" in mixers:
                    zero_pad_rows()
                c0, c1 = MIXER_COLS[mx]
                W, wn = load_w(wslot_i[0] % 2, c0, c1)
                wslot_i[0] += 1
                if mx == "C":
                    s.op("pool", I("dma_start", out=wqb_t[:, :, 0:384], in_=c_wqb[l].rearrange("(kc p) n -> p kc n", p=128)),
                         writes=["wqb_t"], dmakey="wqb_t")
                    kvv = c_wkvb[l].rearrange("p (h two d) -> p h two d", h=4, two=2)
                    s.op("pool", I("dma_start", out=wkvn_t[:, 0:256].rearrange("p (h d) -> p h d", h=4), in_=kvv[:, :, 0, :]), writes=["wkvn_t"],
                         dmakey="wkvn_t")
                    s.op("pool", I("dma_start", out=wkvv_t[:], in_=kvv[:, :, 1, :]), writes=["wkvv_t"],
                         dmakey="wkvv_t")
                for t in range(NT):
                    hbt, hbn = hring.next()
                    tsl = slice(t * 512, (t + 1) * 512)
                    if first_pass:
                        norm_chunk(xin_fm, lambda t=t: ["%s:%d" % (xin_name, t)], "n1g", t * 512, 512, 0, 512,
                                   lambda kc, hbt=hbt: hbt[:, kc, :], [hbn])
                        s.op("sp", I("dma_start", out=ht_fm[:, :, tsl], in_=hbt[:]),
                             reads=[hbn], writes=["HT:%d" % t], dmakey="st_" + hbn)
                    else:
                        s.op("sp", I("dma_start", out=hbt[:], in_=ht_fm[:, :, tsl]),
                             reads=["HT:%d" % t], writes=[hbn], dmakey="ld_" + hbn)
                    items = []
                    if mx == "A":
                        tabs, tabn = load_rope(ropeA, 64, t)
                        for h in range(4):
                            items.append((
                                lambda h=h: proj_fm(W, wn, A_Q + 64 * h, 64, hbt, hbn),
                                lambda r, h=h: norm_rope(r[0][0:64, :], r[1], 64, bdt[:, 0, :], 1.0 / 64,
                                                         pcol("a_qn", 0, 64), ptn, rgt[:, 0, :], rgn, tabs, tabn,
                                                         QT[0:64, h, tsl], ["Q:%d:%d" % (h, t)])))
                        for h in range(2):
                            items.append((
                                lambda h=h: proj_fm(W, wn, A_K + 64 * h, 64, hbt, hbn),
                                lambda r, h=h: norm_rope(r[0][0:64, :], r[1], 64, bdt[:, 0, :], 1.0 / 64,
                                                         pcol("a_kn", 0, 64), ptn, rgt[:, 1, :], rgn, tabs, tabn,
                                                         KT[0:64, h, tsl], ["K:%d:%d" % (h, t)])))
                        for blk in range(4):
                            items.append((lambda blk=blk: proj_v1(W, wn, A_V, 2, hbt, hbn, blk),
                                          lambda r, blk=blk: evac_v(r, 2, t * 4 + blk)))
                    elif mx in "BD":
                        qg, kg = ("b_qn", "b_kn") if mx == "B" else ("d_qn", "d_kn")
                        onesm = bdt[:, 1, :] if mx == "B" else bdt[:, 0, :]
                        inv_n = 1.0 / 32 if mx == "B" else 1.0 / 64
                        for h in range(4):
                            for (gname, coff, dst, tag) in ((qg, 0, QT, "Q"), (kg, 256, KT, "K")):
                                items.append((
                                    lambda h=h, coff=coff: proj_fm(W, wn, coff + 64 * h, 64, hbt, hbn),
                                    lambda r, h=h, gname=gname, dst=dst, tag=tag: norm_rope(
                                        r[0][0:64, :], r[1], 64, onesm, inv_n, pcol(gname, 0, 64), ptn, None, None, None,
                                        None, dst[0:64, h, tsl], ["%s:%d:%d" % (tag, h, t)])))
                        for blk in range(4):
                            items.append((lambda blk=blk: proj_v1(W, wn, 512, 4, hbt, hbn, blk),
                                          lambda r, blk=blk: evac_v(r, 4, t * 4 + blk)))
                    else:
                        tabs, tabn = load_rope(ropeC, 96, t)

                        def c_qlat1():
                            return proj_fm(W, wn, 0, 128, hbt, hbn), proj_fm(W, wn, 128, 128, hbt, hbn)

                        def c_qlat2(r):
                            pm_, pmn_ = psM.next()
                            for j, (pp_, ppn_) in enumerate(r):
                                sqt, sqn = sq_r.next()
                                s.op("act", I("activation", out=sqt[:, :], in_=pp_[:, :], func=AF.Square),
                                     reads=[ppn_], writes=[sqn])
                                s.op("pe", I("matmul", pm_[:, :], lhsT=ones_bf[:, :], rhs=sqt[:, :],
                                             start=(j == 0), stop=(j == 1)), reads=[sqn, "ones_bf"], writes=[pmn_])
                            rt, rn = rstd_from_ssq(pm_[:, :], pmn_, 128, 1.0 / 256)
                            for j, (pp_, ppn_) in enumerate(r):
                                s.op("dve", I("scalar_tensor_tensor", out=qa_t[:, j, :], in0=pp_[:, :],
                                              scalar=pcol("c_qa", j), in1=rt[:, :], op0=ALU.mult, op1=ALU.mult),
                                     reads=[ppn_, ptn, rn], writes=["qa_t"])

                        items.append((c_qlat1, c_qlat2))
                        items.append((lambda: proj_fm(W, wn, 256, 128, hbt, hbn),
                                      lambda r: norm_rope(r[0][:, :], r[1], 128, ones_bf[:, :], 1.0 / 128, pcol("c_kva"),
                                                          ptn, None, None, None, None, kvn_t[:, :], ["kvn_t"])))
                        items.append((lambda: proj_fm(W, wn, 384, 32, hbt, hbn),
                                      lambda r: s.op("act", I("activation", out=kcraw[64:96, :], in_=r[0][0:32, :],
                                                              func=AF.Copy), reads=[r[1]], writes=["kcraw_r"])))

                        def c_q1(h):
                            p, pn = psS.next()
                            for j in range(2):
                                s.op("pe", I("matmul", p[:, :], lhsT=wqb_t[:, j, 96 * h:96 * h + 128], rhs=qa_t[:, j, :],
                                             start=(j == 0), stop=(j == 1)), reads=["wqb_t", "qa_t"], writes=[pn])
                            return p, pn

                        def c_k1(h):
                            p, pn = psS.next()
                            s.op("pe", I("matmul", p[:, :], lhsT=wkvn_t[:, 64 * h:64 * h + 128], rhs=kvn_t[:, :], start=True, stop=True),
                                 reads=["wkvn_t", "kvn_t"], writes=[pn])
                            return p, pn

                        def c_k2(r, h):
                            s.op("act", I("activation", out=kcraw[0:64, :], in_=r[0][0:64, :], func=AF.Copy),
                                 reads=[r[1]], writes=["kcraw_n"])
                            norm_rope(kcraw[0:96, :], ["kcraw_n", "kcraw_r"], 96, bdt[:, 2, :], 1.0 / 96,
                                      pcol("c_kn", 0, 96), ptn, rgt[:, 3, :], rgn, tabs, tabn,
                                      KT[0:96, h, tsl], ["K:%d:%d" % (h, t)])

                        for h in range(4):
                            items.append((lambda h=h: c_q1(h),
                                          lambda r, h=h: norm_rope(r[0][0:96, :], r[1], 96, bdt[:, 2, :], 1.0 / 96,
                                                                   pcol("c_qn", 0, 96), ptn, rgt[:, 2, :], rgn, tabs,
                                                                   tabn, QT[0:96, h, tsl], ["Q:%d:%d" % (h, t)])))
                            items.append((lambda h=h: c_k1(h), lambda r, h=h: c_k2(r, h)))

                        def c_v1(blk):
                            p, pn = psS.next()
                            s.op("pe", I("matmul", p[:, 0:256], lhsT=kvn_t[:, blk * 128:(blk + 1) * 128],
                                         rhs=wkvv_t[:].rearrange("p h d -> p (h d)"), start=True, stop=True),
                                 reads=["kvn_t", "wkvv_t"], writes=[pn])
                            return p, pn

                        for blk in range(4):
                            items.append((lambda blk=blk: c_v1(blk), lambda r, blk=blk: evac_v(r, 4, t * 4 + blk)))
                    pq = []
                    for (st1, st2) in items:
                        pq.append((st2, st1()))
                        if len(pq) > PIPE_DEPTH:
                            f, r = pq.pop(0)
                            f(r)
                    for f, r in pq:
                        f(r)
                first_pass = False
                if mx == mixers[0] and do_ffn:
                    for r in range(8):
                        s.op("pool", I("dma_start", out=WUPB[r * 128:(r + 1) * 128, :], in_=w_up[l][r * 128:(r + 1) * 128, :]),
                             writes=["WUPB"], dmakey="castup")
                    for r in range(NCF):
                        s.op("pool", I("dma_start", out=WDNB[r * 128:(r + 1) * 128, :], in_=w_down[l][r * 128:(r + 1) * 128, :]),
                             writes=["WDNB"], dmakey="castdn")
                if do_attn:
                    s.phase = "L%d attn%s" % (l, mx)
                    attention(l, mx, pd, pdn, pt, ptn)

            s.phase = "L%d wout" % l
            wo_fm = w_out[l].rearrange("(kc p) n -> p kc n", p=128)
            for half in range(2):
                Wh = WSLOT[half]
                s.op("pool", I("dma_start", out=Wh[:, :, 0:512],
                                                                     in_=wo_fm[:, :, half * 512:(half + 1) * 512]),
                     reads=[GUARD], writes=["wslot%d" % half], dmakey="wslot%d" % half)
            for t in range(NT):
                hbt, hbn = hring.next()
                tsl = slice(t * 512, (t + 1) * 512)
                s.op("sp", I("dma_start", out=hbt[:], in_=mix_fm[:, :, tsl]),
                     reads=["MIX:%d" % t], writes=[hbn], dmakey="ld_" + hbn)
                xl = {}

                def xload(c):
                    xt_, xn_ = xring.next()
                    s.op("sp", I("dma_start", out=xt_[:, :], in_=xin_fm[:, c, tsl]),
                         reads=["%s:%d" % (xin_name, t)], writes=[xn_], dmakey=xn_)
                    xl[c] = (xt_, xn_)

                xload(0)
                xload(1)
                for c in range(8):
                    Wh = WSLOT[c // 4]
                    cc = (c % 4) * 128
                    p, pn = psS.next()
                    for kc in range(8):
                        s.op("pe", I("matmul", p[:, :], lhsT=Wh[:, kc, cc:cc + 128], rhs=hbt[:, kc, :],
                                     start=(kc == 0), stop=(kc == 7)),
                             reads=["wslot%d" % (c // 4), hbn, GUARD], writes=[pn])
                    if c + 2 < 8:
                        xload(c + 2)
                    xt_, xn_ = xl[c]
                    ot, on = xoring.next()
                    s.op("dve", I("tensor_tensor", out=ot[:, :], in0=p[:, :], in1=xt_[:, :], op=ALU.add),
                         reads=[pn, xn_], writes=[on])
                    s.op("sp", I("dma_start", out=XA[c * 128:(c + 1) * 128, tsl], in_=ot[:, :]),
                         reads=[on], writes=["XA:%d" % t], dmakey="st_" + on)

            if not do_ffn:
                return
            phase_barrier()
            s.phase = "L%d ffn" % l
            xa_fm = XA.rearrange("(kc p) t -> p kc t", p=128)
            wup_fm = WUPB.rearrange("(kc p) n -> p kc n", p=128)
            wdn_fm = WDNB.rearrange("(kc p) n -> p kc n", p=128)
            wins = []
            for w in range(9):
                tok0 = 510 * w - 1
                n_in = min(512, S + 1 - tok0)
                lo = 1 if w == 0 else 0
                hi = min(n_in, S - tok0)
                wins.append((w, tok0, n_in, lo, hi))
            wup_i = [0]
            wdn_i = [0]
            def ffn_norm(grp):
                for wi, (w, tok0, n_in, lo, hi) in enumerate(grp):
                    chunks = sorted(set([max(tok0, 0) // 512, min(tok0 + n_in - 1, S - 1) // 512]))
                    norm_chunk(xa_fm, lambda chunks=chunks: ["XA:%d" % c for c in chunks], "n2g", tok0, n_in, lo, hi,
                               lambda kc, wi=wi, n_in=n_in: H2[:, kc, wi, 0:n_in], ["H2:%d" % wi],
                               zero_pad=(lo > 0 or hi < n_in))

            def ffn_up(grp):
                for j in range(NCF):
                    wu = WUP[wup_i[0] % 3]
                    wun = "wup%d" % (wup_i[0] % 3)
                    wup_i[0] += 1
                    for half in range(2):
                        s.op("pool", I("dma_start", out=wu[:, :, half * 128:(half + 1) * 128],
                                       in_=wup_fm[:, :, half * DFF + j * 128:half * DFF + (j + 1) * 128]),
                             reads=[GUARD, "WUPB"], writes=[wun], dmakey=wun)
                    for wi, (w, tok0, n_in, lo, hi) in enumerate(grp):
                        no = n_in - 2
                        t3s = []
                        for half in range(2):
                            p, pn = psS.next()
                            for kc in range(8):
                                s.op("pe", I("matmul", p[:, 0:n_in], lhsT=wu[:, kc, half * 128:(half + 1) * 128], rhs=H2[:, kc, wi, 0:n_in],
                                    start=(kc == 0), stop=(kc == 7)),
                                    reads=[wun, "H2:%d" % wi, GUARD], writes=[pn])
                            cj = half * NCF + j
                            a1, a1n = t1_r.next()
                            a2, a2n = t2_r.next()
                            a3, a3n = (t3_r if half == 0 else tb_r).next()
                            s.op("act", I("activation", out=a1[:, 0:no], in_=p[:, 1:1 + no], func=AF.Identity,
                                bias=pt[:, PP["cb"][0] + cj:PP["cb"][0] + cj + 1],
                                scale=pt[:, PP["cw1"][0] + cj:PP["cw1"][0] + cj + 1]),
                                reads=[pn, ptn], writes=[a1n])
                            s.op("dve", I("scalar_tensor_tensor", out=a2[:, 0:no], in0=p[:, 0:no], scalar=pt[:, PP["cw0"][0] + cj:PP["cw0"][0] + cj + 1],
                                in1=a1[:, 0:no], op0=ALU.mult, op1=ALU.add), reads=[pn, a1n, ptn], writes=[a2n])
                            s.op("dve", I("scalar_tensor_tensor", out=a3[:, 0:no], in0=p[:, 2:2 + no], scalar=pt[:, PP["cw2"][0] + cj:PP["cw2"][0] + cj + 1],
                                in1=a2[:, 0:no], op0=ALU.mult, op1=ALU.add), reads=[pn, a2n, ptn], writes=[a3n])
                            t3s.append((a3, a3n))
                        (gt, gn), (vt, vn) = t3s
                        sg, sgn = ln_r.next()
                        s.op("act", I("activation", out=sg[:, 0:no], in_=gt[:, 0:no],
                                                                                func=AF.Silu),
                             reads=[gn], writes=[sgn])
                        s.op("dve", I("tensor_tensor", out=ACTT[:, j, wi, 0:no], in0=sg[:, 0:no], in1=vt[:, 0:no], op=ALU.mult),
                            reads=[sgn, vn, GUARD], writes=["ACT:%d" % wi])
            def ffn_down(grp):
                for c in range(8):
                    wd = WDN[wdn_i[0] % 2]
                    wdn = "wdn%d" % (wdn_i[0] % 2)
                    wdn_i[0] += 1
                    s.op("pool", I("dma_start", out=wd[:], in_=wdn_fm[:, :, c * 128:(c + 1) * 128]),
                         reads=[GUARD, "WDNB"], writes=[wdn], dmakey=wdn)
                    for wi, (w, tok0, n_in, lo, hi) in enumerate(grp):
                        no = n_in - 2
                        o0 = tok0 + 1
                        p, pn = psS.next()
                        for kc in range(NCF):
                            s.op("pe", I("matmul", p[:, 0:no], lhsT=wd[:, kc, :], rhs=ACTT[:, kc, wi, 0:no],
                                start=(kc == 0), stop=(kc == NCF - 1)),
                                reads=[wdn, "ACT:%d" % wi, GUARD], writes=[pn])
                        chunks = sorted(set([o0 // 512, (o0 + no - 1) // 512]))
                        xt_, xn_ = xring.next()
                        s.op("sp", I("dma_start", out=xt_[:, 0:no], in_=xa_fm[:, c, o0:o0 + no]),
                            reads=["XA:%d" % cc for cc in chunks], writes=[xn_], dmakey=xn_)
                        ot, on = xoring.next()
                        s.op("dve", I("tensor_tensor", out=ot[:, 0:no], in0=p[:, 0:no], in1=xt_[:, 0:no], op=ALU.add),
                            reads=[pn, xn_], writes=[on])
                        s.op("sp", I("dma_start", out=Xout[c * 128:(c + 1) * 128, o0:o0 + no], in_=ot[:, 0:no]),
                            reads=[on], writes=["%s:%d" % (xout_name, cc) for cc in chunks], dmakey="st_" + on)

            groups = [wins[g0:g0 + G] for g0 in range(0, 9, G)]
            ffn_norm(groups[0])
            for gi, grp in enumerate(groups):
                ffn_up(grp)
                if gi + 1 < len(groups):
                    ffn_norm(groups[gi + 1])
                ffn_down(grp)

        def attention(l, mx, pd, pdn, pt, ptn):
            base_head = {"A": 0, "B": 4, "C": 8, "D": 12}[mx]
            d = {"A": 64, "B": 32, "C": 96, "D": 64}[mx]
            scale = float(d) ** -0.5

            def kv_of(h):
                return h // 2 if mx == "A" else h

            def dist_min(k0, q0):
                if k0 + 127 < q0:
                    return q0 - (k0 + 127)
                if k0 > q0 + 511:
                    return k0 - (q0 + 511)
                return 0

            active = []

            def step_epilogues():
                for g in list(active):
                    try:
                        next(g)
                    except StopIteration:
                        active.remove(g)

            def flush_epilogues():
                while active:
                    step_epilogues()

            def evac(po, pon):
                ob, obn = fin_r.next()
                ld, ldn = ln_r.next()
                if mx in "AC":
                    s.op("dve", I("tensor_copy", out=ob[:, :], in_=po[0:64, :]), reads=[pon], writes=[obn])
                    s.op("dve", I("reciprocal", out=ld[64:65, :], in_=po[64:65, :]), reads=[pon], writes=[ldn])
                else:
                    s.op("act", I("activation", out=ob[:, :], in_=po[0:64, :], func=AF.Copy), reads=[pon], writes=[obn])
                    s.op("act", I("activation", out=ld[64:65, :], in_=po[64:65, :], func=AF.Ln), reads=[pon], writes=[ldn])
                return ob, obn, ld, ldn

            def epilogue(items, h, q0, qc):
                mo, mon = None, None
                for mi, (ob, obn, ld, ldn) in enumerate(items):
                    if mx not in "AC":
                        s.op("act", I("activation", out=ld[64:65, :], in_=ld[64:65, :], func=AF.Exp, scale=-1.0),
                             reads=[ldn], writes=[ldn])
                        yield
                    rh, rhn = qs_r.next()
                    rl, rln = qs_r.next()
                    s.op("dve", I("tensor_copy", out=rh[64:65, :], in_=ld[64:65, :]), reads=[ldn], writes=[rhn])
                    s.op("dve", I("tensor_tensor", out=rl[64:65, :], in0=ld[64:65, :], in1=rh[64:65, :], op=ALU.subtract),
                         reads=[ldn, rhn], writes=[rln])
                    yield
                    pb, pbn = psB.next()
                    s.op("pe", I("matmul", pb[:, :], lhsT=bdt[:, 3, :], rhs=rh[:, :], start=True, stop=False),
                         reads=[rhn, "blk32"], writes=[pbn])
                    s.op("pe", I("matmul", pb[:, :], lhsT=bdt[:, 3, :], rhs=rl[:, :], start=False, stop=True),
                         reads=[rln, "blk32"], writes=[pbn])
                    yield
                    if mx != "B":
                        mo, mon = mo_r.next()
                        s.op("dve", I("tensor_tensor", out=mo[:, :], in0=ob[:, :], in1=pb[0:64, :], op=ALU.mult),
                             reads=[obn, pbn], writes=[mon])
                    else:
                        s.op("dve", I("tensor_tensor", out=ob[:, :], in0=ob[:, :], in1=pb[0:64, :], op=ALU.mult),
                             reads=[obn, pbn], writes=[obn])
                    yield
                if mx == "B":
                    (n1, n1n, _, _), (n2, n2n, _, _) = items
                    df, dfn = t1_r.next()
                    s.op("dve", I("scalar_tensor_tensor", out=df[0:64, :], in0=n2[:, :], scalar=pd[0:64, 0:1],
                                  in1=n1[:, :], op0=ALU.mult, op1=ALU.add), reads=[n1n, n2n, pdn], writes=[dfn])
                    yield
                    sqt, sqn = sq_r.next()
                    s.op("act", I("activation", out=sqt[0:64, :], in_=df[0:64, :], func=AF.Square), reads=[dfn], writes=[sqn])
                    yield
                    pm, pmn = psB.next()
                    s.op("pe", I("matmul", pm[:, :], lhsT=bdt[:, 0, :], rhs=sqt[:, :], start=True, stop=True),
                         reads=[sqn, "blk32"], writes=[pmn])
                    yield
                    lt, ltn = rs_r.next()
                    s.op("act", I("activation", out=lt[0:64, :], in_=pm[0:64, :], func=AF.Ln, bias=EPS, scale=1.0 / 64),
                         reads=[pmn], writes=[ltn])
                    yield
                    s.op("act", I("activation", out=lt[0:64, :], in_=lt[0:64, :], func=AF.Exp, scale=-0.5),
                         reads=[ltn], writes=[ltn])
                    yield
                    mo, mon = mo_r.next()
                    s.op("dve", I("scalar_tensor_tensor", out=mo[:, :], in0=df[0:64, :], scalar=pd[0:64, 1:2],
                                  in1=lt[0:64, :], op0=ALU.mult, op1=ALU.mult), reads=[dfn, pdn, ltn], writes=[mon])
                hh = base_head + h
                s.op("sp", I("dma_start", out=MIX[hh * 64:(hh + 1) * 64, q0:q0 + 512], in_=mo[:, :]),
                     reads=[mon], writes=["MIX:%d" % qc], dmakey="st_" + mon)

            pend = []
            WIDE_KEEP = 2
            SLOPE_KEEP = 2

            def do_pv(item, last_flush=False):
                po, pon, Eap, En, kb, kvh, q0, first, last, on_last = item
                step_epilogues()
                if mx == "D":
                    c_lo = MS_C0 - (kb * 128 - q0)
                    s.op("dve", I("tensor_tensor", out=Eap, in0=Eap, in1=cMst[:, c_lo:c_lo + 512], op=ALU.mult),
                         reads=[En, "cMst"], writes=[En])
                voff = 32768 + (kvh * 32 + kb) * 65
                s.op("pe", I("matmul", po[:, :], lhsT=arena[:, voff:voff + 128], rhs=Eap, start=first, stop=last),
                     reads=[En, "V:%d" % kb, "VONES", GUARD], writes=[pon])
                if last:
                    on_last(po, pon)

            def flush_to(n):
                while len(pend) > n:
                    do_pv(pend.pop(0))

            for h in range(4):
                slope = (SLOPES[h] if mx == "B" else SLOPES[4 + h]) if mx in "BD" else None
                maps = [(0, 32), (32, 64)] if mx == "B" else [(0, d)]
                kvh = kv_of(h)
                for qc in range(8):
                    q0 = qc * 512
                    if mx == "D":
                        kbs = [kb for kb in range(32) if q0 - 1024 <= kb * 128 < q0 + 512 + 1024]
                    elif mx == "B":
                        kbs = [kb for kb in range(32) if slope * dist_min(kb * 128, q0) <= 60.0]
                    else:
                        kbs = list(range(32))
                    accs = []

                    evs = []

                    def after_last(po, pon, evs=evs, nmaps=len(maps), h=h, q0=q0, qc=qc):
                        if not evs:
                            flush_epilogues()
                        evs.append(evac(po, pon))
                        if len(evs) == nmaps:
                            active.append(epilogue(evs, h, q0, qc))

                    for mi, (r0, r1) in enumerate(maps):
                        po, pon = psO.next()
                        accs.append((po, pon))
                        on_last = after_last
                        nk = len(kbs)
                        if slope is None:
                            i = 0
                            while i < nk:
                                grp = kbs[i:i + 2]
                                pp2, (pna, pnb) = psP.next()
                                names = [pna, pnb][:len(grp)]
                                for gi, kb in enumerate(grp):
                                    k0 = kb * 128
                                    s.op("pe", I("matmul", pp2[:, gi * 512:(gi + 1) * 512], lhsT=KT[:, kvh, k0:k0 + 128],
                                                 rhs=QT[:, h, q0:q0 + 512], start=True, stop=True),
                                         reads=["K:%d:%d" % (kvh, kb // 4), "Q:%d:%d" % (h, qc), GUARD], writes=[names[gi]])
                                flush_to(WIDE_KEEP)
                                Et, En = E_r.next()
                                w = 512 * len(grp)
                                s.op("act", I("activation", out=Et[:, 0:w], in_=pp2[:, 0:w], func=AF.Exp, scale=scale),
                                     reads=names, writes=[En])
                                for gi, kb in enumerate(grp):
                                    idx = i + gi
                                    pend.append((po, pon, Et[:, gi * 512:(gi + 1) * 512], En, kb, kvh, q0, idx == 0,
                                                 idx == nk - 1, on_last))
                                i += 2
                        else:
                            for i, kb in enumerate(kbs):
                                k0 = kb * 128
                                ps_, psn = psS.next()
                                rr0, rr1 = (r0, r1) if mx == "B" else (0, 128)
                                s.op("pe", I("matmul", ps_[:, :], lhsT=KT[rr0:rr1, kvh, k0:k0 + 128],
                                             rhs=QT[rr0:rr1, h, q0:q0 + 512], start=True, stop=True),
                                     reads=["K:%d:%d" % (kvh, kb // 4), "Q:%d:%d" % (h, qc), GUARD], writes=[psn])
                                flush_to(SLOPE_KEEP)
                                Et, En = E_r.next()
                                tt, ttn = tb_r.next()
                                if k0 + 127 < q0:
                                    bias_ap, mul, cb = cLt[:, :], slope / scale, -slope * (q0 - k0)
                                elif k0 > q0 + 511:
                                    bias_ap, mul, cb = cLt[:, :], -slope / scale, -slope * (k0 - q0)
                                else:
                                    j = (k0 - q0) // 128
                                    bias_ap, mul, cb = cW0t[:, 384 - 128 * j:896 - 128 * j], slope / scale, 0.0
                                s.op("dve", I("scalar_tensor_tensor", out=tt[:, :], in0=bias_ap, scalar=float(mul),
                                              in1=ps_[:, :], op0=ALU.mult, op1=ALU.add),
                                     reads=[psn, "cLt", "cW0t"], writes=[ttn])
                                s.op("act", I("activation", out=Et[:, 0:512], in_=tt[:, :], func=AF.Exp, scale=scale,
                                              bias=float(cb)), reads=[ttn], writes=[En])
                                pend.append((po, pon, Et[:, 0:512], En, kb, kvh, q0, i == 0, i == nk - 1, on_last))
            while pend:
                do_pv(pend.pop(0))
            flush_epilogues()

        for l in range(n_layers):
            Xin, xin_name = (xT, "X0") if l == 0 else (XB, "XB")
            last = (l == n_layers - 1)
            Xout, xout_name = (outT, "OUT") if last else (XB, "XB")
            if not do_ffn:
                pass
            layer(l, Xin, xin_name, Xout, xout_name)

        finals = [k for k in s.dma_ops if k.startswith("st_")]
        s.emit(st, final_waits=finals)
        counts = {e: len(s.ops[e]) for e in ENGS}
        _CACHE["sched"] = s
    return nc, counts


def _prep_inputs(inputs):
    f32 = np.float32
    consts = _host_constants()
    pp = _pack_pp(inputs)
    shared = {
        "w_in": np.ascontiguousarray(inputs["w_in"], dtype=f32),
        "w_out": np.ascontiguousarray(inputs["w_out"], dtype=f32),
        "w_up": np.ascontiguousarray(inputs["w_up"], dtype=f32),
        "w_down": np.ascontiguousarray(inputs["w_down"], dtype=f32),
        "c_wqb": np.ascontiguousarray(inputs["c_wqb"], dtype=f32),
        "c_wkvb": np.ascontiguousarray(inputs["c_wkvb"], dtype=f32),
        "pp": pp,
    }
    shared.update(consts)
    return shared


def kernel(**inputs):
    x = np.asarray(inputs["x"], dtype=np.float32)
    shared = _prep_inputs(inputs)
    if "nc" not in _CACHE:
        _CACHE["nc"] = build_program()[0]
    nc = _CACHE["nc"]
    in_maps = []
    for c in range(8):
        m = dict(shared)
        m["xT"] = np.ascontiguousarray(x[c].T)
        in_maps.append(m)
    res = run_bass_kernel_spmd(nc, in_maps, core_ids=list(range(8)))
    out = np.stack([np.asarray(r["outT"]).T for r in res.results], axis=0)
    return np.ascontiguousarray(out.astype(np.float32))
```

```python
import contextlib
import math
import numpy as np
import concourse.bass as bass
import concourse.mybir as mybir
from concourse.bass_utils import run_bass_kernel_spmd

F32 = mybir.dt.float32
BF16 = mybir.dt.bfloat16
ALU = mybir.AluOpType
AF = mybir.ActivationFunctionType

S = 4096
DM = 1024
DEPTH = 4
NT = 8
DFF = 2816
NCF = 22
EPS = 1e-6
INW = 2464
ENGS = ("pe", "act", "dve", "pool", "sp")
PIPE_DEPTH = 1


class Buf:
    __slots__ = ("name", "last_w", "readers")

    def __init__(self, name):
        self.name = name
        self.last_w = None
        self.readers = []


class Op:
    __slots__ = ("eng", "fn", "waits", "sig", "sigval", "dmakey", "phase", "fin")

    def __init__(self, eng, fn, dmakey=None):
        self.eng = eng
        self.fn = fn
        self.waits = {}
        self.sig = False
        self.sigval = 0
        self.dmakey = dmakey


class Sched:
    def __init__(self, nc):
        self.nc = nc
        self.ops = {e: [] for e in ENGS}
        self.known = {e: {} for e in ENGS}
        self.dma_ops = {}
        self.bufs = {}
        self.seq = []
        self.phase = "init"

    def buf(self, name):
        b = self.bufs.get(name)
        if b is None:
            b = Buf(name)
            self.bufs[name] = b
        return b

    def op(self, eng, fn, reads=(), writes=(), dmakey=None):
        o = Op(eng, fn, dmakey)
        lst = self.ops[eng]
        idx = len(lst)
        if dmakey is not None:
            dl = self.dma_ops.setdefault(dmakey, [])
            tok = ("dma:" + dmakey, len(dl))
            dl.append(o)
        else:
            tok = (eng, idx)
        deps = {}

        def need(t):
            if t is None:
                return
            k, i = t
            if k == "pe" and tok[0] == "pe":
                return
            if deps.get(k, -1) < i:
                deps[k] = i

        ps_reads = [x for x in reads if x.startswith("ps")]
        if ps_reads:
            reads = [x for x in reads if not x.startswith("ps")]
            writes = list(writes) + [x for x in ps_reads if x not in writes]
        rb = [self.buf(x) for x in reads]
        wb = [self.buf(x) for x in writes]
        for b in rb:
            need(b.last_w)
        for b in wb:
            need(b.last_w)
            for r in b.readers:
                if r[0] != tok[0] or tok[0].startswith("dma:"):
                    need(r)
        kn = self.known[eng]
        for k, i in deps.items():
            if kn.get(k, -1) < i:
                kn[k] = i
                o.waits[k] = i
                if k.startswith("dma:"):
                    self.dma_ops[k[4:]][i].sig = True
                else:
                    self.ops[k][i].sig = True
        for b in rb:
            b.readers.append(tok)
        for b in wb:
            b.last_w = tok
            b.readers = []
        lst.append(o)
        o.phase = self.phase
        self.seq.append(o)
        return o

    def estimate(self, ghz=1.95, sem_lat=150.0, dma_fixed=2000.0, dma_bpns=150.0):
        free = {e: 0.0 for e in ENGS}
        busy = {}
        span = {}
        dma_free = [0.0]

        def fs(ap):
            v = ap.free_size
            return v() if callable(v) else v

        def nb(ap):
            v = ap.nbytes
            return v() if callable(v) else v

        def cost(o):
            m = getattr(o.fn, "meta", None)
            if m is None:
                return 100.0, 0
            fn, args, kw = m
            if fn == "dma_start":
                ap = kw["out"]
                return 60.0, nb(ap)
            if fn == "matmul":
                n = fs(kw["rhs"])
                c = max(n, 64) / ghz + 8
                if kw["rhs"].dtype == F32:
                    c *= 4
                return c, 0
            out = kw.get("out", args[0] if args else None)
            n = fs(out)
            if fn == "activation":
                return 186 + 0.833 * n + 93, 0
            if fn in ("memset",):
                return (n / 2 + 100) / 0.96, 0
            srcs = [kw.get(k) for k in ("in0", "in1", "in_") if kw.get(k) is not None]
            psum = any(type(a.tensor).__name__.startswith("PSum") for a in srcs if hasattr(a, "tensor"))
            allbf = all(getattr(a, "dtype", None) == BF16 for a in srcs + [out] if hasattr(a, "dtype"))
            if fn in ("tensor_tensor", "scalar_tensor_tensor"):
                rate = 2.0 if (allbf and not psum) else 1.0
            else:
                rate = 1.0 if psum else 2.0
            return (n / rate + 151) / 0.96, 0

        for o in self.seq:
            st_ = free[o.eng]
            for k, i in o.waits.items():
                dep = self.dma_ops[k[4:]][i] if k.startswith("dma:") else self.ops[k][i]
                st_ = max(st_, dep.fin + (sem_lat if k != o.eng else 60.0))
            c, nbytes = cost(o)
            if o.dmakey is not None:
                free[o.eng] = st_ + c
                t0 = max(st_ + c, dma_free[0])
                dma_free[0] = t0 + nbytes / dma_bpns
                o.fin = max(st_ + dma_fixed, dma_free[0] + 500.0)
                b = busy.setdefault(o.phase, {}); b["dma"] = b.get("dma", 0.0) + nbytes / dma_bpns
            else:
                o.fin = st_ + c
                free[o.eng] = o.fin
                b = busy.setdefault(o.phase, {}); b[o.eng] = b.get(o.eng, 0.0) + c
            sp_ = span.setdefault(o.phase, [st_, o.fin])
            sp_[0] = min(sp_[0], st_); sp_[1] = max(sp_[1], o.fin)
        total = max(o.fin for o in self.seq)
        return total, span, busy

    def emit(self, st, final_waits=()):
        nc = self.nc
        sems = {}
        for e in ENGS:
            sems[e] = st.enter_context(nc.semaphore("s_" + e))
        for k in self.dma_ops:
            sems["dma:" + k] = st.enter_context(nc.semaphore("d_" + k))
        for e in ENGS:
            c = 0
            for o in self.ops[e]:
                if o.dmakey is None and o.sig:
                    c += 1
                o.sigval = c
        for k, dl in self.dma_ops.items():
            c = 0
            for o in dl:
                c += 16
                o.sigval = c
        block = st.enter_context(nc.Block())
        engmap = {"pe": "tensor", "act": "scalar", "dve": "vector", "pool": "gpsimd", "sp": "sync"}

        def make(e):
            def body(engine):
                for o in self.ops[e]:
                    for k, i in o.waits.items():
                        if k.startswith("dma:"):
                            v = self.dma_ops[k[4:]][i].sigval
                        else:
                            v = self.ops[k][i].sigval
                        engine.wait_ge(sems[k], v)
                    ins = o.fn(engine)
                    if o.dmakey is not None:
                        ins.then_inc(sems["dma:" + o.dmakey], 16)
                    elif o.sig:
                        ins.then_inc(sems[e], 1)
                if e == "sp":
                    for k in final_waits:
                        dl = self.dma_ops[k]
                        engine.wait_ge(sems["dma:" + k], dl[-1].sigval)
            return body

        for e in ENGS:
            getattr(block, engmap[e])(make(e))


class Ring:
    def __init__(self, items):
        self.items = items
        self.i = 0

    def next(self):
        it = self.items[self.i % len(self.items)]
        self.i += 1
        return it


def I(fn, *args, **kw):
    f = lambda e: getattr(e, fn)(*args, **kw)
    f.meta = (fn, args, kw)
    return f


PP = {}
_off = 0
for _n, _w in [("n1g", 8), ("n2g", 8), ("cw0", NCF * 2), ("cw1", NCF * 2), ("cw2", NCF * 2), ("cb", NCF * 2),
               ("a_qn", 1), ("a_kn", 1), ("b_qn", 1), ("b_kn", 1), ("b_sub", 1), ("c_qa", 2), ("c_kva", 1),
               ("c_qn", 1), ("c_kn", 1), ("d_qn", 1), ("d_kn", 1), ("lq1", 1), ("lk1", 1), ("lq2", 1),
               ("lk2", 1), ("lam_init", 1), ("oml", 1)]:
    PP[_n] = (_off, _w)
    _off += _w
NPP = _off

A_Q, A_K, A_V = 0, 256, 384
B_Q, B_K, B_V = 512, 768, 1024
C_QL, C_KVL, C_KR = 1280, 1536, 1664
D_Q, D_K, D_V = 1696, 1952, 2208
MIXER_COLS = {"A": (0, 512), "B": (512, 1280), "C": (1280, 1696), "D": (1696, 2464)}

SLOPES = [2.0 ** (-(i + 1)) for i in range(8)]
MS_C0 = 1408
MS_W = 2944


def _host_constants():
    f32 = np.float32
    pos = np.arange(S, dtype=f32)
    rows = np.floor(pos / 64).astype(f32)
    cols = (pos - rows * 64).astype(f32)

    def freqs(dim):
        return (1.0 / (f32(10000.0) ** (np.arange(0, dim, 2, dtype=f32) / f32(dim)))).astype(f32)

    fa = freqs(32)
    ang_row = (rows[:, None] * fa[None, :]).astype(f32)
    ang_col = (cols[:, None] * fa[None, :]).astype(f32)
    ang_mla = (pos[:, None] * freqs(32)[None, :]).astype(f32)
    ropeA = np.zeros((2, 64, S), f32)
    for m in range(64):
        ang = ang_row if m < 32 else ang_col
        f = (m % 32) % 16
        ropeA[0, m] = np.cos(ang[:, f])
        ropeA[1, m] = np.sin(ang[:, f])
    ropeC = np.zeros((2, 96, S), f32)
    ropeC[0, :64] = 1.0
    for m in range(64, 96):
        f = (m - 64) % 16
        ropeC[0, m] = np.cos(ang_mla[:, f])
        ropeC[1, m] = np.sin(ang_mla[:, f])
    cR = np.zeros((2, 96, 96), f32)
    for base in (0, 32):
        for j in range(16):
            cR[0, base + j + 16, base + j] = -1.0
            cR[0, base + j, base + j + 16] = 1.0
    for j in range(16):
        cR[1, 64 + j + 16, 64 + j] = -1.0
        cR[1, 64 + j, 64 + j + 16] = 1.0
    ki = np.arange(128, dtype=f32)[:, None]
    cL = (ki - np.arange(512, dtype=f32)[None, :]).astype(f32)
    cW0 = (-np.abs(np.arange(896, dtype=f32)[None, :] - ki - 384.0)).astype(f32)
    delta = (np.arange(MS_W)[None, :] - np.arange(128)[:, None] - MS_C0)
    ad = np.abs(delta)
    mult = np.zeros_like(delta, dtype=f32)
    for d in (1, 4, 16):
        mult += ((delta % d == 0) & (ad <= 64 * d)).astype(f32)
    cBD = np.zeros((4, 128, 128), f32)
    for a, b in ((0, 64), (64, 128)):
        cBD[0, a:b, a:b] = 1.0
    for a in range(0, 128, 32):
        cBD[1, a:a + 32, a:a + 32] = 1.0
    cBD[2, 0:96, 0:96] = 1.0
    cBD[2, 96:128, 96:128] = 1.0
    cBD[3, 64, 0:64] = 1.0
    return {"ropeA": ropeA, "ropeC": ropeC, "cR": cR, "cL": cL, "cW0": cW0, "cMs": mult.astype(f32), "cBD": cBD}


def _pack_pp(inp):
    f32 = np.float32
    pp = np.zeros((DEPTH, 128, NPP), f32)

    def put(name, l, arr2d):
        o, w = PP[name]
        r = arr2d.shape[0]
        pp[l, :r, o:o + w] = arr2d

    for l in range(DEPTH):
        put("n1g", l, np.asarray(inp["norm1_g"][l]).reshape(8, 128).T)
        put("n2g", l, np.asarray(inp["norm2_g"][l]).reshape(8, 128).T)
        cw = np.asarray(inp["conv_w"][l])
        for t in range(3):
            put("cw%d" % t, l, cw[t].reshape(2 * NCF, 128).T)
        put("cb", l, np.asarray(inp["conv_b"][l]).reshape(2 * NCF, 128).T)
        put("a_qn", l, np.asarray(inp["a_qn_g"][l])[:, None])
        put("a_kn", l, np.asarray(inp["a_kn_g"][l])[:, None])
        put("b_qn", l, np.concatenate([inp["b_qn_g"][l], inp["b_qn_g"][l]])[:, None])
        put("b_kn", l, np.concatenate([inp["b_kn_g"][l], inp["b_kn_g"][l]])[:, None])
        put("b_sub", l, np.asarray(inp["b_sub_g"][l])[:, None])
        put("c_qa", l, np.asarray(inp["c_qa_g"][l]).reshape(2, 128).T)
        put("c_kva", l, np.asarray(inp["c_kva_g"][l])[:, None])
        put("c_qn", l, np.asarray(inp["c_qn_g"][l])[:, None])
        put("c_kn", l, np.asarray(inp["c_kn_g"][l])[:, None])
        put("d_qn", l, np.asarray(inp["d_qn_g"][l])[:, None])
        put("d_kn", l, np.asarray(inp["d_kn_g"][l])[:, None])
        put("lq1", l, np.asarray(inp["b_lam_q1"][l])[:, None])
        put("lk1", l, np.asarray(inp["b_lam_k1"][l])[:, None])
        put("lq2", l, np.asarray(inp["b_lam_q2"][l])[:, None])
        put("lk2", l, np.asarray(inp["b_lam_k2"][l])[:, None])
        lam_init = 0.8 - 0.6 * math.exp(-0.3 * l)
        put("lam_init", l, np.full((128, 1), lam_init, f32))
        put("oml", l, np.full((128, 1), 1.0 - lam_init, f32))
    return pp


_CACHE = {}


def build_program(n_layers=DEPTH, mixers="ABCD", do_ffn=True, dbg=False, do_attn=True):
    nc = bass.Bass("TRN2", target_bir_lowering=False)
    L = DEPTH

    def din(name, shape, dt=F32):
        return nc.dram_tensor(name, shape, dt, kind="ExternalInput").ap()

    xT = din("xT", [DM, S])
    w_in = din("w_in", [L, DM, INW])
    w_out = din("w_out", [L, DM, DM])
    w_up = din("w_up", [L, DM, 2 * DFF])
    w_down = din("w_down", [L, DFF, DM])
    c_wqb = din("c_wqb", [L, 256, 384])
    c_wkvb = din("c_wkvb", [L, 128, 512])
    ppd = din("pp", [L, 128, NPP])
    ropeA = din("ropeA", [2, 64, S])
    ropeC = din("ropeC", [2, 96, S])
    cR = din("cR", [2, 96, 96])
    cL = din("cL", [128, 512])
    cW0 = din("cW0", [128, 896])
    cMs = din("cMs", [128, MS_W])
    cBD = din("cBD", [4, 128, 128])
    outT = nc.dram_tensor("outT", [DM, S], F32, kind="ExternalOutput").ap()
    skind = "ExternalOutput" if dbg else "Internal"
    XA = nc.dram_tensor("XA", [DM, S], F32, kind=skind).ap()
    XB = nc.dram_tensor("XB", [DM, S], F32, kind=skind).ap()
    HT = nc.dram_tensor("HT", [DM, S], BF16, kind=skind).ap()
    MIX = nc.dram_tensor("MIX", [DM, S], BF16, kind=skind).ap()
    WUPB = nc.dram_tensor("WUPB", [DM, 2 * DFF], BF16).ap()
    WDNB = nc.dram_tensor("WDNB", [DFF, DM], BF16).ap()

    st = contextlib.ExitStack()
    with st:
        def SB(name, shape, dt):
            return st.enter_context(nc.sbuf_tensor(name, shape, dt))

        s = Sched(nc)
        GUARD = "GUARD"

        G = 2
        ARENA_N = 32768 + 4 * 32 * 65 + 64
        arena = SB("arena", [128, ARENA_N], BF16)
        QT = arena[:, 0:16384].rearrange("p (h t) -> p h t", h=4)
        KT = arena[:, 16384:32768].rearrange("p (h t) -> p h t", h=4)
        VV = arena[:, 32768:32768 + 4 * 32 * 65].rearrange("p (h b d) -> p h b d", h=4, b=32)
        H2 = arena[:, 0:8 * G * 512].rearrange("p (k w t) -> p k w t", k=8, w=G)
        ACTT = arena[:, 8 * G * 512:(8 + NCF) * G * 512].rearrange("p (k w t) -> p k w t", k=NCF, w=G)
        warena = SB("warena", [128, 12288], BF16)
        WSLOT = [warena[:, i * 6144:(i + 1) * 6144].rearrange("p (k n) -> p k n", k=8) for i in range(2)]
        WUP = [warena[:, i * 2048:(i + 1) * 2048].rearrange("p (k n) -> p k n", k=8) for i in range(3)]
        WDN = [warena[:, 6144 + i * 2816:6144 + (i + 1) * 2816].rearrange("p (k n) -> p k n", k=NCF) for i in range(2)]
        hb = [SB("hbuf%d" % i, [128, 8, 512], BF16) for i in range(2)]
        hring = Ring([(hb[i], "hbuf%d" % i) for i in range(2)])
        xp = [SB("xp%d" % i, [128, 512], F32) for i in range(3)]
        xring = Ring([(xp[i], "xp%d" % i) for i in range(3)])
        xo = [SB("xo%d" % i, [128, 512], F32) for i in range(2)]
        xoring = Ring([(xo[i], "xo%d" % i) for i in range(2)])

        def mkring(prefix, n, shape, dt):
            ts = [SB("%s%d" % (prefix, i), shape, dt) for i in range(n)]
            return Ring([(ts[i], "%s%d" % (prefix, i)) for i in range(n)])

        sq_r = mkring("sqb", 2, [128, 512], BF16)
        qs_r = mkring("qsb", 2, [128, 512], BF16)
        ln_r = mkring("lnv", 2, [128, 512], F32)
        rs_r = mkring("rstd", 2, [128, 512], F32)
        t1_r = mkring("t1_", 2, [128, 512], F32)
        t2_r = mkring("t2_", 2, [128, 512], F32)
        t3_r = mkring("t3_", 2, [128, 512], F32)
        E_r = mkring("E", 3, [128, 1024], BF16)
        tb_r = mkring("tb", 3, [128, 512], F32)
        rp_r = mkring("rope", 2, [96, 2, 512], F32)
        mo_r = mkring("mo", 2, [64, 512], BF16)
        fin_r = mkring("fin", 3, [64, 512], F32)
        qa_t = SB("qa_t", [128, 2, 512], BF16)
        kvn_t = SB("kvn_t", [128, 512], BF16)
        kcraw = SB("kcraw", [96, 512], F32)
        ppt = [SB("ppt%d" % i, [128, NPP], F32) for i in range(2)]
        pdt = [SB("pdt%d" % i, [128, 8], F32) for i in range(2)]
        ones_bf = SB("ones_bf", [128, 128], BF16)
        bdt = SB("bdt", [128, 4, 128], BF16)
        ones_f = SB("ones_f", [128, 128], F32)
        R0 = SB("R0", [96, 2, 96], F32)
        RG = [SB("RG%d" % i, [128, 4, 128], BF16) for i in range(2)]
        cLt = SB("cLt", [128, 512], F32)
        cW0t = SB("cW0t", [128, 896], F32)
        cMst = SB("cMst", [128, MS_W], BF16)
        wqb_t = SB("wqb_t", [128, 2, 416], BF16)
        wkvn_t = SB("wkvn_t", [128, 320], BF16)
        wkvv_t = SB("wkvv_t", [128, 4, 64], BF16)
        dummy = SB("dummy_t", [128, 8], F32)

        PS2 = [st.enter_context(nc.psum_tensor("psd%d" % i, [128, 1024], F32)) for i in range(4)]
        PS = [PS2[i // 2][:, (i % 2) * 512:(i % 2 + 1) * 512] for i in range(8)]
        psS = Ring([(PS[i], "ps%d" % i) for i in (0, 1, 2, 3)])
        psP = Ring([(PS2[i], ("ps%d" % (2 * i), "ps%d" % (2 * i + 1))) for i in (0, 1)])
        psO = Ring([(PS[i], "ps%d" % i) for i in (4, 5)])
        psB = Ring([(PS[i], "ps%d" % i) for i in (6, 7)])
        psF = Ring([(PS[i], "ps%d" % i) for i in range(8)])
        psM = Ring([(PS[i], "ps%d" % i) for i in (4, 6, 5, 7)])

        s.op("dve", I("memset", ones_bf[:], 1.0), writes=["ones_bf"])
        s.op("dve", I("memset", ones_f[:], 1.0), writes=["ones_f"])
        s.op("pool", I("dma_start", out=bdt[:], in_=cBD.rearrange("a k m -> k a m")), writes=["blk32"], dmakey="c_BD")
        for i in range(2):
            s.op("dve", I("memset", RG[i][:], 0.0), writes=["RG%d" % i])
        for (rt_, rn_) in sq_r.items + qs_r.items:
            s.op("dve", I("memset", rt_[:], 0.0), writes=[rn_])
        s.op("dve", I("memset", arena[:, 32768 + 4 * 32 * 65:ARENA_N], 0.0), writes=["VTAIL"])
        s.op("dve", I("memset", wqb_t[:], 0.0), writes=["wqb_t"])
        s.op("dve", I("memset", wkvn_t[:], 0.0), writes=["wkvn_t"])
        s.op("dve", I("memset", dummy[:], 0.0), writes=["dummy"])
        s.op("sp", I("dma_start", out=R0[:], in_=cR.rearrange("a k m -> k a m")), writes=["R0"], dmakey="c_R0")
        s.op("sp", I("dma_start", out=cLt[:], in_=cL[:, :]), writes=["cLt"], dmakey="c_L")
        s.op("sp", I("dma_start", out=cW0t[:], in_=cW0[:, :]), writes=["cW0t"], dmakey="c_W0")
        s.op("pool", I("dma_start", out=cMst[:], in_=cMs[:, :]), writes=["cMst"], dmakey="c_Ms")

        def phase_barrier():
            s.op("pool", I("memset", dummy[:, 0:1], 0.0), writes=[GUARD, "dummy"])

        def rstd_from_ssq(ssq_ap, ssq_name, d, inv_n, n=512):
            lt, ln_ = ln_r.next()
            rt, rn = rs_r.next()
            s.op("act", I("activation", out=lt[0:d, 0:n], in_=ssq_ap, func=AF.Ln, bias=EPS, scale=inv_n),
                 reads=[ssq_name], writes=[ln_])
            s.op("act", I("activation", out=rt[0:d, 0:n], in_=lt[0:d, 0:n], func=AF.Exp, scale=-0.5),
                 reads=[ln_], writes=[rn])
            return rt, rn

        def norm_a(src, src_name, d, onesm, inv_n, g_ap, g_name, rg, rg_name, tabs, tabs_name, dest, dest_names,
                   ncol=512, ring=None):
            n = ncol
            src_names = list(src_name) if isinstance(src_name, (list, tuple)) else [src_name]
            ring = ring or psM
            sqt, sqn = sq_r.next()
            s.op("act", I("activation", out=sqt[0:d, 0:n], in_=src, func=AF.Square),
                 reads=src_names, writes=[sqn])
            if rg is not None:
                qt, qn = qs_r.next()
                s.op("act", I("activation", out=qt[0:d, 0:n], in_=src, func=AF.Copy),
                     reads=src_names, writes=[qn])
            pm, pmn = ring.next()
            s.op("pe", I("matmul", pm[:, 0:n], lhsT=onesm, rhs=sqt[:, 0:n], start=True, stop=True),
                 reads=[sqn, "ones_bf", "blk32"], writes=[pmn])
            pr, prn = None, None
            if rg is not None:
                pr, prn = ring.next()
                s.op("pe", I("matmul", pr[:, 0:n], lhsT=rg, rhs=qt[:, 0:n], start=True, stop=True),
                     reads=[qn, rg_name], writes=[prn])
            return (src, src_names, d, n, inv_n, g_ap, g_name, rg, tabs, tabs_name, dest, dest_names, pm, pmn, pr, prn)

        def norm_b(ctx):
            (src, src_names, d, n, inv_n, g_ap, g_name, rg, tabs, tabs_name, dest, dest_names, pm, pmn, pr, prn) = ctx
            lt, ln_ = ln_r.next()
            rt, rn = rs_r.next()
            s.op("act", I("activation", out=lt[0:d, 0:n], in_=pm[0:d, 0:n], func=AF.Ln, bias=EPS, scale=inv_n),
                 reads=[pmn], writes=[ln_])
            s.op("act", I("activation", out=rt[0:d, 0:n], in_=lt[0:d, 0:n], func=AF.Exp, scale=-0.5),
                 reads=[ln_], writes=[rn])
            if rg is None:
                s.op("dve", I("scalar_tensor_tensor", out=dest, in0=src, scalar=g_ap, in1=rt[0:d, 0:n],
                              op0=ALU.mult, op1=ALU.mult),
                     reads=src_names + [g_name, rn, GUARD], writes=dest_names)
                return
            a1, a1n = t1_r.next()
            a2, a2n = t2_r.next()
            a3, a3n = t3_r.next()
            cos_ap, sin_ap = tabs
            s.op("dve", I("scalar_tensor_tensor", out=a1[0:d, 0:n], in0=src, scalar=g_ap, in1=cos_ap,
                          op0=ALU.mult, op1=ALU.mult),
                 reads=src_names + [g_name, tabs_name], writes=[a1n])
            s.op("dve", I("tensor_tensor", out=a2[0:d, 0:n], in0=pr[0:d, 0:n], in1=sin_ap, op=ALU.mult),
                 reads=[prn, tabs_name], writes=[a2n])
            s.op("dve", I("tensor_tensor", out=a3[0:d, 0:n], in0=a1[0:d, 0:n], in1=a2[0:d, 0:n], op=ALU.add),
                 reads=[a1n, a2n], writes=[a3n])
            s.op("dve", I("tensor_tensor", out=dest, in0=a3[0:d, 0:n], in1=rt[0:d, 0:n], op=ALU.mult),
                 reads=[a3n, rn, GUARD], writes=dest_names)

        def norm_rope(*a, **kw):
            norm_b(norm_a(*a, **kw))

        def proj_fm(W, wname, c0, M, hbt, hbn):
            p, pn = psS.next()
            for kc in range(8):
                s.op("pe", I("matmul", p[:, :], lhsT=W[:, kc, c0:c0 + 128], rhs=hbt[:, kc, :],
                             start=(kc == 0), stop=(kc == 7)),
                     reads=[wname, hbn, GUARD], writes=[pn])
            return p, pn

        def proj_v(W, wname, c0, nh, hbt, hbn, t):
            ncols = nh * 64
            for blk in range(4):
                p, pn = psS.next()
                for kc in range(8):
                    s.op("pe", I("matmul",
                        p[:, 0:ncols], lhsT=hbt[:, kc, blk * 128:(blk + 1) * 128], rhs=W[:, kc, c0:c0 + ncols],
                        start=(kc == 0), stop=(kc == 7)),
                        reads=[wname, hbn, GUARD], writes=[pn])
                b = t * 4 + blk
                s.op("act", I("activation",
                    out=VV[:, 0:nh, b, 0:64], in_=p[:, 0:ncols].rearrange("p (h d) -> p h d", h=nh), func=AF.Copy),
                    reads=[pn, GUARD], writes=["V:%d" % b])

        def proj_v1(W, wname, c0, nh, hbt, hbn, blk):
            ncols = nh * 64
            p, pn = psS.next()
            for kc in range(8):
                s.op("pe", I("matmul", p[:, 0:ncols], lhsT=hbt[:, kc, blk * 128:(blk + 1) * 128],
                             rhs=W[:, kc, c0:c0 + ncols], start=(kc == 0), stop=(kc == 7)),
                     reads=[wname, hbn, GUARD], writes=[pn])
            return p, pn

        def evac_v(r, nh, b):
            p, pn = r
            s.op("act", I("activation", out=VV[:, 0:nh, b, 0:64],
                          in_=p[:, 0:nh * 64].rearrange("p (h d) -> p h d", h=nh), func=AF.Copy),
                 reads=[pn, GUARD], writes=["V:%d" % b])

        def load_rope(tab_dram, d, t):
            rt_, rn_ = rp_r.next()
            s.op("sp", I("dma_start", out=rt_[0:d, :, :],
                                             in_=tab_dram[:, :, t * 512:(t + 1) * 512].rearrange("a d t -> d a t")),
                 writes=[rn_], dmakey=rn_)
            return (rt_[0:d, 0, :], rt_[0:d, 1, :]), rn_

        def layer(l, Xin, xin_name, Xout, xout_name):
            pt = ppt[l % 2]
            ptn = "ppt%d" % (l % 2)
            pd = pdt[l % 2]
            pdn = "pdt%d" % (l % 2)
            rgt = RG[l % 2]
            rgn = "RG%d" % (l % 2)

            def pcol(name, j=0, rows=128):
                o, w = PP[name]
                return pt[0:rows, o + j:o + j + 1]

            s.op("sp", I("dma_start", out=pt[:], in_=ppd[l]), writes=[ptn], dmakey=ptn)
            s.op("dve", I("tensor_tensor", out=pd[0:32, 2:3], in0=pcol("lq1", 0, 32), in1=pcol("lk1", 0, 32),
                                                  op=ALU.mult), reads=[ptn], writes=[pdn + "a"])
            s.op("dve", I("tensor_tensor", out=pd[0:32, 3:4], in0=pcol("lq2", 0, 32), in1=pcol("lk2", 0, 32),
                                                  op=ALU.mult), reads=[ptn], writes=[pdn + "a"])
            pm, pmn = psM.next()
            s.op("pe", I("matmul", pm[:, 0:2], lhsT=ones_f[0:32, :], rhs=pd[0:32, 2:4], start=True, stop=True),
                 reads=[pdn + "a", "ones_f"], writes=[pmn])
            s.op("act", I("activation", out=pd[:, 4:6], in_=pm[:, 0:2], func=AF.Exp),
                 reads=[pmn], writes=[pdn + "b"])
            s.op("dve", I("tensor_tensor", out=pd[:, 6:7], in0=pd[:, 5:6], in1=pd[:, 4:5], op=ALU.subtract),
                 reads=[pdn + "b"], writes=[pdn + "c"])
            s.op("dve", I("tensor_tensor", out=pd[:, 0:1], in0=pd[:, 6:7], in1=pcol("lam_init"), op=ALU.subtract),
                 reads=[pdn + "c", ptn], writes=[pdn])
            s.op("dve", I("tensor_tensor", out=pd[:, 1:2], in0=pcol("b_sub"), in1=pcol("oml"), op=ALU.mult),
                 reads=[ptn], writes=[pdn])
            for j, (gname, ri, d) in enumerate([("a_qn", 0, 64), ("a_kn", 0, 64), ("c_qn", 1, 96), ("c_kn", 1, 96)]):
                s.op("dve", I("tensor_scalar",
                    out=rgt[0:d, j, 0:d], in0=R0[0:d, ri, 0:d], scalar1=pcol(gname, 0, d), scalar2=None,
                    op0=ALU.mult), reads=[ptn, "R0"], writes=[rgn])

            xin_fm = Xin.rearrange("(kc p) t -> p kc t", p=128)
            ht_fm = HT.rearrange("(kc p) t -> p kc t", p=128)
            mix_fm = MIX.rearrange("(kc p) t -> p kc t", p=128)
            win_fm = w_in[l].rearrange("(kc p) n -> p kc n", p=128)

            def norm_chunk(Xsrc_fm, xname_fn, gname, tok0, ncol, col_lo, col_hi, dest_fn, dest_names, zero_pad=False):
                pm_, pmn_ = psM.next()
                for kc in range(8):
                    xt_, xn_ = xring.next()
                    if zero_pad:
                        s.op("dve", I("memset", xt_[:, 0:ncol], 0.0), writes=[xn_])
                    s.op("sp", I("dma_start", out=xt_[:, col_lo:col_hi], in_=Xsrc_fm[:, kc, tok0 + col_lo:tok0 + col_hi]),
                        reads=xname_fn(), writes=[xn_], dmakey=xn_)
                    sqt, sqn = sq_r.next()
                    s.op("act", I("activation", out=sqt[:, 0:ncol], in_=xt_[:, 0:ncol],
                                                                          func=AF.Square),
                         reads=[xn_], writes=[sqn])
                    s.op("pe", I("matmul", pm_[:, 0:ncol], lhsT=ones_bf[:, :], rhs=sqt[:, 0:ncol],
                                                                  start=(kc == 0), stop=(kc == 7)),
                         reads=[sqn, "ones_bf"], writes=[pmn_])
                rt, rn = rstd_from_ssq(pm_[:, 0:ncol], pmn_, 128, 1.0 / DM, ncol)
                for kc in range(8):
                    xt_, xn_ = xring.next()
                    if zero_pad:
                        s.op("dve", I("memset", xt_[:, 0:ncol], 0.0), writes=[xn_])
                    s.op("sp", I("dma_start", out=xt_[:, col_lo:col_hi], in_=Xsrc_fm[:, kc, tok0 + col_lo:tok0 + col_hi]),
                        reads=xname_fn(), writes=[xn_], dmakey=xn_)
                    s.op("dve", I("scalar_tensor_tensor", out=dest_fn(kc), in0=xt_[:, 0:ncol], scalar=pcol(gname, kc), in1=rt[:, 0:ncol],
                        op0=ALU.mult, op1=ALU.mult),
                        reads=[xn_, ptn, rn, GUARD], writes=dest_names)

            def load_w(slot, c0, c1):
                W = WSLOT[slot]
                wn = "wslot%d" % slot
                s.op("pool", I("dma_start", out=W[:, :, 0:c1 - c0], in_=win_fm[:, :, c0:c1]),
                     reads=[GUARD], writes=[wn], dmakey=wn)
                return W, wn

            wslot_i = [0]

            first_pass = True
            phase_barrier()
            s.op("dve", I("memset", VV[:, :, :, 64:65], 1.0), reads=[GUARD], writes=["VONES"])
            allqk = ["%s:%d:%d" % (a, h, t) for a in "QK" for h in range(4) for t in range(NT)]

            def zero_pad_rows():
                s.op("dve", I("memset", QT[64:128, :, :], 0.0), reads=[GUARD], writes=allqk)
                s.op("dve", I("memset", KT[64:128, :, :], 0.0), reads=[GUARD], writes=allqk)

            zero_pad_rows()
            for mx in mixers:
                s.phase = "L%d proj%s" % (l, mx)
                if mx == "D" and "C" in mixers:
                    zero_pad_rows()
                c0, c1 = MIXER_COLS[mx]
                W, wn = load_w(wslot_i[0] % 2, c0, c1)
                wslot_i[0] += 1
                if mx == "C":
                    s.op("pool", I("dma_start", out=wqb_t[:, :, 0:384], in_=c_wqb[l].rearrange("(kc p) n -> p kc n", p=128)),
                         writes=["wqb_t"], dmakey="wqb_t")
                    kvv = c_wkvb[l].rearrange("p (h two d) -> p h two d", h=4, two=2)
                    s.op("pool", I("dma_start", out=wkvn_t[:, 0:256].rearrange("p (h d) -> p h d", h=4), in_=kvv[:, :, 0, :]), writes=["wkvn_t"],
                         dmakey="wkvn_t")
                    s.op("pool", I("dma_start", out=wkvv_t[:], in_=kvv[:, :, 1, :]), writes=["wkvv_t"],
                         dmakey="wkvv_t")
                for t in range(NT):
                    hbt, hbn = hring.next()
                    tsl = slice(t * 512, (t + 1) * 512)
                    if first_pass:
                        norm_chunk(xin_fm, lambda t=t: ["%s:%d" % (xin_name, t)], "n1g", t * 512, 512, 0, 512,
                                   lambda kc, hbt=hbt: hbt[:, kc, :], [hbn])
                        s.op("sp", I("dma_start", out=ht_fm[:, :, tsl], in_=hbt[:]),
                             reads=[hbn], writes=["HT:%d" % t], dmakey="st_" + hbn)
                    else:
                        s.op("sp", I("dma_start", out=hbt[:], in_=ht_fm[:, :, tsl]),
                             reads=["HT:%d" % t], writes=[hbn], dmakey="ld_" + hbn)
                    items = []
                    if mx == "A":
                        tabs, tabn = load_rope(ropeA, 64, t)
                        for h in range(4):
                            items.append((
                                lambda h=h: proj_fm(W, wn, A_Q + 64 * h, 64, hbt, hbn),
                                (lambda r, h=h: (r[0][0:64, :], r[1], 64, bdt[:, 0, :], 1.0 / 64,
                                                 pcol("a_qn", 0, 64), ptn, rgt[:, 0, :], rgn, tabs, tabn,
                                                 QT[0:64, h, tsl], ["Q:%d:%d" % (h, t)]),)))
                        for h in range(2):
                            items.append((
                                lambda h=h: proj_fm(W, wn, A_K + 64 * h, 64, hbt, hbn),
                                (lambda r, h=h: (r[0][0:64, :], r[1], 64, bdt[:, 0, :], 1.0 / 64,
                                                 pcol("a_kn", 0, 64), ptn, rgt[:, 1, :], rgn, tabs, tabn,
                                                 KT[0:64, h, tsl], ["K:%d:%d" % (h, t)]),)))
                        for blk in range(4):
                            items.append((lambda blk=blk: proj_v1(W, wn, A_V, 2, hbt, hbn, blk),
                                          lambda r, blk=blk: evac_v(r, 2, t * 4 + blk)))
                    elif mx in "BD":
                        qg, kg = ("b_qn", "b_kn") if mx == "B" else ("d_qn", "d_kn")
                        onesm = bdt[:, 1, :] if mx == "B" else bdt[:, 0, :]
                        inv_n = 1.0 / 32 if mx == "B" else 1.0 / 64
                        for h in range(4):
                            for (gname, coff, dst, tag) in ((qg, 0, QT, "Q"), (kg, 256, KT, "K")):
                                items.append((
                                    lambda h=h, coff=coff: proj_fm(W, wn, coff + 64 * h, 64, hbt, hbn),
                                    (lambda r, h=h, gname=gname, dst=dst, tag=tag: (
                                        r[0][0:64, :], r[1], 64, onesm, inv_n, pcol(gname, 0, 64), ptn, None, None, None,
                                        None, dst[0:64, h, tsl], ["%s:%d:%d" % (tag, h, t)]),)))
                        for blk in range(4):
                            items.append((lambda blk=blk: proj_v1(W, wn, 512, 4, hbt, hbn, blk),
                                          lambda r, blk=blk: evac_v(r, 4, t * 4 + blk)))
                    else:
                        tabs, tabn = load_rope(ropeC, 96, t)

                        def c_qlat1():
                            return proj_fm(W, wn, 0, 128, hbt, hbn), proj_fm(W, wn, 128, 128, hbt, hbn)

                        def c_qlat2(r):
                            pm_, pmn_ = psM.next()
                            for j, (pp_, ppn_) in enumerate(r):
                                sqt, sqn = sq_r.next()
                                s.op("act", I("activation", out=sqt[:, :], in_=pp_[:, :], func=AF.Square),
                                     reads=[ppn_], writes=[sqn])
                                s.op("pe", I("matmul", pm_[:, :], lhsT=ones_bf[:, :], rhs=sqt[:, :],
                                             start=(j == 0), stop=(j == 1)), reads=[sqn, "ones_bf"], writes=[pmn_])
                            rt, rn = rstd_from_ssq(pm_[:, :], pmn_, 128, 1.0 / 256)
                            for j, (pp_, ppn_) in enumerate(r):
                                s.op("dve", I("scalar_tensor_tensor", out=qa_t[:, j, :], in0=pp_[:, :],
                                              scalar=pcol("c_qa", j), in1=rt[:, :], op0=ALU.mult, op1=ALU.mult),
                                     reads=[ppn_, ptn, rn], writes=["qa_t"])

                        items.append((c_qlat1, c_qlat2))
                        items.append((lambda: proj_fm(W, wn, 256, 128, hbt, hbn),
                                      lambda r: norm_rope(r[0][:, :], r[1], 128, ones_bf[:, :], 1.0 / 128, pcol("c_kva"),
                                                          ptn, None, None, None, None, kvn_t[:, :], ["kvn_t"])))
                        items.append((lambda: proj_fm(W, wn, 384, 32, hbt, hbn),
                                      lambda r: s.op("act", I("activation", out=kcraw[64:96, :], in_=r[0][0:32, :],
                                                              func=AF.Copy), reads=[r[1]], writes=["kcraw_r"])))

                        def c_q1(h):
                            p, pn = psS.next()
                            for j in range(2):
                                s.op("pe", I("matmul", p[:, :], lhsT=wqb_t[:, j, 96 * h:96 * h + 128], rhs=qa_t[:, j, :],
                                             start=(j == 0), stop=(j == 1)), reads=["wqb_t", "qa_t"], writes=[pn])
                            return p, pn

                        def c_k1(h):
                            p, pn = psS.next()
                            s.op("pe", I("matmul", p[:, :], lhsT=wkvn_t[:, 64 * h:64 * h + 128], rhs=kvn_t[:, :], start=True, stop=True),
                                 reads=["wkvn_t", "kvn_t"], writes=[pn])
                            return p, pn

                        def c_k2(r, h):
                            s.op("act", I("activation", out=kcraw[0:64, :], in_=r[0][0:64, :], func=AF.Copy),
                                 reads=[r[1]], writes=["kcraw_n"])
                            norm_rope(kcraw[0:96, :], ["kcraw_n", "kcraw_r"], 96, bdt[:, 2, :], 1.0 / 96,
                                      pcol("c_kn", 0, 96), ptn, rgt[:, 3, :], rgn, tabs, tabn,
                                      KT[0:96, h, tsl], ["K:%d:%d" % (h, t)])

                        for h in range(4):
                            items.append((lambda h=h: c_q1(h),
                                          (lambda r, h=h: (r[0][0:96, :], r[1], 96, bdt[:, 2, :], 1.0 / 96,
                                                           pcol("c_qn", 0, 96), ptn, rgt[:, 2, :], rgn, tabs,
                                                           tabn, QT[0:96, h, tsl], ["Q:%d:%d" % (h, t)]),)))
                            items.append((lambda h=h: c_k1(h), lambda r, h=h: c_k2(r, h)))

                        def c_v1(blk):
                            p, pn = psS.next()
                            s.op("pe", I("matmul", p[:, 0:256], lhsT=kvn_t[:, blk * 128:(blk + 1) * 128],
                                         rhs=wkvv_t[:].rearrange("p h d -> p (h d)"), start=True, stop=True),
                                 reads=["kvn_t", "wkvv_t"], writes=[pn])
                            return p, pn

                        for blk in range(4):
                            items.append((lambda blk=blk: c_v1(blk), lambda r, blk=blk: evac_v(r, 4, t * 4 + blk)))
                    qa_, qb_ = [], []

                    def run_a():
                        st2, r = qa_.pop(0)
                        if isinstance(st2, tuple):
                            qb_.append(norm_a(*st2[0](r)))
                        else:
                            st2(r)
                            qb_.append(None)

                    def run_b():
                        ctx = qb_.pop(0)
                        if ctx is not None:
                            norm_b(ctx)

                    for (st1, st2) in items:
                        qa_.append((st2, st1()))
                        if len(qa_) > 1:
                            run_a()
                        if len(qb_) > 1:
                            run_b()
                    while qa_:
                        run_a()
                        if len(qb_) > 1:
                            run_b()
                    while qb_:
                        run_b()
                first_pass = False
                if mx == mixers[0] and do_ffn:
                    for r in range(8):
                        s.op("pool", I("dma_start", out=WUPB[r * 128:(r + 1) * 128, :], in_=w_up[l][r * 128:(r + 1) * 128, :]),
                             writes=["WUPB"], dmakey="castup")
                    for r in range(NCF):
                        s.op("pool", I("dma_start", out=WDNB[r * 128:(r + 1) * 128, :], in_=w_down[l][r * 128:(r + 1) * 128, :]),
                             writes=["WDNB"], dmakey="castdn")
                if do_attn:
                    s.phase = "L%d attn%s" % (l, mx)
                    attention(l, mx, pd, pdn, pt, ptn)

            s.phase = "L%d wout" % l
            wo_fm = w_out[l].rearrange("(kc p) n -> p kc n", p=128)
            for half in range(2):
                Wh = WSLOT[half]
                s.op("pool", I("dma_start", out=Wh[:, :, 0:512],
                                                                     in_=wo_fm[:, :, half * 512:(half + 1) * 512]),
                     reads=[GUARD], writes=["wslot%d" % half], dmakey="wslot%d" % half)
            for t in range(NT):
                hbt, hbn = hring.next()
                tsl = slice(t * 512, (t + 1) * 512)
                s.op("sp", I("dma_start", out=hbt[:], in_=mix_fm[:, :, tsl]),
                     reads=["MIX:%d" % t], writes=[hbn], dmakey="ld_" + hbn)
                xl = {}

                def xload(c):
                    xt_, xn_ = xring.next()
                    s.op("sp", I("dma_start", out=xt_[:, :], in_=xin_fm[:, c, tsl]),
                         reads=["%s:%d" % (xin_name, t)], writes=[xn_], dmakey=xn_)
                    xl[c] = (xt_, xn_)

                xload(0)
                xload(1)
                for c in range(8):
                    Wh = WSLOT[c // 4]
                    cc = (c % 4) * 128
                    p, pn = psS.next()
                    for kc in range(8):
                        s.op("pe", I("matmul", p[:, :], lhsT=Wh[:, kc, cc:cc + 128], rhs=hbt[:, kc, :],
                                     start=(kc == 0), stop=(kc == 7)),
                             reads=["wslot%d" % (c // 4), hbn, GUARD], writes=[pn])
                    if c + 2 < 8:
                        xload(c + 2)
                    xt_, xn_ = xl[c]
                    ot, on = xoring.next()
                    s.op("dve", I("tensor_tensor", out=ot[:, :], in0=p[:, :], in1=xt_[:, :], op=ALU.add),
                         reads=[pn, xn_], writes=[on])
                    s.op("sp", I("dma_start", out=XA[c * 128:(c + 1) * 128, tsl], in_=ot[:, :]),
                         reads=[on], writes=["XA:%d" % t], dmakey="st_" + on)

            if not do_ffn:
                return
            phase_barrier()
            s.phase = "L%d ffn" % l
            xa_fm = XA.rearrange("(kc p) t -> p kc t", p=128)
            wup_fm = WUPB.rearrange("(kc p) n -> p kc n", p=128)
            wdn_fm = WDNB.rearrange("(kc p) n -> p kc n", p=128)
            wins = []
            for w in range(9):
                tok0 = 510 * w - 1
                n_in = min(512, S + 1 - tok0)
                lo = 1 if w == 0 else 0
                hi = min(n_in, S - tok0)
                wins.append((w, tok0, n_in, lo, hi))
            wup_i = [0]
            wdn_i = [0]
            def ffn_norm(grp):
                for wi, (w, tok0, n_in, lo, hi) in enumerate(grp):
                    chunks = sorted(set([max(tok0, 0) // 512, min(tok0 + n_in - 1, S - 1) // 512]))
                    norm_chunk(xa_fm, lambda chunks=chunks: ["XA:%d" % c for c in chunks], "n2g", tok0, n_in, lo, hi,
                               lambda kc, wi=wi, n_in=n_in: H2[:, kc, wi, 0:n_in], ["H2:%d" % wi],
                               zero_pad=(lo > 0 or hi < n_in))

            def ffn_up(grp):
                for j in range(NCF):
                    wu = WUP[wup_i[0] % 3]
                    wun = "wup%d" % (wup_i[0] % 3)
                    wup_i[0] += 1
                    for half in range(2):
                        s.op("pool", I("dma_start", out=wu[:, :, half * 128:(half + 1) * 128],
                                       in_=wup_fm[:, :, half * DFF + j * 128:half * DFF + (j + 1) * 128]),
                             reads=[GUARD, "WUPB"], writes=[wun], dmakey=wun)
                    for wi, (w, tok0, n_in, lo, hi) in enumerate(grp):
                        no = n_in - 2
                        t3s = []
                        for half in range(2):
                            p, pn = psS.next()
                            for kc in range(8):
                                s.op("pe", I("matmul", p[:, 0:n_in], lhsT=wu[:, kc, half * 128:(half + 1) * 128], rhs=H2[:, kc, wi, 0:n_in],
                                    start=(kc == 0), stop=(kc == 7)),
                                    reads=[wun, "H2:%d" % wi, GUARD], writes=[pn])
                            cj = half * NCF + j
                            a1, a1n = t1_r.next()
                            a2, a2n = t2_r.next()
                            a3, a3n = (t3_r if half == 0 else tb_r).next()
                            s.op("act", I("activation", out=a1[:, 0:no], in_=p[:, 1:1 + no], func=AF.Identity,
                                bias=pt[:, PP["cb"][0] + cj:PP["cb"][0] + cj + 1],
                                scale=pt[:, PP["cw1"][0] + cj:PP["cw1"][0] + cj + 1]),
                                reads=[pn, ptn], writes=[a1n])
                            s.op("dve", I("scalar_tensor_tensor", out=a2[:, 0:no], in0=p[:, 0:no], scalar=pt[:, PP["cw0"][0] + cj:PP["cw0"][0] + cj + 1],
                                in1=a1[:, 0:no], op0=ALU.mult, op1=ALU.add), reads=[pn, a1n, ptn], writes=[a2n])
                            s.op("dve", I("scalar_tensor_tensor", out=a3[:, 0:no], in0=p[:, 2:2 + no], scalar=pt[:, PP["cw2"][0] + cj:PP["cw2"][0] + cj + 1],
                                in1=a2[:, 0:no], op0=ALU.mult, op1=ALU.add), reads=[pn, a2n, ptn], writes=[a3n])
                            t3s.append((a3, a3n))
                        (gt, gn), (vt, vn) = t3s
                        sg, sgn = ln_r.next()
                        s.op("act", I("activation", out=sg[:, 0:no], in_=gt[:, 0:no],
                                                                                func=AF.Silu),
                             reads=[gn], writes=[sgn])
                        s.op("dve", I("tensor_tensor", out=ACTT[:, j, wi, 0:no], in0=sg[:, 0:no], in1=vt[:, 0:no], op=ALU.mult),
                            reads=[sgn, vn, GUARD], writes=["ACT:%d" % wi])
            def ffn_down(grp):
                for c in range(8):
                    wd = WDN[wdn_i[0] % 2]
                    wdn = "wdn%d" % (wdn_i[0] % 2)
                    wdn_i[0] += 1
                    s.op("pool", I("dma_start", out=wd[:], in_=wdn_fm[:, :, c * 128:(c + 1) * 128]),
                         reads=[GUARD, "WDNB"], writes=[wdn], dmakey=wdn)
                    for wi, (w, tok0, n_in, lo, hi) in enumerate(grp):
                        no = n_in - 2
                        o0 = tok0 + 1
                        p, pn = psS.next()
                        for kc in range(NCF):
                            s.op("pe", I("matmul", p[:, 0:no], lhsT=wd[:, kc, :], rhs=ACTT[:, kc, wi, 0:no],
                                start=(kc == 0), stop=(kc == NCF - 1)),
                                reads=[wdn, "ACT:%d" % wi, GUARD], writes=[pn])
                        chunks = sorted(set([o0 // 512, (o0 + no - 1) // 512]))
                        xt_, xn_ = xring.next()
                        s.op("sp", I("dma_start", out=xt_[:, 0:no], in_=xa_fm[:, c, o0:o0 + no]),
                            reads=["XA:%d" % cc for cc in chunks], writes=[xn_], dmakey=xn_)
                        ot, on = xoring.next()
                        s.op("dve", I("tensor_tensor", out=ot[:, 0:no], in0=p[:, 0:no], in1=xt_[:, 0:no], op=ALU.add),
                            reads=[pn, xn_], writes=[on])
                        s.op("sp", I("dma_start", out=Xout[c * 128:(c + 1) * 128, o0:o0 + no], in_=ot[:, 0:no]),
                            reads=[on], writes=["%s:%d" % (xout_name, cc) for cc in chunks], dmakey="st_" + on)

            groups = [wins[g0:g0 + G] for g0 in range(0, 9, G)]
            ffn_norm(groups[0])
            for gi, grp in enumerate(groups):
                ffn_up(grp)
                if gi + 1 < len(groups):
                    ffn_norm(groups[gi + 1])
                ffn_down(grp)

        def attention(l, mx, pd, pdn, pt, ptn):
            base_head = {"A": 0, "B": 4, "C": 8, "D": 12}[mx]
            d = {"A": 64, "B": 32, "C": 96, "D": 64}[mx]
            scale = float(d) ** -0.5

            def kv_of(h):
                return h // 2 if mx == "A" else h

            def dist_min(k0, q0):
                if k0 + 127 < q0:
                    return q0 - (k0 + 127)
                if k0 > q0 + 511:
                    return k0 - (q0 + 511)
                return 0

            active = []

            def step_epilogues():
                for g in list(active):
                    try:
                        next(g)
                    except StopIteration:
                        active.remove(g)

            def flush_epilogues():
                while active:
                    step_epilogues()

            def evac(po, pon):
                ob, obn = fin_r.next()
                ld, ldn = ln_r.next()
                if mx in "AC":
                    s.op("dve", I("tensor_copy", out=ob[:, :], in_=po[0:64, :]), reads=[pon], writes=[obn])
                    s.op("dve", I("reciprocal", out=ld[64:65, :], in_=po[64:65, :]), reads=[pon], writes=[ldn])
                else:
                    s.op("act", I("activation", out=ob[:, :], in_=po[0:64, :], func=AF.Copy), reads=[pon], writes=[obn])
                    s.op("act", I("activation", out=ld[64:65, :], in_=po[64:65, :], func=AF.Ln), reads=[pon], writes=[ldn])
                return ob, obn, ld, ldn

            def epilogue(items, h, q0, qc):
                mo, mon = None, None
                for mi, (ob, obn, ld, ldn) in enumerate(items):
                    if mx not in "AC":
                        s.op("act", I("activation", out=ld[64:65, :], in_=ld[64:65, :], func=AF.Exp, scale=-1.0),
                             reads=[ldn], writes=[ldn])
                        yield
                    rh, rhn = qs_r.next()
                    rl, rln = qs_r.next()
                    s.op("dve", I("tensor_copy", out=rh[64:65, :], in_=ld[64:65, :]), reads=[ldn], writes=[rhn])
                    s.op("dve", I("tensor_tensor", out=rl[64:65, :], in0=ld[64:65, :], in1=rh[64:65, :], op=ALU.subtract),
                         reads=[ldn, rhn], writes=[rln])
                    yield
                    pb, pbn = psB.next()
                    s.op("pe", I("matmul", pb[:, :], lhsT=bdt[:, 3, :], rhs=rh[:, :], start=True, stop=False),
                         reads=[rhn, "blk32"], writes=[pbn])
                    s.op("pe", I("matmul", pb[:, :], lhsT=bdt[:, 3, :], rhs=rl[:, :], start=False, stop=True),
                         reads=[rln, "blk32"], writes=[pbn])
                    yield
                    if mx != "B":
                        mo, mon = mo_r.next()
                        s.op("dve", I("tensor_tensor", out=mo[:, :], in0=ob[:, :], in1=pb[0:64, :], op=ALU.mult),
                             reads=[obn, pbn], writes=[mon])
                    else:
                        s.op("dve", I("tensor_tensor", out=ob[:, :], in0=ob[:, :], in1=pb[0:64, :], op=ALU.mult),
                             reads=[obn, pbn], writes=[obn])
                    yield
                if mx == "B":
                    (n1, n1n, _, _), (n2, n2n, _, _) = items
                    df, dfn = t1_r.next()
                    s.op("dve", I("scalar_tensor_tensor", out=df[0:64, :], in0=n2[:, :], scalar=pd[0:64, 0:1],
                                  in1=n1[:, :], op0=ALU.mult, op1=ALU.add), reads=[n1n, n2n, pdn], writes=[dfn])
                    yield
                    sqt, sqn = sq_r.next()
                    s.op("act", I("activation", out=sqt[0:64, :], in_=df[0:64, :], func=AF.Square), reads=[dfn], writes=[sqn])
                    yield
                    pm, pmn = psB.next()
                    s.op("pe", I("matmul", pm[:, :], lhsT=bdt[:, 0, :], rhs=sqt[:, :], start=True, stop=True),
                         reads=[sqn, "blk32"], writes=[pmn])
                    yield
                    lt, ltn = rs_r.next()
                    s.op("act", I("activation", out=lt[0:64, :], in_=pm[0:64, :], func=AF.Ln, bias=EPS, scale=1.0 / 64),
                         reads=[pmn], writes=[ltn])
                    yield
                    s.op("act", I("activation", out=lt[0:64, :], in_=lt[0:64, :], func=AF.Exp, scale=-0.5),
                         reads=[ltn], writes=[ltn])
                    yield
                    mo, mon = mo_r.next()
                    s.op("dve", I("scalar_tensor_tensor", out=mo[:, :], in0=df[0:64, :], scalar=pd[0:64, 1:2],
                                  in1=lt[0:64, :], op0=ALU.mult, op1=ALU.mult), reads=[dfn, pdn, ltn], writes=[mon])
                hh = base_head + h
                s.op("sp", I("dma_start", out=MIX[hh * 64:(hh + 1) * 64, q0:q0 + 512], in_=mo[:, :]),
                     reads=[mon], writes=["MIX:%d" % qc], dmakey="st_" + mon)

            pend = []
            WIDE_KEEP = 2
            SLOPE_KEEP = 2

            def do_pv(item, last_flush=False):
                po, pon, Eap, En, kb, kvh, q0, first, last, on_last = item
                step_epilogues()
                if mx == "D":
                    c_lo = MS_C0 - (kb * 128 - q0)
                    s.op("dve", I("tensor_tensor", out=Eap, in0=Eap, in1=cMst[:, c_lo:c_lo + 512], op=ALU.mult),
                         reads=[En, "cMst"], writes=[En])
                voff = 32768 + (kvh * 32 + kb) * 65
                s.op("pe", I("matmul", po[:, :], lhsT=arena[:, voff:voff + 128], rhs=Eap, start=first, stop=last),
                     reads=[En, "V:%d" % kb, "VONES", GUARD], writes=[pon])
                if last:
                    on_last(po, pon)

            def flush_to(n):
                while len(pend) > n:
                    do_pv(pend.pop(0))

            for h in range(4):
                slope = (SLOPES[h] if mx == "B" else SLOPES[4 + h]) if mx in "BD" else None
                maps = [(0, 32), (32, 64)] if mx == "B" else [(0, d)]
                kvh = kv_of(h)
                for qc in range(8):
                    q0 = qc * 512
                    if mx == "D":
                        kbs = [kb for kb in range(32) if q0 - 1024 <= kb * 128 < q0 + 512 + 1024]
                    elif mx == "B":
                        kbs = [kb for kb in range(32) if slope * dist_min(kb * 128, q0) <= 60.0]
                    else:
                        kbs = list(range(32))
                    accs = []

                    evs = []

                    def after_last(po, pon, evs=evs, nmaps=len(maps), h=h, q0=q0, qc=qc):
                        if not evs:
                            flush_epilogues()
                        evs.append(evac(po, pon))
                        if len(evs) == nmaps:
                            active.append(epilogue(evs, h, q0, qc))

                    for mi, (r0, r1) in enumerate(maps):
                        po, pon = psO.next()
                        accs.append((po, pon))
                        on_last = after_last
                        nk = len(kbs)
                        if slope is None:
                            i = 0
                            while i < nk:
                                grp = kbs[i:i + 2]
                                pp2, (pna, pnb) = psP.next()
                                names = [pna, pnb][:len(grp)]
                                for gi, kb in enumerate(grp):
                                    k0 = kb * 128
                                    s.op("pe", I("matmul", pp2[:, gi * 512:(gi + 1) * 512], lhsT=KT[:, kvh, k0:k0 + 128],
                                                 rhs=QT[:, h, q0:q0 + 512], start=True, stop=True),
                                         reads=["K:%d:%d" % (kvh, kb // 4), "Q:%d:%d" % (h, qc), GUARD], writes=[names[gi]])
                                flush_to(WIDE_KEEP)
                                Et, En = E_r.next()
                                w = 512 * len(grp)
                                s.op("act", I("activation", out=Et[:, 0:w], in_=pp2[:, 0:w], func=AF.Exp, scale=scale),
                                     reads=names, writes=[En])
                                for gi, kb in enumerate(grp):
                                    idx = i + gi
                                    pend.append((po, pon, Et[:, gi * 512:(gi + 1) * 512], En, kb, kvh, q0, idx == 0,
                                                 idx == nk - 1, on_last))
                                i += 2
                        else:
                            for i, kb in enumerate(kbs):
                                k0 = kb * 128
                                ps_, psn = psS.next()
                                rr0, rr1 = (r0, r1) if mx == "B" else (0, 128)
                                s.op("pe", I("matmul", ps_[:, :], lhsT=KT[rr0:rr1, kvh, k0:k0 + 128],
                                             rhs=QT[rr0:rr1, h, q0:q0 + 512], start=True, stop=True),
                                     reads=["K:%d:%d" % (kvh, kb // 4), "Q:%d:%d" % (h, qc), GUARD], writes=[psn])
                                flush_to(SLOPE_KEEP)
                                Et, En = E_r.next()
                                tt, ttn = tb_r.next()
                                if k0 + 127 < q0:
                                    bias_ap, mul, cb = cLt[:, :], slope / scale, -slope * (q0 - k0)
                                elif k0 > q0 + 511:
                                    bias_ap, mul, cb = cLt[:, :], -slope / scale, -slope * (k0 - q0)
                                else:
                                    j = (k0 - q0) // 128
                                    bias_ap, mul, cb = cW0t[:, 384 - 128 * j:896 - 128 * j], slope / scale, 0.0
                                s.op("dve", I("scalar_tensor_tensor", out=tt[:, :], in0=bias_ap, scalar=float(mul),
                                              in1=ps_[:, :], op0=ALU.mult, op1=ALU.add),
                                     reads=[psn, "cLt", "cW0t"], writes=[ttn])
                                s.op("act", I("activation", out=Et[:, 0:512], in_=tt[:, :], func=AF.Exp, scale=scale,
                                              bias=float(cb)), reads=[ttn], writes=[En])
                                pend.append((po, pon, Et[:, 0:512], En, kb, kvh, q0, i == 0, i == nk - 1, on_last))
            while pend:
                do_pv(pend.pop(0))
            flush_epilogues()

        for l in range(n_layers):
            Xin, xin_name = (xT, "X0") if l == 0 else (XB, "XB")
            last = (l == n_layers - 1)
            Xout, xout_name = (outT, "OUT") if last else (XB, "XB")
            if not do_ffn:
                pass
            layer(l, Xin, xin_name, Xout, xout_name)

        finals = [k for k in s.dma_ops if k.startswith("st_")]
        s.emit(st, final_waits=finals)
        counts = {e: len(s.ops[e]) for e in ENGS}
        _CACHE["sched"] = s
    return nc, counts


def _prep_inputs(inputs):
    f32 = np.float32
    consts = _host_constants()
    pp = _pack_pp(inputs)
    shared = {
        "w_in": np.ascontiguousarray(inputs["w_in"], dtype=f32),
        "w_out": np.ascontiguousarray(inputs["w_out"], dtype=f32),
        "w_up": np.ascontiguousarray(inputs["w_up"], dtype=f32),
        "w_down": np.ascontiguousarray(inputs["w_down"], dtype=f32),
        "c_wqb": np.ascontiguousarray(inputs["c_wqb"], dtype=f32),
        "c_wkvb": np.ascontiguousarray(inputs["c_wkvb"], dtype=f32),
        "pp": pp,
    }
    shared.update(consts)
    return shared


def kernel(**inputs):
    x = np.asarray(inputs["x"], dtype=np.float32)
    shared = _prep_inputs(inputs)
    if "nc" not in _CACHE:
        _CACHE["nc"] = build_program()[0]
    nc = _CACHE["nc"]
    in_maps = []
    for c in range(8):
        m = dict(shared)
        m["xT"] = np.ascontiguousarray(x[c].T)
        in_maps.append(m)
    res = run_bass_kernel_spmd(nc, in_maps, core_ids=list(range(8)))
    out = np.stack([np.asarray(r["outT"]).T for r in res.results], axis=0)
    return np.ascontiguousarray(out.astype(np.float32))
```
